# Optimizing a Trainium2 kernel written in Bass

```python
import math
import jax, jax.numpy as jnp
from jax import lax
import numpy as np

D_MODEL = 1024
BATCH = 1
SEQ = 16384
DEPTH = 4

N_MIXERS = 4
D_FF = 2816
EPS = 1e-6
S5_GROUP = 16
S5_GROUPS = D_MODEL // S5_GROUP
S5_STATE = 64
DT_MIN = 1e-3
DT_MAX = 1e-1
CONV_W = 31
GM_CHUNK = 128
GM_E = 2 * D_MODEL
GM_HEADS = 8
HEAD_DIM = 64
AT_HEADS = D_MODEL // HEAD_DIM
PATTERNS = ((128, 1), (512, 4), (2048, 16))
N_PATTERNS = len(PATTERNS)
BLOCK = 128
NUM_BUCKETS = 32
MAX_DISTANCE = 2048

kernel_name = "hybrid_interleaved_s5_conv_gmlp_dilated_attn"


def rmsnorm(x, g):
    xf = x.astype(jnp.float32)
    y = xf * lax.rsqrt(jnp.mean(xf * xf, axis=-1, keepdims=True) + EPS)
    return (y * g.astype(jnp.float32)).astype(x.dtype)


def layernorm(x, g, b):
    xf = x.astype(jnp.float32)
    mu = jnp.mean(xf, axis=-1, keepdims=True)
    var = jnp.mean(jnp.square(xf - mu), axis=-1, keepdims=True)
    y = (xf - mu) * lax.rsqrt(var + EPS)
    return (y * g.astype(jnp.float32) + b.astype(jnp.float32)).astype(x.dtype)


def swiglu(h, w1, w3, w2):
    return (jax.nn.silu(h @ w1) * (h @ w3)) @ w2


def _complex_scan_op(left, right):
    a1r, a1i, b1r, b1i = left
    a2r, a2i, b2r, b2i = right
    return (a2r * a1r - a2i * a1i,
            a2r * a1i + a2i * a1r,
            a2r * b1r - a2i * b1i + b2r,
            a2r * b1i + a2i * b1r + b2i)


def s5_mixer(h, w_in, a_re, a_im, log_dt, b_re, b_im, c_re, c_im, d_skip, w_glu, b_glu, w_out):
    bsz, seq, _ = h.shape
    u = (h @ w_in).astype(jnp.float32).reshape(bsz, seq, S5_GROUPS, S5_GROUP)
    ar = a_re.astype(jnp.float32)
    ai = a_im.astype(jnp.float32)
    dt = jnp.exp(log_dt.astype(jnp.float32))[:, None]
    mag = jnp.exp(dt * ar)
    abar_re = mag * jnp.cos(dt * ai)
    abar_im = mag * jnp.sin(dt * ai)
    den = ar * ar + ai * ai
    nr = abar_re - 1.0
    f_re = (nr * ar + abar_im * ai) / den
    f_im = (abar_im * ar - nr * ai) / den
    br = b_re.astype(jnp.float32)
    bi = b_im.astype(jnp.float32)
    bb_re = f_re[..., None] * br - f_im[..., None] * bi
    bb_im = f_re[..., None] * bi + f_im[..., None] * br
    bu_re = jnp.einsum('blgh,gph->blgp', u, bb_re)
    bu_im = jnp.einsum('blgh,gph->blgp', u, bb_im)
    elems = (jnp.broadcast_to(abar_re, bu_re.shape), jnp.broadcast_to(abar_im, bu_re.shape), bu_re, bu_im)
    _, _, st_re, st_im = lax.associative_scan(_complex_scan_op, elems, axis=1)
    y = (jnp.einsum('blgp,ghp->blgh', st_re, c_re.astype(jnp.float32))
         - jnp.einsum('blgp,ghp->blgh', st_im, c_im.astype(jnp.float32))
         + d_skip.astype(jnp.float32).reshape(S5_GROUPS, S5_GROUP) * u)
    y = jax.nn.gelu(y.reshape(bsz, seq, D_MODEL)).astype(h.dtype)
    z = y * jax.nn.sigmoid(y @ w_glu + b_glu)
    return z @ w_out


def conv_mixer(h, w_in, b_in, dw, dw_b, ln_g, ln_b, w_out, b_out):
    z = h @ w_in + b_in
    a, g = jnp.split(z, 2, axis=-1)
    z = a * jax.nn.sigmoid(g)
    z = lax.conv_general_dilated(z, dw[:, None, :].astype(z.dtype), window_strides=(1,),
                                 padding=[(CONV_W - 1, 0)],
                                 dimension_numbers=('NWC', 'WIO', 'NWC'),
                                 feature_group_count=D_MODEL) + dw_b
    z = jax.nn.silu(layernorm(z, ln_g, ln_b))
    return z @ w_out + b_out


def gmlp_mixer(h, w_in, b_in, ln_g, ln_b, w_s, b_s, w_out, b_out):
    bsz, seq, _ = h.shape
    z = jax.nn.gelu(h @ w_in + b_in)
    u, v = jnp.split(z, 2, axis=-1)
    v = layernorm(v, ln_g, ln_b)
    vc = v.reshape(bsz, seq // GM_CHUNK, GM_CHUNK, GM_HEADS, GM_E // GM_HEADS)
    causal = jnp.tril(jnp.ones((GM_CHUNK, GM_CHUNK), jnp.float32))
    s = jnp.einsum('hts,bnshc->bnthc', w_s * causal, vc) + b_s.T[None, None, :, :, None]
    s = s.reshape(bsz, seq, GM_E).astype(u.dtype)
    return (u * s) @ w_out + b_out


def t5_bucket(dist):
    max_exact = NUM_BUCKETS // 2
    distf = jnp.maximum(dist, 1).astype(jnp.float32)
    large = max_exact + (jnp.log(distf / max_exact) / math.log(MAX_DISTANCE / max_exact)
                         * (NUM_BUCKETS - max_exact)).astype(jnp.int32)
    large = jnp.minimum(large, NUM_BUCKETS - 1)
    return jnp.where(dist < max_exact, dist, large)


def _band_delta():
    return (jnp.arange(BLOCK)[:, None] + BLOCK) - jnp.arange(2 * BLOCK)[None, :]


def rel_bias_block(table, dilation):
    dist = jnp.maximum(_band_delta(), 0) * dilation
    return table.astype(jnp.float32)[t5_bucket(dist)].transpose(2, 0, 1)


def dilated_window_attention(q, k, v, bias, window, dilation):
    bsz, seq, heads, hd = q.shape
    span = BLOCK * dilation
    seq_p = -(-seq // span) * span
    n_sub = seq_p // dilation
    n_blk = n_sub // BLOCK

    def to_blocks(t):
        t = jnp.pad(t.astype(jnp.float32), ((0, 0), (0, seq_p - seq), (0, 0), (0, 0)))
        t = t.reshape(bsz, n_sub, dilation, heads, hd).transpose(0, 2, 1, 3, 4)
        return t.reshape(bsz, dilation, n_blk, BLOCK, heads, hd)

    def with_prev(t):
        prev = jnp.pad(t, ((0, 0), (0, 0), (1, 0), (0, 0), (0, 0), (0, 0)))[:, :, :-1]
        return jnp.concatenate([prev, t], axis=3)

    qb = to_blocks(q)
    kk = with_prev(to_blocks(k))
    vv = with_prev(to_blocks(v))
    logits = jnp.einsum('brnqhd,brnkhd->brnhqk', qb, kk) * (hd ** -0.5) + bias[None, None, None]
    delta = _band_delta()
    in_band = (delta >= 0) & (delta <= window // dilation)
    not_before_start = (jnp.arange(n_blk)[:, None, None] > 0) | (jnp.arange(2 * BLOCK) >= BLOCK)[None, None, :]
    mask = in_band[None] & not_before_start
    logits = jnp.where(mask[None, None, :, None], logits, -jnp.inf)
    m = jnp.max(logits, axis=-1, keepdims=True)
    p = jnp.exp(logits - m)
    den = jnp.sum(p, axis=-1)
    o = jnp.einsum('brnhqk,brnkhd->brnqhd', p, vv) / jnp.swapaxes(den, -1, -2)[..., None]
    lse = jnp.swapaxes(m[..., 0] + jnp.log(den), -1, -2)
    o = o.reshape(bsz, dilation, n_sub, heads, hd).transpose(0, 2, 1, 3, 4).reshape(bsz, seq_p, heads, hd)[:, :seq]
    lse = lse.reshape(bsz, dilation, n_sub, heads).transpose(0, 2, 1, 3).reshape(bsz, seq_p, heads)[:, :seq]
    return o, lse


def attention_mixer(h, w_qkv, w_out, rel_bias):
    bsz, seq, _ = h.shape
    qkv = (h @ w_qkv).reshape(bsz, seq, N_PATTERNS, 3, AT_HEADS, HEAD_DIM)
    outs, lses = [], []
    for g, (window, dilation) in enumerate(PATTERNS):
        bias = rel_bias_block(rel_bias[:, g * AT_HEADS:(g + 1) * AT_HEADS], dilation)
        o, lse = dilated_window_attention(qkv[:, :, g, 0], qkv[:, :, g, 1], qkv[:, :, g, 2], bias, window, dilation)
        outs.append(o)
        lses.append(lse)
    wts = jax.nn.softmax(jnp.stack(lses, axis=0), axis=0)
    o = jnp.sum(wts[..., None] * jnp.stack(outs, axis=0), axis=0)
    return o.reshape(bsz, seq, AT_HEADS * HEAD_DIM).astype(h.dtype) @ w_out


def _count(kind):
    return len(range(kind, DEPTH, N_MIXERS))


def setup_inputs(seed: int = 0) -> dict:
    key = jax.random.key(seed)
    ks = iter(jax.random.split(key, 48))

    def nrm(shape, scale):
        return scale * jax.random.normal(next(ks), shape, jnp.float32)

    na, nb, nc, nd = _count(0), _count(1), _count(2), _count(3)
    D, F, G, P, HG, E = D_MODEL, D_FF, S5_GROUPS, S5_STATE, S5_GROUP, GM_E
    return {
        "x": nrm((BATCH, SEQ, D), 1.0),
        "norm_pre": 1.0 + nrm((DEPTH, 3, D), 0.05),
        "norm_post": 1.0 + nrm((DEPTH, 3, D), 0.05),
        "ffn_w1": nrm((DEPTH, 2, D, F), D ** -0.5),
        "ffn_w3": nrm((DEPTH, 2, D, F), D ** -0.5),
        "ffn_w2": nrm((DEPTH, 2, F, D), F ** -0.5),
        "rel_bias": nrm((NUM_BUCKETS, N_PATTERNS * AT_HEADS), 0.5),
        "s5_w_in": nrm((na, D, D), D ** -0.5),
        "s5_a_re": -0.5 + nrm((na, G, P), 0.01),
        "s5_a_im": jnp.pi * jnp.arange(P, dtype=jnp.float32) + nrm((na, G, P), 0.01),
        "s5_log_dt": jax.random.uniform(next(ks), (na, G), jnp.float32, math.log(DT_MIN), math.log(DT_MAX)),
        "s5_b_re": nrm((na, G, P, HG), (2 * HG) ** -0.5),
        "s5_b_im": nrm((na, G, P, HG), (2 * HG) ** -0.5),
        "s5_c_re": nrm((na, G, HG, P), (2 * P) ** -0.5),
        "s5_c_im": nrm((na, G, HG, P), (2 * P) ** -0.5),
        "s5_d": nrm((na, D), 1.0),
        "s5_w_glu": nrm((na, D, D), D ** -0.5),
        "s5_b_glu": nrm((na, D), 0.01),
        "s5_w_out": nrm((na, D, D), D ** -0.5),
        "cv_w_in": nrm((nb, D, 2 * D), D ** -0.5),
        "cv_b_in": nrm((nb, 2 * D), 0.01),
        "cv_dw": nrm((nb, CONV_W, D), CONV_W ** -0.5),
        "cv_dw_b": nrm((nb, D), 0.01),
        "cv_ln_g": 1.0 + nrm((nb, D), 0.05),
        "cv_ln_b": nrm((nb, D), 0.01),
        "cv_w_out": nrm((nb, D, D), D ** -0.5),
        "cv_b_out": nrm((nb, D), 0.01),
        "gm_w_in": nrm((nc, D, 2 * E), D ** -0.5),
        "gm_b_in": nrm((nc, 2 * E), 0.01),
        "gm_ln_g": 1.0 + nrm((nc, E), 0.05),
        "gm_ln_b": nrm((nc, E), 0.01),
        "gm_w_s": nrm((nc, GM_HEADS, GM_CHUNK, GM_CHUNK), GM_CHUNK ** -0.5),
        "gm_b_s": 1.0 + nrm((nc, GM_HEADS, GM_CHUNK), 0.01),
        "gm_w_out": nrm((nc, E, D), E ** -0.5),
        "gm_b_out": nrm((nc, D), 0.01),
        "at_w_qkv": nrm((nd, D, N_PATTERNS * 3 * AT_HEADS * HEAD_DIM), D ** -0.5),
        "at_w_out": nrm((nd, AT_HEADS * HEAD_DIM, D), (AT_HEADS * HEAD_DIM) ** -0.5),
    }


def reference(x, norm_pre, norm_post, ffn_w1, ffn_w3, ffn_w2, rel_bias,
              s5_w_in, s5_a_re, s5_a_im, s5_log_dt, s5_b_re, s5_b_im, s5_c_re, s5_c_im,
              s5_d, s5_w_glu, s5_b_glu, s5_w_out,
              cv_w_in, cv_b_in, cv_dw, cv_dw_b, cv_ln_g, cv_ln_b, cv_w_out, cv_b_out,
              gm_w_in, gm_b_in, gm_ln_g, gm_ln_b, gm_w_s, gm_b_s, gm_w_out, gm_b_out,
              at_w_qkv, at_w_out):
    for i in range(DEPTH):
        kind, j = i % N_MIXERS, i // N_MIXERS
        h = swiglu(rmsnorm(x, norm_pre[i, 0]), ffn_w1[i, 0], ffn_w3[i, 0], ffn_w2[i, 0])
        x = x + 0.5 * rmsnorm(h, norm_post[i, 0])
        h = rmsnorm(x, norm_pre[i, 1])
        if kind == 0:
            h = s5_mixer(h, s5_w_in[j], s5_a_re[j], s5_a_im[j], s5_log_dt[j], s5_b_re[j], s5_b_im[j],
                         s5_c_re[j], s5_c_im[j], s5_d[j], s5_w_glu[j], s5_b_glu[j], s5_w_out[j])
        elif kind == 1:
            h = conv_mixer(h, cv_w_in[j], cv_b_in[j], cv_dw[j], cv_dw_b[j], cv_ln_g[j], cv_ln_b[j],
                           cv_w_out[j], cv_b_out[j])
        elif kind == 2:
            h = gmlp_mixer(h, gm_w_in[j], gm_b_in[j], gm_ln_g[j], gm_ln_b[j], gm_w_s[j], gm_b_s[j],
                           gm_w_out[j], gm_b_out[j])
        else:
            h = attention_mixer(h, at_w_qkv[j], at_w_out[j], rel_bias)
        x = x + rmsnorm(h, norm_post[i, 1])
        h = swiglu(rmsnorm(x, norm_pre[i, 2]), ffn_w1[i, 1], ffn_w3[i, 1], ffn_w2[i, 1])
        x = x + 0.5 * rmsnorm(h, norm_post[i, 2])
    return x
```

```python
import contextlib
import numpy as np
import concourse.bass as bass
import concourse.mybir as mybir
from concourse.bass_utils import run_bass_kernel_spmd

F32 = mybir.dt.float32
BF16 = mybir.dt.bfloat16
AF = mybir.ActivationFunctionType
ALU = mybir.AluOpType
AX = mybir.AxisListType

NCORES = 8
D = 1024
FF = 2816
NFC = FF // 128
SEQ = 16384
TOK = SEQ // NCORES
NTT = TOK // 128
EPS = 1e-6

ENGS = ("pe", "act", "dve", "pool", "sp")
SEM_ROT = 30000
DMA_ROT = 2000


class Prog:
    def __init__(self, nc):
        self.nc = nc
        self.ops = {e: [] for e in ENGS}
        self.res = {}
        self.dma_sems = {}
        self.waited = {e: {} for e in ENGS}
        self.signaled = {e: set() for e in ENGS}
        self.grouped = set()

    def _deps(self, eng, reads, writes):
        deps = []
        for r in reads:
            ent = self.res.get(r)
            if ent and ent[0] is not None:
                deps.append(ent[0])
            if ent and isinstance(r, tuple) and r[0] == "bank":
                deps.extend((k, v) for k, v in ent[1].items() if k != eng)
        for w in writes:
            ent = self.res.get(w)
            if ent:
                if ent[0] is not None:
                    deps.append(ent[0])
                deps.extend(ent[1].items())
        out = {}
        for k, v in deps:
            if k == eng and eng == "pe":
                continue
            if v > out.get(k, -1):
                out[k] = v
        waits = []
        for k, v in out.items():
            if self.waited[eng].get(k, -1) >= v:
                continue
            self.waited[eng][k] = v
            waits.append((k, v))
            if k in ENGS:
                self.signaled[k].add(v)
        return waits

    def op(self, eng, fn, reads=(), writes=()):
        waits = self._deps(eng, reads, writes)
        seq = len(self.ops[eng])
        self.ops[eng].append(dict(fn=fn, waits=waits, dma=None))
        for r in reads:
            self.res.setdefault(r, [None, {}])[1][eng] = seq
        for w in writes:
            self.res[w] = [(eng, seq), {}]
        return seq

    def dma(self, eng, fn, sem, reads=(), writes=(), group=False):
        waits = self._deps(eng, reads, writes)
        self.dma_sems[sem] = self.dma_sems.get(sem, 0) + 1
        val = self.dma_sems[sem]
        if group:
            self.grouped.add(sem)
            assert all(k != "dma:" + sem for k, _ in waits), ("dep inside grouped dma batch", sem)
            val = 1 << 30
        self.ops[eng].append(dict(fn=fn, waits=waits, dma=sem))
        key = "dma:" + sem
        for r in reads:
            self.res.setdefault(r, [None, {}])[1][key] = val
        for w in writes:
            self.res[w] = [(key, val), {}]
        return val

    def barrier(self):
        last = {}
        for e in ENGS:
            for i in range(len(self.ops[e]) - 1, -1, -1):
                o = self.ops[e][i]
                if o["dma"] is None and o["fn"] is not None:
                    last[e] = i
                    break
        dl = {"dma:" + s: ((1 << 30) if s in self.grouped else c) for s, c in self.dma_sems.items()}
        for e in ENGS:
            waits = []
            for k, v in list(last.items()) + list(dl.items()):
                if k == e:
                    continue
                if self.waited[e].get(k, -1) >= v:
                    continue
                self.waited[e][k] = v
                waits.append((k, v))
                if k in ENGS:
                    self.signaled[k].add(v)
            if waits:
                self.ops[e].append(dict(fn=None, waits=waits, dma=None))

    def emit(self, st):
        nc = self.nc
        rank, esems = {}, {}
        for e in ENGS:
            sig = sorted(self.signaled[e])
            rank[e] = {s: i for i, s in enumerate(sig)}
            n = max(1, (len(sig) + SEM_ROT - 1) // SEM_ROT)
            esems[e] = [st.enter_context(nc.semaphore("s_%s_%d" % (e, i))) for i in range(n)]
        dsems = {}
        for s, c in self.dma_sems.items():
            n = max(1, (c + DMA_ROT - 1) // DMA_ROT)
            dsems[s] = [st.enter_context(nc.semaphore("d_%s_%d" % (s, i))) for i in range(n)]
        dcount = {s: 0 for s in self.dma_sems}

        def sem_for(k, v):
            if k in ENGS:
                r = rank[k][v]
                return esems[k][r // SEM_ROT], (r % SEM_ROT) + 1
            if k[4:] in self.grouped:
                assert self.dma_sems[k[4:]] <= DMA_ROT
                return dsems[k[4:]][0], self.dma_sems[k[4:]] * 16
            i = v - 1
            return dsems[k[4:]][i // DMA_ROT], ((i % DMA_ROT) + 1) * 16

        block = st.enter_context(nc.Block())
        engmap = dict(pe=block.tensor, act=block.scalar, dve=block.vector,
                      pool=block.gpsimd, sp=block.sync)

        def make(e):
            def body(eng):
                for seq, o in enumerate(self.ops[e]):
                    for k, v in o["waits"]:
                        sm, val = sem_for(k, v)
                        eng.wait_ge(sm, val)
                    if o["fn"] is None:
                        continue
                    ins = o["fn"](eng)
                    if o["dma"] is not None:
                        s = o["dma"]
                        i = dcount[s]
                        dcount[s] += 1
                        ins.then_inc(dsems[s][i // DMA_ROT], 16)
                    elif seq in rank[e]:
                        r = rank[e][seq]
                        ins.then_inc(esems[e][r // SEM_ROT], 1)
            return body

        for e in ENGS:
            if self.ops[e]:
                engmap[e](make(e))


class Ctx:
    ARENA = 100 * 1024

    def __init__(self, st):
        self.st = st
        self.nc = bass.Bass("TRN2", target_bir_lowering=False)
        self.P = Prog(self.nc)
        self.arena = st.enter_context(self.nc.sbuf_tensor("arena", [128, self.ARENA], BF16))
        self.top = 0
        self.banks = [st.enter_context(self.nc.psum_tensor("bank%d" % i, [128, 512], F32))
                      for i in range(8)]
        self.din = {}

    def dram_in(self, name, shape, dtype=F32):
        t = self.nc.dram_tensor(name, list(shape), dtype, kind="ExternalInput").ap()
        self.din[name] = t
        return t

    def dram_out(self, name, shape, dtype=F32):
        return self.nc.dram_tensor(name, list(shape), dtype, kind="ExternalOutput").ap()

    def mark(self):
        return self.top

    def release(self, m):
        self.top = m

    def alloc(self, n, dtype=BF16):
        four = dtype in (F32, mybir.dt.int32)
        units = n * (2 if four else 1)
        units = (units + 15) // 16 * 16
        a = self.arena[:, self.top:self.top + units]
        assert self.top + units <= self.ARENA, ("SBUF arena overflow", self.top + units)
        self.top += units
        if four:
            a = a.bitcast(dtype)
        return a[:, 0:n]

    def bank(self, i, dtype=F32):
        b = self.banks[i][:, :]
        return b if dtype == F32 else b.bitcast(dtype)


def load_consts(c, ident_d):
    P = c.P
    idf = c.alloc(128, F32)
    idb = c.alloc(128, BF16)
    P.dma("sp", lambda e: e.dma_start(out=idf, in_=ident_d), "const0", writes=["idf"], group=True)
    P.op("dve", lambda e: e.tensor_copy(out=idb, in_=idf), reads=["idf"], writes=["idb"])
    c.idf, c.idb = idf, idb


def load_x(c, X, x_d):
    P = c.P
    xv = x_d.rearrange("(t p) d -> p t d", p=128)
    Xv = X.rearrange("p (t d) -> p t d", t=NTT)
    for h in range(4):
        P.dma("sp", lambda e, h=h: e.dma_start(out=Xv[:, 4 * h:4 * h + 4, :], in_=xv[:, 4 * h:4 * h + 4, :]),
              "xin", writes=[("X", t) for t in range(4 * h, 4 * h + 4)], group=True)


def store_x(c, X, y_d):
    P = c.P
    yv = y_d.rearrange("(t p) d -> p t d", p=128)
    Xv = X.rearrange("p (t d) -> p t d", t=NTT)
    for h in range(4):
        P.dma("sp", lambda e, h=h: e.dma_start(out=yv[:, 4 * h:4 * h + 4, :], in_=Xv[:, 4 * h:4 * h + 4, :]),
              "xout", reads=[("X", t) for t in range(4 * h, 4 * h + 4)], group=True)


def rms_rstd(c, src, ss, rstd, key_in, key_ss, junk, n=D):
    P = c.P
    P.op("act", lambda e: e.activation(out=junk, in_=src, func=AF.Square, accum_out=ss),
         reads=[key_in], writes=[key_ss, "junk"])
    P.op("dve", lambda e: e.tensor_scalar(out=ss, in0=ss, scalar1=1.0 / n, scalar2=EPS,
                                          op0=ALU.mult, op1=ALU.add),
         reads=[key_ss], writes=[key_ss])
    P.op("pool", lambda e: e.tensor_tensor(out=rstd, in0=ss, in1=c.mhalf, op=ALU.pow),
         reads=[key_ss, "mhalf"], writes=[key_ss + "r"])


def ffn_phase(c, X, w13r, w2r, gpre_d, gpost_d, tag):
    P = c.P
    m = c.mark()
    Xv = X.rearrange("p (t d) -> p t d", t=NTT)
    gbc = c.alloc(2 * D, F32).rearrange("p (a d) -> p a d", a=2)
    xn = c.alloc(2 * D, BF16).rearrange("p (a d) -> p a d", a=2)
    regA = c.alloc(16384, BF16)
    hT = regA[:, 0:8192].rearrange("p (k t) -> p k t", k=8)
    w13 = regA[:, 8192:16384].rearrange("p (s w f) -> p s w f", s=4, w=2)
    H = regA.bitcast(F32).rearrange("p (t d) -> p t d", t=8)
    gT = c.alloc(NFC * 1024, BF16).rearrange("p (f t) -> p f t", f=NFC)
    w2q = c.alloc(2 * NFC * 256, BF16).rearrange("p (a f j) -> p a f j", a=2, f=NFC)
    sil = c.alloc(2 * 512, F32).rearrange("p (a t) -> p a t", a=2)
    junk = c.alloc(D, BF16)
    stt = c.alloc(64, F32)
    K = lambda *a: (tag,) + a

    P.dma("sp", lambda e: e.dma_start(out=gbc[:, 0, :], in_=gpre_d.partition_broadcast(128)),
          "const_" + tag, writes=[K("gbc", 0)], group=True)
    P.dma("sp", lambda e: e.dma_start(out=gbc[:, 1, :], in_=gpost_d.partition_broadcast(128)),
          "const_" + tag, writes=[K("gbc", 1)], group=True)

    w13_n = [0]

    def load_w13(fc):
        s = fc % 4
        P.dma("pool", lambda e: e.dma_start(out=w13[:, s], in_=w13r[fc]), "w13_%d" % s,
              writes=[K("w13", s, 0), K("w13", s, 1), K("H", 4 + s)])
        return s

    w2_n = [0]

    def load_w2(dq):
        s = w2_n[0] % 2
        w2_n[0] += 1
        P.dma("pool", lambda e: e.dma_start(out=w2q[:, s].rearrange("p f j -> p (f j)"), in_=w2r[dq]),
              "w2q_%d" % s, writes=[K("w2q", s)])
        return s

    for b in range(2):
        for fc in range(3):
            load_w13(fc)
        for tt in range(8):
            gt = b * 8 + tt
            a = tt % 2
            ss = stt[:, 2 * tt:2 * tt + 1]
            rstd = stt[:, 2 * tt + 1:2 * tt + 2]
            P.op("act", lambda e, gt=gt, ss=ss: e.activation(out=junk, in_=Xv[:, gt, :], func=AF.Square,
                                                           accum_out=ss),
                 reads=[("X", gt)], writes=[K("ss", tt)])
            P.op("dve", lambda e, ss=ss: e.tensor_scalar(out=ss, in0=ss, scalar1=1.0 / D, scalar2=EPS,
                                                       op0=ALU.mult, op1=ALU.add),
                 reads=[K("ss", tt)], writes=[K("ss", tt)])
            P.op("pool", lambda e, ss=ss, rstd=rstd: e.tensor_tensor(out=rstd, in0=ss, in1=c.mhalf, op=ALU.pow),
                 reads=[K("ss", tt), "mhalf"], writes=[K("rstd", tt)])
            P.op("dve", lambda e, gt=gt, a=a, rstd=rstd: e.scalar_tensor_tensor(
                out=xn[:, a, :], in0=Xv[:, gt, :], scalar=rstd, in1=gbc[:, 0, :], op0=ALU.mult, op1=ALU.mult),
                reads=[("X", gt), K("rstd", tt), K("gbc", 0)], writes=[K("xn", a)])
            pb = c.bank(a, BF16)
            for k in range(8):
                P.op("pe", lambda e, k=k, a=a, pb=pb: e.transpose(out=pb[:, k * 128:(k + 1) * 128],
                                                               in_=xn[:, a, k * 128:(k + 1) * 128],
                                                               identity=c.idb),
                     reads=[K("xn", a), "idb"], writes=[("bank", a)])
            P.op("act", lambda e, tt=tt, pb=pb: e.activation(
                out=hT[:, :, tt * 128:(tt + 1) * 128], in_=pb.rearrange("p (k t) -> p k t", k=8), func=AF.Copy),
                reads=[("bank", a)], writes=[K("hT", tt)] + [K("H", i) for i in range(4)])
        s2 = load_w2(0)
        for fc in range(NFC):
            if fc + 3 < NFC:
                load_w13(fc + 3)
            s = fc % 4
            for sb in range(2):
                bi = 2 + 2 * ((fc * 2 + sb) % 2)
                pa, pbk = c.bank(bi), c.bank(bi + 1)
                for w, pt in ((0, pa), (1, pbk)):
                    for k in range(8):
                        P.op("pe", lambda e, w=w, pt=pt, k=k, s=s, sb=sb: e.matmul(
                            pt, lhsT=w13[:, s, w, k * 128:(k + 1) * 128], rhs=hT[:, k, sb * 512:(sb + 1) * 512],
                            start=(k == 0), stop=(k == 7)),
                            reads=[K("w13", s, w)] + [K("hT", 4 * sb + i) for i in range(4)],
                            writes=[("bank", bi + w)])
                sa = (fc * 2 + sb) % 2
                P.op("act", lambda e, pa=pa, sa=sa: e.activation(out=sil[:, sa, :], in_=pa, func=AF.Silu),
                     reads=[("bank", bi)], writes=[K("sil", sa)])
                P.op("dve", lambda e, pbk=pbk, sa=sa, fc=fc, sb=sb: e.tensor_tensor(
                    out=gT[:, fc, sb * 512:(sb + 1) * 512], in0=sil[:, sa, :], in1=pbk, op=ALU.mult),
                    reads=[K("sil", sa), ("bank", bi + 1)], writes=[K("gT", fc)])
        for dq in range(4):
            s = s2
            if dq < 3:
                s2 = load_w2(dq + 1)
            for tt in range(8):
                bi = 6 + (tt % 2)
                pt = c.bank(bi)[:, 0:256]
                for fc in range(NFC):
                    P.op("pe", lambda e, pt=pt, fc=fc, tt=tt, s=s: e.matmul(
                        pt, lhsT=gT[:, fc, tt * 128:(tt + 1) * 128], rhs=w2q[:, s, fc, :],
                        start=(fc == 0), stop=(fc == NFC - 1)),
                        reads=[K("gT", fc), K("w2q", s)], writes=[("bank", bi)])
                P.op("act", lambda e, pt=pt, tt=tt, dq=dq: e.activation(
                    out=H[:, tt, dq * 256:(dq + 1) * 256], in_=pt, func=AF.Copy),
                    reads=[("bank", bi)],
                    writes=[K("H", tt)] + ([K("hT", i) for i in range(8)] if tt < 4 else
                                           [K("w13", tt - 4, 0), K("w13", tt - 4, 1)]))
        for tt in range(8):
            gt = b * 8 + tt
            ss = stt[:, 16 + 2 * tt:16 + 2 * tt + 1]
            rstd = stt[:, 17 + 2 * tt:17 + 2 * tt + 1]
            P.op("act", lambda e, tt=tt, ss=ss: e.activation(out=junk, in_=H[:, tt, :], func=AF.Square,
                                                           accum_out=ss),
                 reads=[K("H", tt)], writes=[K("ss2", tt)])
            P.op("dve", lambda e, ss=ss: e.tensor_scalar(out=ss, in0=ss, scalar1=1.0 / D, scalar2=EPS,
                                                       op0=ALU.mult, op1=ALU.add),
                 reads=[K("ss2", tt)], writes=[K("ss2", tt)])
            P.op("pool", lambda e, ss=ss, rstd=rstd: e.tensor_tensor(out=rstd, in0=ss, in1=c.mhalf, op=ALU.pow),
                 reads=[K("ss2", tt), "mhalf"], writes=[K("rstd2", tt)])
            P.op("pool", lambda e, rstd=rstd: e.tensor_scalar(out=rstd, in0=rstd, scalar1=0.5, scalar2=None,
                                                            op0=ALU.mult),
                 reads=[K("rstd2", tt)], writes=[K("rstd2", tt)])
            P.op("dve", lambda e, tt=tt, rstd=rstd: e.scalar_tensor_tensor(
                out=H[:, tt, :], in0=H[:, tt, :], scalar=rstd, in1=gbc[:, 1, :], op0=ALU.mult, op1=ALU.mult),
                reads=[K("H", tt), K("rstd2", tt), K("gbc", 1)], writes=[K("H", tt)])
            P.op("pool", lambda e, tt=tt, gt=gt: e.tensor_tensor(out=Xv[:, gt, :], in0=Xv[:, gt, :], in1=H[:, tt, :],
                                                                op=ALU.add),
                 reads=[K("H", tt), ("X", gt)], writes=[("X", gt)])
    P.barrier()
    c.release(m)


def setup_common(c, ident_d):
    P = c.P
    load_consts(c, ident_d)
    c.mhalf = c.alloc(1, F32)
    P.op("pool", lambda e: e.memset(c.mhalf, -0.5), writes=["mhalf"])


def build_ffn_prog(nph=1):
    st = contextlib.ExitStack()
    c = Ctx(st)
    x_d = c.dram_in("x", [TOK, D])
    ident_d = c.dram_in("ident", [128, 128])
    ws = []
    for q in range(nph):
        ws.append((c.dram_in("w13r%d" % q, [NFC, 128, 2, 1024]), c.dram_in("w2r%d" % q, [4, 128, NFC * 256]),
                   c.dram_in("gpre%d" % q, [D]), c.dram_in("gpost%d" % q, [D])))
    y_d = c.dram_out("y", [TOK, D])
    setup_common(c, ident_d)
    X = c.alloc(NTT * D, F32)
    load_x(c, X, x_d)
    for q in range(nph):
        ffn_phase(c, X, ws[q][0], ws[q][1], ws[q][2], ws[q][3], "f%d" % q)
    store_x(c, X, y_d)
    c.P.barrier()
    c.P.emit(st)
    return c, st


def lay_w13(w):
    return np.ascontiguousarray(w.reshape(8, 128, NFC, 128).transpose(2, 1, 0, 3).reshape(NFC, 128, 1024))


def lay_w2(w):
    return np.ascontiguousarray(w.reshape(NFC, 128, 4, 256).transpose(2, 1, 0, 3).reshape(4, 128, NFC * 256))


_PROGS = {}


def run_ffn(xs, plist):
    nph = len(plist)
    name = "ffn%d" % nph
    if name not in _PROGS:
        _PROGS[name] = build_ffn_prog(nph)
    c, st = _PROGS[name]
    sh = dict(ident=np.eye(128, dtype=np.float32))
    for q, (w1, w3, w2, gpre, gpost) in enumerate(plist):
        sh["w13r%d" % q] = np.ascontiguousarray(np.stack([lay_w13(w1), lay_w13(w3)], axis=2))
        sh["w2r%d" % q] = lay_w2(w2)
        sh["gpre%d" % q] = np.ascontiguousarray(gpre)
        sh["gpost%d" % q] = np.ascontiguousarray(gpost)
    in_maps = [dict(sh, x=np.ascontiguousarray(xs[i])) for i in range(NCORES)]
    res = run_bass_kernel_spmd(c.nc, in_maps, core_ids=list(range(NCORES)))
    return [res.results[i]["y"] for i in range(NCORES)]


GELU_NATIVE = True


def gelu_act(c, out, in_, bias, reads, writes, tmp=None):
    P = c.P
    P.op("act", lambda e: e.activation(out=out, in_=in_, func=AF.Gelu_apprx_tanh, bias=bias),
         reads=reads, writes=writes)


def gmlp_phase(c, x_d, y_d, d, tag):
    P = c.P
    m = c.mark()
    K = lambda *a: (tag,) + a
    E = 2048
    w_in = c.alloc(8 * 4096, BF16).rearrange("p (k n) -> p k n", k=8)
    w_out = c.alloc(16 * 1024, BF16).rearrange("p (k n) -> p k n", k=16)
    gpost = c.alloc(D, F32)
    gpre = c.alloc(8, F32)
    binu = c.alloc(16, F32)
    lng = c.alloc(16, F32)
    lnb = c.alloc(16, F32)
    xt = c.alloc(2 * D, F32).rearrange("p (a d) -> p a d", a=2)
    xn = c.alloc(2 * D, BF16).rearrange("p (a d) -> p a d", a=2)
    hT = c.alloc(2 * D, BF16).rearrange("p (a k t) -> p a k t", a=2, k=8)
    v = c.alloc(E, F32)
    vhat = c.alloc(E, BF16)
    uT = c.alloc(E, BF16).rearrange("p (k t) -> p k t", k=16)
    gated = c.alloc(E, BF16).rearrange("p (k t) -> p k t", k=16)
    tmpc = c.alloc(E, F32).rearrange("p (k t) -> p k t", k=16)
    rs = c.alloc(1024, F32)
    bs = c.alloc(1024, F32)
    wsf = c.alloc(1024, F32)
    mk = c.alloc(1024, F32)
    wsm = c.alloc(1024, BF16).rearrange("p (h t) -> p h t", h=8)
    Hh = c.alloc(D, F32)
    stmp = c.alloc(2 * 128, F32).rearrange("p (a t) -> p a t", a=2)
    junk = c.alloc(E, BF16)
    brow = c.alloc(E + D + 128, BF16)
    browf = c.alloc(E + D, F32)
    onesb = c.alloc(128, BF16)
    stt = c.alloc(16, F32)

    for k in range(8):
        P.dma("pool", lambda e, k=k: e.dma_start(out=w_in[:, k, :], in_=d["w_in"][:, k, :]), "gm_w",
              writes=[K("w_in", k)], group=True)
    P.dma("pool", lambda e: e.dma_start(out=w_out.rearrange("p k n -> p (k n)"), in_=d["w_out"]), "gm_w",
          writes=[K("w_out")], group=True)
    for dst, src, key in ((gpost, d["gpost"].partition_broadcast(128), "gpost"), (gpre, d["gpre"], "gpre"),
                          (binu, d["b_in_u"], "binu"), (lng, d["ln_g"], "lng"), (lnb, d["ln_b"], "lnb"),
                          (bs, d["b_s"].partition_broadcast(128), "bs"), (wsf, d["wsT"], "wsf"),
                          (mk, d["mask"], "mk"), (browf[0:1, 0:E], d["b_in_v"], "browf"),
                          (browf[0:1, E:E + D], d["b_out"], "browf2")):
        P.dma("sp", lambda e, dst=dst, src=src: e.dma_start(out=dst, in_=src), "const_" + tag, writes=[K(key)],
              group=True)
    P.op("dve", lambda e: e.tensor_copy(out=brow[0:1, 0:E + D], in_=browf[0:1, :]),
         reads=[K("browf"), K("browf2")], writes=[K("brow")])
    P.op("pool", lambda e: e.memset(brow[0:1, E + D:E + D + 128], 1.0), writes=[K("brow1")])
    P.op("pool", lambda e: e.memset(onesb, 1.0), writes=[K("onesb")])
    ones_row = brow[0:1, E + D:E + D + 128]
    P.op("dve", lambda e: e.tensor_tensor(out=wsm.rearrange("p h t -> p (h t)"), in0=wsf, in1=mk, op=ALU.mult),
         reads=[K("wsf"), K("mk")], writes=[K("wsm")])
    for hf in range(2):
        pt = c.bank(hf)
        P.op("pe", lambda e, pt=pt, hf=hf: e.matmul(pt, lhsT=onesb, rhs=wsm.rearrange("p h t -> p (h t)")[:, hf * 512:(hf + 1) * 512],
                                                   start=True, stop=True),
             reads=[K("onesb"), K("wsm")], writes=[("bank", hf)])
        P.op("act", lambda e, pt=pt, hf=hf: e.activation(out=rs[:, hf * 512:(hf + 1) * 512], in_=pt, func=AF.Copy),
             reads=[("bank", hf)], writes=[K("rs", hf)])
    for cc in range(16):
        hd = cc // 2
        P.op("dve", lambda e, cc=cc, hd=hd: e.scalar_tensor_tensor(
            out=tmpc[:, cc, :], in0=rs[:, hd * 128:(hd + 1) * 128], scalar=lnb[:, cc:cc + 1],
            in1=bs[:, hd * 128:(hd + 1) * 128], op0=ALU.mult, op1=ALU.add),
            reads=[K("rs", hd // 4), K("lnb"), K("bs")], writes=[K("tmpc", cc)])

    xv = x_d.rearrange("(t p) d -> t p d", p=128)
    yv = y_d.rearrange("(t p) d -> t p d", p=128)
    for tt in range(NTT):
        a = tt % 2
        P.dma("sp", lambda e, tt=tt, a=a: e.dma_start(out=xt[:, a, :], in_=xv[tt]), "gm_x%d" % a,
              writes=[K("xt", a)])
        ss, rstd = stt[:, 0:1], stt[:, 1:2]
        P.op("act", lambda e, a=a: e.activation(out=junk[:, 0:D], in_=xt[:, a, :], func=AF.Square, accum_out=ss),
             reads=[K("xt", a)], writes=[K("ss")])
        P.op("dve", lambda e: e.tensor_scalar(out=ss, in0=ss, scalar1=1.0 / D, scalar2=EPS, op0=ALU.mult, op1=ALU.add),
             reads=[K("ss")], writes=[K("ss")])
        P.op("pool", lambda e: e.tensor_tensor(out=rstd, in0=ss, in1=c.mhalf, op=ALU.pow),
             reads=[K("ss"), "mhalf"], writes=[K("rstd")])
        P.op("dve", lambda e, a=a: e.tensor_scalar(out=xn[:, a, :], in0=xt[:, a, :], scalar1=rstd, scalar2=None,
                                                   op0=ALU.mult),
             reads=[K("xt", a), K("rstd")], writes=[K("xn", a)])
        pb = c.bank(a, BF16)
        for k in range(8):
            P.op("pe", lambda e, k=k, a=a, pb=pb: e.transpose(out=pb[:, k * 128:(k + 1) * 128],
                                                           in_=xn[:, a, k * 128:(k + 1) * 128], identity=c.idb),
                 reads=[K("xn", a), "idb"], writes=[("bank", a)])
        for k in range(8):
            P.op("act", lambda e, k=k, a=a, pb=pb: e.activation(out=hT[:, a, k, :], in_=pb[:, k * 128:(k + 1) * 128],
                                                             func=AF.Copy, scale=gpre[:, k:k + 1]),
                 reads=[("bank", a), K("gpre")], writes=[K("hT", a)])
        for cb in range(4):
            bi = 2 + cb % 2
            pt = c.bank(bi)
            for k in range(8):
                P.op("pe", lambda e, k=k, a=a, pt=pt, cb=cb: e.matmul(
                    pt, lhsT=hT[:, a, k, :], rhs=w_in[:, k, E + cb * 512:E + (cb + 1) * 512], start=(k == 0), stop=False),
                    reads=[K("hT", a), K("w_in", k)], writes=[("bank", bi)])
            P.op("pe", lambda e, pt=pt, cb=cb: e.matmul(pt, lhsT=ones_row, rhs=brow[0:1, cb * 512:(cb + 1) * 512],
                                                       start=False, stop=True),
                 reads=[K("brow"), K("brow1")], writes=[("bank", bi)])
            P.op("act", lambda e, pt=pt, cb=cb: e.activation(out=v[:, cb * 512:(cb + 1) * 512], in_=pt,
                                                           func=AF.Gelu_apprx_tanh, accum_out=stt[:, 4 + cb:5 + cb]),
                 reads=[("bank", bi)], writes=[K("v", cb), K("s1", cb)])
        P.op("act", lambda e: e.activation(out=junk, in_=v, func=AF.Square, accum_out=stt[:, 8:9]),
             reads=[K("v", i) for i in range(4)], writes=[K("s2")])
        P.op("dve", lambda e: e.tensor_reduce(out=stt[:, 9:10], in_=stt[:, 4:8], axis=AX.X, op=ALU.add),
             reads=[K("s1", i) for i in range(4)], writes=[K("mean")])
        P.op("dve", lambda e: e.tensor_scalar(out=stt[:, 9:10], in0=stt[:, 9:10], scalar1=1.0 / E, scalar2=None,
                                              op0=ALU.mult), reads=[K("mean")], writes=[K("mean")])
        P.op("dve", lambda e: e.tensor_tensor(out=stt[:, 10:11], in0=stt[:, 9:10], in1=stt[:, 9:10], op=ALU.mult),
             reads=[K("mean")], writes=[K("msq")])
        P.op("dve", lambda e: e.scalar_tensor_tensor(out=stt[:, 11:12], in0=stt[:, 8:9], scalar=1.0 / E,
                                                     in1=stt[:, 10:11], op0=ALU.mult, op1=ALU.subtract),
             reads=[K("s2"), K("msq")], writes=[K("var")])
        P.op("dve", lambda e: e.tensor_scalar(out=stt[:, 11:12], in0=stt[:, 11:12], scalar1=EPS, scalar2=None,
                                              op0=ALU.add), reads=[K("var")], writes=[K("var")])
        P.op("pool", lambda e: e.tensor_tensor(out=stt[:, 12:13], in0=stt[:, 11:12], in1=c.mhalf, op=ALU.pow),
             reads=[K("var"), "mhalf"], writes=[K("lrstd")])
        P.op("dve", lambda e: e.tensor_scalar(out=vhat, in0=v, scalar1=stt[:, 9:10], scalar2=stt[:, 12:13],
                                              op0=ALU.subtract, op1=ALU.mult),
             reads=[K("v", i) for i in range(4)] + [K("mean"), K("lrstd")], writes=[K("vhat")])
        for g4 in range(4):
            bi = 4 + g4 % 2
            for cc in range(4 * g4, 4 * g4 + 4):
                pt = c.bank(bi)[:, (cc % 4) * 128:(cc % 4 + 1) * 128]
                for k in range(8):
                    P.op("pe", lambda e, k=k, a=a, pt=pt, cc=cc: e.matmul(
                        pt, lhsT=w_in[:, k, cc * 128:(cc + 1) * 128], rhs=hT[:, a, k, :], start=(k == 0), stop=(k == 7)),
                        reads=[K("hT", a), K("w_in", k)], writes=[("bank", bi)])
            for cc in range(4 * g4, 4 * g4 + 4):
                pt = c.bank(bi)[:, (cc % 4) * 128:(cc % 4 + 1) * 128]
                P.op("act", lambda e, pt=pt, cc=cc: e.activation(out=uT[:, cc, :], in_=pt, func=AF.Gelu_apprx_tanh,
                                                               bias=binu[:, cc:cc + 1]),
                     reads=[("bank", bi), K("binu")], writes=[K("uT", cc)])
        for g4 in range(4):
            bi = 6 + g4 % 2
            for cc in range(4 * g4, 4 * g4 + 4):
                pt = c.bank(bi)[:, (cc % 4) * 128:(cc % 4 + 1) * 128]
                hd = cc // 2
                P.op("pe", lambda e, pt=pt, cc=cc, hd=hd: e.matmul(pt, lhsT=vhat[:, cc * 128:(cc + 1) * 128],
                                                                  rhs=wsm[:, hd, :], start=True, stop=True),
                     reads=[K("vhat"), K("wsm")], writes=[("bank", bi)])
            for cc in range(4 * g4, 4 * g4 + 4):
                pt = c.bank(bi)[:, (cc % 4) * 128:(cc % 4 + 1) * 128]
                sa = cc % 2
                P.op("dve", lambda e, pt=pt, cc=cc, sa=sa: e.scalar_tensor_tensor(
                    out=stmp[:, sa, :], in0=pt, scalar=lng[:, cc:cc + 1], in1=tmpc[:, cc, :], op0=ALU.mult, op1=ALU.add),
                    reads=[("bank", bi), K("lng"), K("tmpc", cc)], writes=[K("stmp", sa)])
                P.op("dve", lambda e, cc=cc, sa=sa: e.tensor_tensor(out=gated[:, cc, :], in0=uT[:, cc, :],
                                                                    in1=stmp[:, sa, :], op=ALU.mult),
                     reads=[K("uT", cc), K("stmp", sa)], writes=[K("gated", cc)])
        for hf in range(2):
            bi = 2 + hf
            pt = c.bank(bi)
            for cc in range(16):
                P.op("pe", lambda e, pt=pt, cc=cc, hf=hf: e.matmul(
                    pt, lhsT=gated[:, cc, :], rhs=w_out[:, cc, hf * 512:(hf + 1) * 512], start=(cc == 0), stop=False),
                    reads=[K("gated", cc), K("w_out")], writes=[("bank", bi)])
            P.op("pe", lambda e, pt=pt, hf=hf: e.matmul(pt, lhsT=ones_row, rhs=brow[0:1, E + hf * 512:E + (hf + 1) * 512],
                                                       start=False, stop=True),
                 reads=[K("brow"), K("brow1")], writes=[("bank", bi)])
            P.op("act", lambda e, pt=pt, hf=hf: e.activation(out=Hh[:, hf * 512:(hf + 1) * 512], in_=pt, func=AF.Copy),
                 reads=[("bank", bi)], writes=[K("Hh", hf)])
        post_norm_add(c, Hh, [K("Hh", 0), K("Hh", 1)], xt[:, a, :], K("xt", a), gpost, K("gpost"), junk[:, 0:D],
                      stt[:, 13:14], stt[:, 14:15], K, 1.0)
        P.dma("sp", lambda e, tt=tt, a=a: e.dma_start(out=yv[tt], in_=xt[:, a, :]), "gm_y%d" % a,
              reads=[K("xt", a)])
    P.barrier()
    c.release(m)


def post_norm_add(c, Hh, hkeys, xtile, xkey, gpost, gkey, junk, ss, rstd, K, coef):
    P = c.P
    P.op("act", lambda e: e.activation(out=junk, in_=Hh, func=AF.Square, accum_out=ss),
         reads=hkeys, writes=[K("pss")])
    P.op("dve", lambda e: e.tensor_scalar(out=ss, in0=ss, scalar1=1.0 / D, scalar2=EPS, op0=ALU.mult, op1=ALU.add),
         reads=[K("pss")], writes=[K("pss")])
    P.op("pool", lambda e: e.tensor_tensor(out=rstd, in0=ss, in1=c.mhalf, op=ALU.pow),
         reads=[K("pss"), "mhalf"], writes=[K("prstd")])
    if coef != 1.0:
        P.op("pool", lambda e: e.tensor_scalar(out=rstd, in0=rstd, scalar1=coef, scalar2=None, op0=ALU.mult),
             reads=[K("prstd")], writes=[K("prstd")])
    P.op("dve", lambda e: e.scalar_tensor_tensor(out=Hh, in0=Hh, scalar=rstd, in1=gpost, op0=ALU.mult, op1=ALU.mult),
         reads=hkeys + [K("prstd"), gkey], writes=hkeys)
    P.op("pool", lambda e: e.tensor_tensor(out=xtile, in0=xtile, in1=Hh, op=ALU.add),
         reads=hkeys + [xkey], writes=[xkey])


def build_gmlp_prog():
    st = contextlib.ExitStack()
    c = Ctx(st)
    x_d = c.dram_in("x", [TOK, D])
    ident_d = c.dram_in("ident", [128, 128])
    d = dict(w_in=c.dram_in("w_in", [128, 8, 4096]), w_out=c.dram_in("w_out", [128, 16 * 1024]),
             gpost=c.dram_in("gpost", [D]), gpre=c.dram_in("gpre", [128, 8]), b_in_u=c.dram_in("b_in_u", [128, 16]),
             ln_g=c.dram_in("ln_g", [128, 16]), ln_b=c.dram_in("ln_b", [128, 16]), b_s=c.dram_in("b_s", [1024]),
             wsT=c.dram_in("wsT", [128, 1024]), mask=c.dram_in("mask", [128, 1024]),
             b_in_v=c.dram_in("b_in_v", [1, 2048]), b_out=c.dram_in("b_out", [1, D]))
    y_d = c.dram_out("y", [TOK, D])
    setup_common(c, ident_d)
    gmlp_phase(c, x_d, y_d, d, "gm")
    c.P.barrier()
    c.P.emit(st)
    return c, st


def col128(v, n):
    return np.ascontiguousarray(v.reshape(n, 128).T)


def gmlp_inputs(I, i):
    j = i // 4
    w_in = I["gm_w_in"][j]
    causal = np.tril(np.ones((128, 128), np.float32))
    mask = np.ascontiguousarray(np.tile(causal.T[:, None, :], (1, 8, 1)).reshape(128, 1024))
    return dict(
        w_in=np.ascontiguousarray(w_in.reshape(8, 128, 4096).transpose(1, 0, 2)),
        w_out=np.ascontiguousarray(I["gm_w_out"][j].reshape(16, 128, 1024).transpose(1, 0, 2).reshape(128, 16 * 1024)),
        gpost=np.ascontiguousarray(I["norm_post"][i, 1]), gpre=col128(I["norm_pre"][i, 1], 8),
        b_in_u=col128(I["gm_b_in"][j][:2048], 16), ln_g=col128(I["gm_ln_g"][j], 16), ln_b=col128(I["gm_ln_b"][j], 16),
        b_s=np.ascontiguousarray(I["gm_b_s"][j].reshape(1024)),
        wsT=np.ascontiguousarray(I["gm_w_s"][j].transpose(2, 0, 1).reshape(128, 1024)),
        mask=mask, b_in_v=np.ascontiguousarray(I["gm_b_in"][j][2048:].reshape(1, 2048)),
        b_out=np.ascontiguousarray(I["gm_b_out"][j].reshape(1, D)))


def run_prog(name, builder, shared, xs):
    if name not in _PROGS:
        _PROGS[name] = builder()
    c, st = _PROGS[name]
    ident = np.eye(128, dtype=np.float32)
    in_maps = []
    for i in range(NCORES):
        mp = dict(shared)
        mp["ident"] = ident
        mp.update(xs[i])
        in_maps.append(mp)
    res = run_bass_kernel_spmd(c.nc, in_maps, core_ids=list(range(NCORES)))
    return res.results


def norm_T(c, K, xt_ap, xkey, xn_ap, xnkey, dst, dkey, gpre, gkey, bi, junk, ss, rstd):
    P = c.P
    P.op("act", lambda e: e.activation(out=junk, in_=xt_ap, func=AF.Square, accum_out=ss),
         reads=[xkey], writes=[K("nss")])
    P.op("dve", lambda e: e.tensor_scalar(out=ss, in0=ss, scalar1=1.0 / D, scalar2=EPS, op0=ALU.mult, op1=ALU.add),
         reads=[K("nss")], writes=[K("nss")])
    P.op("pool", lambda e: e.tensor_tensor(out=rstd, in0=ss, in1=c.mhalf, op=ALU.pow),
         reads=[K("nss"), "mhalf"], writes=[K("nrstd")])
    P.op("dve", lambda e: e.tensor_scalar(out=xn_ap, in0=xt_ap, scalar1=rstd, scalar2=None, op0=ALU.mult),
         reads=[xkey, K("nrstd")], writes=[xnkey])
    pb = c.bank(bi, BF16)
    for k in range(8):
        P.op("pe", lambda e, k=k: e.transpose(out=pb[:, k * 128:(k + 1) * 128], in_=xn_ap[:, k * 128:(k + 1) * 128],
                                              identity=c.idb),
             reads=[xnkey, "idb"], writes=[("bank", bi)])
    for k in range(8):
        P.op("act", lambda e, k=k: e.activation(out=dst[:, k, :], in_=pb[:, k * 128:(k + 1) * 128], func=AF.Copy,
                                                scale=gpre[:, k:k + 1]),
             reads=[("bank", bi), gkey], writes=[dkey])


def conv_phase(c, x_d, y_d, d, tag):
    P = c.P
    m = c.mark()
    K = lambda *a: (tag,) + a
    NB = TOK // 256
    w_in = c.alloc(8 * 2048, BF16).rearrange("p (k n) -> p k n", k=8)
    w_out = c.alloc(8 * 1024, BF16).rearrange("p (k n) -> p k n", k=8)
    hT = c.alloc(8 * (128 + TOK), BF16).rearrange("p (k t) -> p k t", k=8)
    gpost = c.alloc(D, F32)
    sm = c.alloc(8 * 6 + 8 * 31 + 1, F32)
    gpre, bia, big, dwb, lng, lnb = [sm[:, 8 * i:8 * i + 8] for i in range(6)]
    dw = sm[:, 48:48 + 248].rearrange("p (k j) -> p k j", k=8)
    flag = sm[:, 296:297]
    xt = c.alloc(2 * D, F32).rearrange("p (a d) -> p a d", a=2)
    xn = c.alloc(2 * D, BF16).rearrange("p (a d) -> p a d", a=2)
    sg = c.alloc(2 * 288, F32).rearrange("p (a t) -> p a t", a=2)
    z = c.alloc(2 * 288, BF16).rearrange("p (a t) -> p a t", a=2)
    dg = c.alloc(8 * 31 * 128, BF16).rearrange("p (k j n) -> p k j n", k=8, j=31)
    y = c.alloc(8 * 256, F32).rearrange("p (k t) -> p k t", k=8)
    yb = c.alloc(8 * 256, BF16).rearrange("p (k t) -> p k t", k=8)
    ysq = c.alloc(8 * 256, BF16).rearrange("p (k t) -> p k t", k=8)
    mean = c.alloc(256, F32)
    var = c.alloc(256, F32)
    tt_ = c.alloc(2 * 256, F32).rearrange("p (a t) -> p a t", a=2)
    znT = c.alloc(8 * 256, BF16).rearrange("p (k t) -> p k t", k=8)
    Hh = c.alloc(D, F32)
    junk = c.alloc(D, BF16)
    brow = c.alloc(D + 128, BF16)
    onesb = c.alloc(128, BF16)
    mh256 = c.alloc(256, F32)
    stt = c.alloc(8, F32)

    for k in range(8):
        P.dma("pool", lambda e, k=k: e.dma_start(out=w_in[:, k, :], in_=d["w_in"][:, k, :]), "cv_w",
              writes=[K("w_in", k)], group=True)
    P.dma("pool", lambda e: e.dma_start(out=w_out.rearrange("p k n -> p (k n)"), in_=d["w_out"]), "cv_w",
          writes=[K("w_out")], group=True)
    for dst, src, key in ((gpost, d["gpost"].partition_broadcast(128), "gpost"), (sm[:, 0:297], d["small"], "sm"),
                          ):
        P.dma("sp", lambda e, dst=dst, src=src: e.dma_start(out=dst, in_=src), "const_" + tag, writes=[K(key)],
              group=True)
    P.dma("pool", lambda e: e.dma_start(out=brow[0:1, 0:D], in_=d["b_out"]), "cv_w", writes=[K("brow")], group=True)
    P.op("pool", lambda e: e.memset(brow[0:1, D:D + 128], 1.0), writes=[K("brow1")])
    P.op("pool", lambda e: e.memset(onesb, 1.0), writes=[K("onesb")])
    P.op("pool", lambda e: e.memset(mh256, -0.5), writes=[K("mh256")])
    ones_row = brow[0:1, D:D + 128]
    for cc in range(8):
        for j in range(31):
            P.op("pool", lambda e, cc=cc, j=j: e.tensor_scalar(out=dg[:, cc, j, :], in0=c.idb, scalar1=dw[:, cc, j:j + 1],
                                                               scalar2=None, op0=ALU.mult),
                 reads=["idb", K("sm")], writes=[K("dg", cc)])

    xv = x_d.rearrange("(t p) d -> t p d", p=128)
    yv = y_d.rearrange("(t p) d -> t p d", p=128)
    for ti in range(NTT + 1):
        a = ti % 2
        src = d["xh"] if ti == 0 else xv[ti - 1]
        P.dma("sp", lambda e, src=src, a=a: e.dma_start(out=xt[:, a, :], in_=src), "cv_x%d" % a, writes=[K("xt", a)])
        norm_T(c, K, xt[:, a, :], K("xt", a), xn[:, a, :], K("xn", a), hT[:, :, ti * 128:(ti + 1) * 128],
               K("hT", ti), gpre, K("sm"), a, junk, stt[:, 0:1], stt[:, 1:2])
    for blk in range(NB):
        c0 = 128 + blk * 256 - 32
        tiles = list(range(c0 // 128, (c0 + 287) // 128 + 1))
        hkeys = [K("hT", t) for t in tiles]
        for cc in range(8):
            sa = cc % 2
            ba, bg = 2 + 2 * sa, 3 + 2 * sa
            pa, pg = c.bank(ba)[:, 0:288], c.bank(bg)[:, 0:288]
            for (pt, bi, off) in ((pa, ba, 0), (pg, bg, 1024)):
                for k in range(8):
                    P.op("pe", lambda e, pt=pt, k=k, cc=cc, off=off, c0=c0: e.matmul(
                        pt, lhsT=w_in[:, k, off + cc * 128:off + (cc + 1) * 128], rhs=hT[:, k, c0:c0 + 288],
                        start=(k == 0), stop=(k == 7)),
                        reads=hkeys + [K("w_in", k)], writes=[("bank", bi)])
            P.op("act", lambda e, pg=pg, sa=sa, cc=cc: e.activation(out=sg[:, sa, :], in_=pg, func=AF.Sigmoid,
                                                                 bias=big[:, cc:cc + 1]),
                 reads=[("bank", bg), K("sm")], writes=[K("sg", sa)])
            P.op("dve", lambda e, pa=pa, sa=sa, cc=cc: e.scalar_tensor_tensor(
                out=z[:, sa, :], in0=pa, scalar=bia[:, cc:cc + 1], in1=sg[:, sa, :], op0=ALU.add, op1=ALU.mult),
                reads=[("bank", ba), K("sg", sa), K("sm")], writes=[K("z", sa)])
            if blk == 0:
                P.op("dve", lambda e, sa=sa: e.tensor_scalar(out=z[:, sa, 0:32], in0=z[:, sa, 0:32], scalar1=flag,
                                                             scalar2=None, op0=ALU.mult),
                     reads=[K("z", sa), K("sm")], writes=[K("z", sa)])
            bt = 6 + cc % 2
            py = c.bank(bt)[:, 0:256]
            for j in range(31):
                P.op("pe", lambda e, py=py, sa=sa, cc=cc, j=j: e.matmul(py, lhsT=dg[:, cc, j, :], rhs=z[:, sa, 2 + j:258 + j],
                                                                     start=(j == 0), stop=(j == 30)),
                     reads=[K("dg", cc), K("z", sa)], writes=[("bank", bt)])
            P.op("act", lambda e, py=py, cc=cc: e.activation(out=y[:, cc, :], in_=py, func=AF.Identity, bias=dwb[:, cc:cc + 1]),
                 reads=[("bank", bt), K("sm")], writes=[K("y", cc)])
            P.op("act", lambda e, cc=cc: e.activation(out=yb[:, cc, :], in_=y[:, cc, :], func=AF.Copy),
                 reads=[K("y", cc)], writes=[K("yb", cc)])
            P.op("act", lambda e, cc=cc: e.activation(out=ysq[:, cc, :], in_=y[:, cc, :], func=AF.Square),
                 reads=[K("y", cc)], writes=[K("ysq", cc)])
        pm, pq = c.bank(6)[:, 0:256], c.bank(7)[:, 0:256]
        for cc in range(8):
            P.op("pe", lambda e, cc=cc: e.matmul(pm, lhsT=onesb, rhs=yb[:, cc, :], start=(cc == 0), stop=(cc == 7)),
                 reads=[K("onesb"), K("yb", cc)], writes=[("bank", 6)])
        for cc in range(8):
            P.op("pe", lambda e, cc=cc: e.matmul(pq, lhsT=onesb, rhs=ysq[:, cc, :], start=(cc == 0), stop=(cc == 7)),
                 reads=[K("onesb"), K("ysq", cc)], writes=[("bank", 7)])
        P.op("dve", lambda e: e.tensor_scalar(out=mean, in0=pm, scalar1=1.0 / D, scalar2=None, op0=ALU.mult),
             reads=[("bank", 6)], writes=[K("mean")])
        P.op("dve", lambda e: e.tensor_tensor(out=var, in0=mean, in1=mean, op=ALU.mult),
             reads=[K("mean")], writes=[K("var")])
        P.op("dve", lambda e: e.scalar_tensor_tensor(out=var, in0=pq, scalar=1.0 / D, in1=var, op0=ALU.mult,
                                                     op1=ALU.subtract),
             reads=[("bank", 7), K("var")], writes=[K("var")])
        P.op("dve", lambda e: e.tensor_scalar(out=var, in0=var, scalar1=EPS, scalar2=None, op0=ALU.add),
             reads=[K("var")], writes=[K("var")])
        P.op("pool", lambda e: e.tensor_tensor(out=var, in0=var, in1=mh256, op=ALU.pow),
             reads=[K("var"), K("mh256")], writes=[K("var")])
        for cc in range(8):
            sa = cc % 2
            P.op("dve", lambda e, cc=cc, sa=sa: e.tensor_tensor(out=tt_[:, sa, :], in0=y[:, cc, :], in1=mean,
                                                                op=ALU.subtract),
                 reads=[K("y", cc), K("mean")], writes=[K("t", sa)])
            P.op("dve", lambda e, sa=sa: e.tensor_tensor(out=tt_[:, sa, :], in0=tt_[:, sa, :], in1=var, op=ALU.mult),
                 reads=[K("t", sa), K("var")], writes=[K("t", sa)])
            P.op("act", lambda e, cc=cc, sa=sa: e.activation(out=znT[:, cc, :], in_=tt_[:, sa, :], func=AF.Silu,
                                                           scale=lng[:, cc:cc + 1], bias=lnb[:, cc:cc + 1]),
                 reads=[K("t", sa), K("sm")], writes=[K("znT", cc)])
        for t2 in range(2):
            ti = blk * 2 + t2
            a = ti % 2
            P.dma("sp", lambda e, ti=ti, a=a: e.dma_start(out=xt[:, a, :], in_=xv[ti]), "cv_x%d" % a,
                  writes=[K("xt", a)])
            for hf in range(2):
                bi = hf
                pt = c.bank(bi)
                for cc in range(8):
                    P.op("pe", lambda e, pt=pt, cc=cc, hf=hf, t2=t2: e.matmul(
                        pt, lhsT=znT[:, cc, t2 * 128:(t2 + 1) * 128], rhs=w_out[:, cc, hf * 512:(hf + 1) * 512],
                        start=(cc == 0), stop=False),
                        reads=[K("znT", cc), K("w_out")], writes=[("bank", bi)])
                P.op("pe", lambda e, pt=pt, hf=hf: e.matmul(pt, lhsT=ones_row, rhs=brow[0:1, hf * 512:(hf + 1) * 512],
                                                           start=False, stop=True),
                     reads=[K("brow"), K("brow1")], writes=[("bank", bi)])
                P.op("act", lambda e, pt=pt, hf=hf: e.activation(out=Hh[:, hf * 512:(hf + 1) * 512], in_=pt, func=AF.Copy),
                     reads=[("bank", bi)], writes=[K("Hh", hf)])
            post_norm_add(c, Hh, [K("Hh", 0), K("Hh", 1)], xt[:, a, :], K("xt", a), gpost, K("gpost"), junk,
                          stt[:, 2:3], stt[:, 3:4], K, 1.0)
            P.dma("sp", lambda e, ti=ti, a=a: e.dma_start(out=yv[ti], in_=xt[:, a, :]), "cv_y%d" % a,
                  reads=[K("xt", a)])
    P.barrier()
    c.release(m)


def build_conv_prog():
    st = contextlib.ExitStack()
    c = Ctx(st)
    x_d = c.dram_in("x", [TOK, D])
    ident_d = c.dram_in("ident", [128, 128])
    d = dict(w_in=c.dram_in("w_in", [128, 8, 2048]), w_out=c.dram_in("w_out", [128, 8 * 1024]),
             gpost=c.dram_in("gpost", [D]), small=c.dram_in("small", [128, 297]), b_out=c.dram_in("b_out", [1, D]),
             xh=c.dram_in("xh", [128, D]))
    y_d = c.dram_out("y", [TOK, D])
    setup_common(c, ident_d)
    conv_phase(c, x_d, y_d, d, "cv")
    c.P.barrier()
    c.P.emit(st)
    return c, st


def conv_inputs(I, i):
    j = i // 4
    small = np.zeros((128, 297), np.float32)
    small[:, 0:8] = col128(I["norm_pre"][i, 1], 8)
    small[:, 8:16] = col128(I["cv_b_in"][j][:1024], 8)
    small[:, 16:24] = col128(I["cv_b_in"][j][1024:], 8)
    small[:, 24:32] = col128(I["cv_dw_b"][j], 8)
    small[:, 32:40] = col128(I["cv_ln_g"][j], 8)
    small[:, 40:48] = col128(I["cv_ln_b"][j], 8)
    small[:, 48:296] = I["cv_dw"][j].reshape(31, 8, 128).transpose(2, 1, 0).reshape(128, 248)
    return dict(
        w_in=np.ascontiguousarray(I["cv_w_in"][j].reshape(8, 128, 2048).transpose(1, 0, 2)),
        w_out=np.ascontiguousarray(I["cv_w_out"][j].reshape(8, 128, 1024).transpose(1, 0, 2).reshape(128, 8 * 1024)),
        gpost=np.ascontiguousarray(I["norm_post"][i, 1]), small=small,
        b_out=np.ascontiguousarray(I["cv_b_out"][j].reshape(1, D)))


def conv_percore(small, xfull, i):
    sm = small.copy()
    sm[:, 296] = 0.0 if i == 0 else 1.0
    xh = np.zeros((128, D), np.float32) if i == 0 else np.ascontiguousarray(xfull[i * TOK - 128:i * TOK])
    return dict(x=np.ascontiguousarray(xfull[i * TOK:(i + 1) * TOK]), xh=xh, small=sm)


PATTERNS = ((128, 1), (512, 4), (2048, 16))
NEG = -30000.0
ATT_DEBUG = 0


def t5_onehot():
    oh = np.zeros((3, 32, 129), np.float32)
    for g, (_, dil) in enumerate(PATTERNS):
        dist = (np.arange(129) * dil).astype(np.int32)
        distf = np.maximum(dist, 1).astype(np.float32)
        large = 16 + (np.log(distf / np.float32(16)) / np.float32(np.log(2048 / 16)) * np.float32(16)).astype(np.int32)
        large = np.minimum(large, 31)
        b = np.where(dist < 16, dist, large)
        oh[g, b, np.arange(129)] = 1.0
    return oh


def attn_phase(c, x_d, y_d, d, tag):
    P = c.P
    nc = c.nc
    m = c.mark()
    K = lambda *a: (tag,) + a
    T2 = 2 * TOK
    hT = c.alloc(8 * T2, BF16).rearrange("p (k t) -> p k t", k=8)
    OT = c.alloc(8 * TOK, BF16).rearrange("p (h t) -> p h t", h=8)
    gpost = c.alloc(D, F32)
    sm = c.alloc(16, F32)
    gpre, hneg = sm[:, 0:8], sm[:, 8:9]
    junk = c.alloc(D, BF16)
    stt = c.alloc(8, F32)
    onesE = c.alloc(2 * 128, BF16).rearrange("p (h n) -> p h n", h=2)
    aid = c.alloc(128, F32)
    aidb = c.alloc(128, BF16)
    Fd = nc.dram_tensor("Fd_" + tag, [48, 384], F32)
    xv = x_d.rearrange("(t p) d -> t p d", p=128)
    xhv = d["xh"].rearrange("(t p) d -> t p d", p=128)
    yv = y_d.rearrange("(t p) d -> t p d", p=128)

    for dst, src, key in ((gpost, d["gpost"].partition_broadcast(128), "gpost"), (sm[:, 0:9], d["small"], "sm"),
                          (aid, d["antiid"], "aid")):
        P.dma("sp", lambda e, dst=dst, src=src: e.dma_start(out=dst, in_=src), "const_" + tag, writes=[K(key)],
              group=True)
    P.op("dve", lambda e: e.tensor_copy(out=aidb, in_=aid), reads=[K("aid")], writes=[K("aidb")])
    P.op("pool", lambda e: e.memset(onesE, 0.0), writes=[K("onesE")])
    P.op("pool", lambda e: e.memset(onesE[:, 0, 0:64], 1.0), writes=[K("onesE")])
    P.op("pool", lambda e: e.memset(onesE[:, 1, 64:128], 1.0), writes=[K("onesE")])

    m1 = c.mark()
    rb = c.alloc(48, F32)
    oh = c.alloc(3 * 129, F32).rearrange("p (g n) -> p g n", g=3)
    Fs = c.alloc(384, F32)
    stg = c.alloc(3 * 129, F32).rearrange("p (g n) -> p g n", g=3)
    P.dma("sp", lambda e: e.dma_start(out=rb[0:32, :], in_=d["rel_bias"]), "const2_" + tag, writes=[K("rb")], group=True)
    P.dma("sp", lambda e: e.dma_start(out=oh[0:32], in_=d["onehot"]), "const2_" + tag, writes=[K("oh")], group=True)
    P.op("pool", lambda e: e.memset(Fs[0:48, :], NEG), writes=[K("Fs")])
    P.dma("sp", lambda e: e.dma_start(out=Fd.ap(), in_=Fs[0:48, :]), "fd_" + tag, reads=[K("Fs")], writes=[K("Fd")])
    rbh = c.alloc(48, BF16)
    rbl = c.alloc(48, BF16)
    rbr = c.alloc(48, F32)
    ohb = c.alloc(3 * 129, BF16).rearrange("p (g n) -> p g n", g=3)
    P.op("dve", lambda e: e.tensor_copy(out=rbh[0:32, :], in_=rb[0:32, :]), reads=[K("rb")], writes=[K("rbh")])
    P.op("dve", lambda e: e.tensor_tensor(out=rbr[0:32, :], in0=rb[0:32, :], in1=rbh[0:32, :], op=ALU.subtract),
         reads=[K("rb"), K("rbh")], writes=[K("rbr")])
    P.op("dve", lambda e: e.tensor_copy(out=rbl[0:32, :], in_=rbr[0:32, :]), reads=[K("rbr")], writes=[K("rbl")])
    P.op("dve", lambda e: e.tensor_copy(out=ohb[0:32], in_=oh[0:32]), reads=[K("oh")], writes=[K("ohb")])
    for g in range(3):
        pt = c.bank(g)[0:48, 0:129]
        P.op("pe", lambda e, g=g, pt=pt: e.matmul(pt, lhsT=rbh[0:32, :], rhs=ohb[0:32, g, :], start=True, stop=False),
             reads=[K("rbh"), K("ohb")], writes=[("bank", g)])
        P.op("pe", lambda e, g=g, pt=pt: e.matmul(pt, lhsT=rbl[0:32, :], rhs=ohb[0:32, g, :], start=False, stop=True),
             reads=[K("rbl"), K("ohb")], writes=[("bank", g)])
        P.op("act", lambda e, g=g, pt=pt: e.activation(out=stg[0:48, g, :], in_=pt, func=AF.Copy),
             reads=[("bank", g)], writes=[K("stg", g)])
        P.dma("sp", lambda e, g=g: e.dma_start(out=Fd.ap()[16 * g:16 * g + 16, 127:256],
                                               in_=stg[16 * g:16 * g + 16, g, :]),
              "fd_" + tag, reads=[K("stg", g)], writes=[K("Fd")])

    xt = c.alloc(2 * D, F32).rearrange("p (a d) -> p a d", a=2)
    xn = c.alloc(2 * D, BF16).rearrange("p (a d) -> p a d", a=2)
    for ti in range(2 * NTT):
        a = ti % 2
        src = xhv[ti] if ti < NTT else xv[ti - NTT]
        P.dma("sp", lambda e, src=src, a=a: e.dma_start(out=xt[:, a, :], in_=src), "at_x%d" % a, writes=[K("xt", a)])
        norm_T(c, K, xt[:, a, :], K("xt", a), xn[:, a, :], K("xn", a), hT[:, :, ti * 128:(ti + 1) * 128],
               K("hT", ti), gpre, K("sm"), a, junk, stt[:, 0:1], stt[:, 1:2])
    P.barrier()
    c.release(m1)

    m2 = c.mark()
    wq = c.alloc(2 * 3 * 1024, BF16).rearrange("p (a j k n) -> p a j k n", a=2, j=3, k=8)
    qTm = c.alloc(2 * TOK, BF16).rearrange("p (h t) -> p h t", h=2)
    kT = c.alloc(T2, BF16)
    VE = c.alloc(32 * 2 * 128, BF16).rearrange("p (u h n) -> p u h n", u=32, h=2)
    accn = c.alloc(TOK, F32)
    accd = c.alloc(TOK, F32)
    bT = c.alloc(2 * 512, F32).rearrange("p (a x) -> p a x", a=2)
    bTs = c.alloc(2 * 512, F32).rearrange("p (a x) -> p a x", a=2)
    bTh = c.alloc(2 * 512, BF16).rearrange("p (a x) -> p a x", a=2)
    bTl = c.alloc(2 * 512, BF16).rearrange("p (a x) -> p a x", a=2)
    sc = c.alloc(2 * 256, F32).rearrange("p (a x) -> p a x", a=2)
    pT = c.alloc(2 * 512, BF16).rearrange("p (a h x) -> p a h x", a=2, h=2)
    P.op("pool", lambda e: e.memset(VE.rearrange("p u h n -> p (u h n)"), 0.0), writes=[K("VE")])
    P.op("pool", lambda e: e.memset(qTm.rearrange("p h t -> p (h t)"), 0.0), writes=[K("qT")])
    hkeys_all = [K("hT", t) for t in range(2 * NTT)]
    widx = 0
    for hp in range(8):
        P.op("pool", lambda e: e.memset(accn, 0.0), writes=[K("accn")])
        P.op("pool", lambda e: e.memset(accd, 1.0 if ATT_DEBUG else 0.0), writes=[K("accd")])
        for g, (_, dil) in enumerate(PATTERNS):
            if ATT_DEBUG == 1:
                continue
            S = 128 * dil
            nS = TOK // S
            wa = widx % 2
            widx += 1
            P.dma("pool", lambda e, g=g, hp=hp, wa=wa: e.dma_start(
                out=wq[:, wa].rearrange("p j k n -> p j (k n)"), in_=d["wqkv"][g, hp].rearrange("j p n -> p j n")),
                "at_w%d" % wa, writes=[K("wq", wa)])
            r0 = 16 * g + 2 * hp
            src = bass.AP(Fd, r0 * 384, [[1, 128], [384, 2], [128, 2], [1, 128]])
            P.dma("sp", lambda e, src=src, wa=wa: e.dma_start(
                out=bTs[:, wa, :].rearrange("p (h t q) -> p h t q", h=2, t=2), in_=src),
                "at_b%d" % wa, reads=[K("Fd")], writes=[K("bTs", wa)])
            pt = c.bank(wa)
            P.op("dve", lambda e, wa=wa: e.tensor_copy(out=bTh[:, wa, :], in_=bTs[:, wa, :]),
                 reads=[K("bTs", wa)], writes=[K("bTh", wa)])
            P.op("dve", lambda e, wa=wa: e.tensor_tensor(out=bTs[:, wa, :], in0=bTs[:, wa, :], in1=bTh[:, wa, :],
                                                         op=ALU.subtract),
                 reads=[K("bTs", wa), K("bTh", wa)], writes=[K("bTs", wa)])
            P.op("dve", lambda e, wa=wa: e.tensor_copy(out=bTl[:, wa, :], in_=bTs[:, wa, :]),
                 reads=[K("bTs", wa)], writes=[K("bTl", wa)])
            P.op("pe", lambda e, pt=pt, wa=wa: e.matmul(pt, lhsT=aidb, rhs=bTh[:, wa, :], start=True, stop=False),
                 reads=[K("aidb"), K("bTh", wa)], writes=[("bank", wa)])
            P.op("pe", lambda e, pt=pt, wa=wa: e.matmul(pt, lhsT=aidb, rhs=bTl[:, wa, :], start=False, stop=True),
                 reads=[K("aidb"), K("bTl", wa)], writes=[("bank", wa)])
            P.op("act", lambda e, pt=pt, wa=wa: e.activation(out=bT[:, wa, :], in_=pt, func=AF.Copy),
                 reads=[("bank", wa)], writes=[K("bT", wa)])
            if ATT_DEBUG == 2:
                continue
            blocks = [(TOK + b * 512, 512, True) for b in range(4)]
            if S >= 512:
                blocks = [(TOK - S + b * 512, 512, False) for b in range(S // 512)] + blocks
            else:
                blocks = [(TOK - S, S, False)] + blocks
            bi_n = 0
            for (c0, n, own) in blocks:
                tiles = [K("hT", t) for t in range(c0 // 128, (c0 + n - 1) // 128 + 1)]
                for j in ((0, 1) if own else (1,)):
                    bi = 2 + bi_n % 2
                    bi_n += 1
                    pt = c.bank(bi)[:, 0:n]
                    for k in range(8):
                        P.op("pe", lambda e, pt=pt, wa=wa, j=j, k=k, c0=c0, n=n: e.matmul(
                            pt, lhsT=wq[:, wa, j, k, :], rhs=hT[:, k, c0:c0 + n], start=(k == 0), stop=(k == 7)),
                            reads=tiles + [K("wq", wa)], writes=[("bank", bi)])
                    if j == 0:
                        for hh in range(2):
                            P.op("act", lambda e, pt=pt, hh=hh, c0=c0, n=n: e.activation(
                                out=qTm[64 * hh:64 * hh + 64, hh, c0 - TOK:c0 - TOK + n], in_=pt[64 * hh:64 * hh + 64, :],
                                func=AF.Copy), reads=[("bank", bi)], writes=[K("qT")])
                    else:
                        P.op("act", lambda e, pt=pt, c0=c0, n=n: e.activation(out=kT[:, c0:c0 + n], in_=pt, func=AF.Copy),
                             reads=[("bank", bi)], writes=[K("kT")])
            if ATT_DEBUG == 3:
                continue
            units = [(n_, r_) for n_ in range(-1, nS) for r_ in range(dil)]
            for ui, (n_, r_) in enumerate(units):
                bi = 4 + (ui // 4) % 2
                pt = c.bank(bi)[:, (ui % 4) * 128:(ui % 4 + 1) * 128]
                t0 = TOK + n_ * S + r_
                tiles = [K("hT", t) for t in range(t0 // 128, (t0 + 127 * dil) // 128 + 1)]
                for k in range(8):
                    P.op("pe", lambda e, pt=pt, wa=wa, k=k, t0=t0, dil=dil: e.matmul(
                        pt, lhsT=hT[:, k, t0:t0 + 127 * dil + 1:dil], rhs=wq[:, wa, 2, k, :], start=(k == 0), stop=(k == 7)),
                        reads=tiles + [K("wq", wa)], writes=[("bank", bi)])
                if ui % 4 == 3 or ui == len(units) - 1:
                    for u2 in range(ui - ui % 4, ui + 1):
                        p2 = c.bank(bi)[:, (u2 % 4) * 128:(u2 % 4 + 1) * 128]
                        P.op("act", lambda e, p2=p2, u2=u2: e.activation(out=VE[:, u2, 0, 0:64], in_=p2[:, 0:64], func=AF.Copy),
                             reads=[("bank", bi)], writes=[K("VE", u2)])
                        P.op("act", lambda e, p2=p2, u2=u2: e.activation(out=VE[:, u2, 1, 64:128], in_=p2[:, 64:128], func=AF.Copy),
                             reads=[("bank", bi)], writes=[K("VE", u2)])
            if ATT_DEBUG == 4:
                continue
            for n_ in range(nS):
                for r_ in range(dil):
                    uc = (n_ + 1) * dil + r_
                    up = n_ * dil + r_
                    q0 = n_ * S + r_
                    kp0 = TOK + (n_ - 1) * S + r_
                    kc0 = TOK + n_ * S + r_
                    ua = (n_ * dil + r_) % 2
                    bs_ = 6
                    for hh in range(2):
                        ps = c.bank(bs_)[:, hh * 256:(hh + 1) * 256]
                        for tl, k0 in ((0, kc0), (1, kp0)):
                            P.op("pe", lambda e, ps=ps, hh=hh, tl=tl, k0=k0, q0=q0, dil=dil: e.matmul(
                                ps[:, tl * 128:(tl + 1) * 128],
                                lhsT=kT[:, k0:k0 + 127 * dil + 1:dil],
                                rhs=qTm[:, hh, q0:q0 + 127 * dil + 1:dil], start=True, stop=True),
                                reads=[K("kT"), K("qT")], writes=[("bank", bs_)])
                    for hh in range(2):
                        ps = c.bank(bs_)[:, hh * 256:(hh + 1) * 256]
                        P.op("dve", lambda e, ps=ps, hh=hh, wa=wa: e.scalar_tensor_tensor(
                            out=sc[:, hh, :], in0=ps, scalar=0.125, in1=bT[:, wa, hh * 256:(hh + 1) * 256],
                            op0=ALU.mult, op1=ALU.add),
                            reads=[("bank", bs_), K("bT", wa)], writes=[K("sc", hh)])
                        if n_ == 0:
                            P.op("act", lambda e, hh=hh, ua=ua: e.activation(out=pT[:, ua, hh, 128:256], in_=sc[:, hh, 128:256],
                                                                           func=AF.Exp, bias=hneg),
                                 reads=[K("sc", hh), K("sm")], writes=[K("pT", ua, hh)])
                            P.op("act", lambda e, hh=hh, ua=ua: e.activation(out=pT[:, ua, hh, 0:128],
                                                                           in_=sc[:, hh, 0:128], func=AF.Exp),
                                 reads=[K("sc", hh)], writes=[K("pT", ua, hh)])
                        else:
                            P.op("act", lambda e, hh=hh, ua=ua: e.activation(out=pT[:, ua, hh, :], in_=sc[:, hh, :],
                                                                           func=AF.Exp),
                                 reads=[K("sc", hh)], writes=[K("pT", ua, hh)])
                    if ATT_DEBUG == 5:
                        continue
                    pn, pd = c.bank(7)[:, 0:128], c.bank(7)[:, 128:256]
                    for (pt, lh) in ((pn, None), (pd, onesE)):
                        i4 = 0
                        for hh in range(2):
                            for tl, uu in ((0, uc), (1, up)):
                                lhs = VE[:, uu, hh, :] if lh is None else onesE[:, hh, :]
                                P.op("pe", lambda e, pt=pt, lhs=lhs, ua=ua, hh=hh, tl=tl, i4=i4: e.matmul(
                                    pt, lhsT=lhs, rhs=pT[:, ua, hh, tl * 128:(tl + 1) * 128], start=(i4 == 0), stop=(i4 == 3)),
                                    reads=[K("pT", ua, hh), K("VE", uu), K("onesE")], writes=[("bank", 7)])
                                i4 += 1
                    sl = slice(q0, q0 + 127 * dil + 1, dil)
                    P.op("dve", lambda e, sl=sl, pn=pn: e.tensor_tensor(out=accn[:, sl], in0=accn[:, sl], in1=pn, op=ALU.add),
                         reads=[("bank", 7), K("accn")], writes=[K("accn")])
                    P.op("dve", lambda e, sl=sl, pd=pd: e.tensor_tensor(out=accd[:, sl], in0=accd[:, sl], in1=pd, op=ALU.add),
                         reads=[("bank", 7), K("accd")], writes=[K("accd")])
        P.op("dve", lambda e: e.reciprocal(out=accd, in_=accd), reads=[K("accd")], writes=[K("accd")])
        P.op("dve", lambda e, hp=hp: e.tensor_tensor(out=OT[:, hp, :], in0=accn, in1=accd, op=ALU.mult),
             reads=[K("accn"), K("accd")], writes=[K("OT", hp)])
    P.barrier()
    c.release(m2)

    w_out = c.alloc(8 * 1024, BF16).rearrange("p (k n) -> p k n", k=8)
    Hh = c.alloc(D, F32)
    xtc = c.alloc(2 * D, F32).rearrange("p (a d) -> p a d", a=2)
    P.dma("pool", lambda e: e.dma_start(out=w_out.rearrange("p k n -> p (k n)"), in_=d["w_out"]), "at_wo",
          writes=[K("w_out")])
    for ti in range(NTT):
        a = ti % 2
        P.dma("sp", lambda e, ti=ti, a=a: e.dma_start(out=xtc[:, a, :], in_=xv[ti]), "at_x%d" % a, writes=[K("xt2", a)])
        for hf in range(2):
            pt = c.bank(hf)
            for hp in range(8):
                P.op("pe", lambda e, pt=pt, hp=hp, hf=hf, ti=ti: e.matmul(
                    pt, lhsT=OT[:, hp, ti * 128:(ti + 1) * 128], rhs=w_out[:, hp, hf * 512:(hf + 1) * 512],
                    start=(hp == 0), stop=(hp == 7)),
                    reads=[K("OT", hp), K("w_out")], writes=[("bank", hf)])
            P.op("act", lambda e, pt=pt, hf=hf: e.activation(out=Hh[:, hf * 512:(hf + 1) * 512], in_=pt, func=AF.Copy),
                 reads=[("bank", hf)], writes=[K("Hh", hf)])
        post_norm_add(c, Hh, [K("Hh", 0), K("Hh", 1)], xtc[:, a, :], K("xt2", a), gpost, K("gpost"), junk,
                      stt[:, 2:3], stt[:, 3:4], K, 1.0)
        P.dma("sp", lambda e, ti=ti, a=a: e.dma_start(out=yv[ti], in_=xtc[:, a, :]), "at_y%d" % a, reads=[K("xt2", a)])
    P.barrier()
    c.release(m)


def build_attn_prog():
    st = contextlib.ExitStack()
    c = Ctx(st)
    x_d = c.dram_in("x", [TOK, D])
    ident_d = c.dram_in("ident", [128, 128])
    d = dict(wqkv=c.dram_in("wqkv", [3, 8, 3, 128, 1024]), w_out=c.dram_in("w_out", [128, 8 * 1024]),
             gpost=c.dram_in("gpost", [D]), small=c.dram_in("small", [128, 9]), xh=c.dram_in("xh", [TOK, D]),
             rel_bias=c.dram_in("rel_bias", [32, 48]), onehot=c.dram_in("onehot", [32, 3, 129]),
             antiid=c.dram_in("antiid", [128, 128]))
    y_d = c.dram_out("y", [TOK, D])
    setup_common(c, ident_d)
    attn_phase(c, x_d, y_d, d, "at")
    c.P.barrier()
    c.P.emit(st)
    return c, st


def attn_inputs(I, i):
    j = i // 4
    w = I["at_w_qkv"][j].reshape(8, 128, 3, 3, 8, 128)
    wqkv = np.ascontiguousarray(w.transpose(2, 4, 3, 1, 0, 5).reshape(3, 8, 3, 128, 1024))
    small = np.zeros((128, 9), np.float32)
    small[:, 0:8] = col128(I["norm_pre"][i, 1], 8)
    return dict(
        wqkv=wqkv,
        w_out=np.ascontiguousarray(I["at_w_out"][j].reshape(8, 128, 1024).transpose(1, 0, 2).reshape(128, 8 * 1024)),
        gpost=np.ascontiguousarray(I["norm_post"][i, 1]), small=small,
        rel_bias=np.ascontiguousarray(I["rel_bias"]), antiid=np.ascontiguousarray(np.eye(128, dtype=np.float32)[::-1]),
        onehot=np.ascontiguousarray(t5_onehot().transpose(1, 0, 2)))


def attn_percore(small, xfull, i):
    sm = small.copy()
    sm[:, 8] = NEG if i == 0 else 0.0
    xh = np.zeros((TOK, D), np.float32) if i == 0 else np.ascontiguousarray(xfull[(i - 1) * TOK:i * TOK])
    return dict(x=np.ascontiguousarray(xfull[i * TOK:(i + 1) * TOK]), xh=xh, small=sm)


I32 = mybir.dt.int32
NLIST = list(range(-7, 9))
NCH = SEQ // 8
PI = float(np.pi)


def s5a_phase(c, x_d, u_d, d, tag):
    P = c.P
    m = c.mark()
    K = lambda *a: (tag,) + a
    w_in = c.alloc(8 * 1024, BF16).rearrange("p (k n) -> p k n", k=8)
    gpre = c.alloc(8, F32)
    xt = c.alloc(2 * D, F32).rearrange("p (a d) -> p a d", a=2)
    xn = c.alloc(2 * D, BF16).rearrange("p (a d) -> p a d", a=2)
    hT = c.alloc(2 * D, BF16).rearrange("p (a k t) -> p a k t", a=2, k=8)
    ut = c.alloc(2 * D, F32).rearrange("p (a d) -> p a d", a=2)
    junk = c.alloc(D, BF16)
    stt = c.alloc(8, F32)
    P.dma("pool", lambda e: e.dma_start(out=w_in.rearrange("p k n -> p (k n)"), in_=d["w_in"]), "s5a_w", writes=[K("w_in")])
    P.dma("sp", lambda e: e.dma_start(out=gpre, in_=d["gpre"]), "const_" + tag, writes=[K("gpre")], group=True)
    xv = x_d.rearrange("(t p) d -> t p d", p=128)
    uv = u_d.rearrange("(t p) d -> t p d", p=128)
    for tt in range(NTT):
        a = tt % 2
        P.dma("sp", lambda e, tt=tt, a=a: e.dma_start(out=xt[:, a, :], in_=xv[tt]), "s5a_x%d" % a, writes=[K("xt", a)])
        norm_T(c, K, xt[:, a, :], K("xt", a), xn[:, a, :], K("xn", a), hT[:, a], K("hT", a), gpre, K("gpre"), a, junk,
               stt[:, 0:1], stt[:, 1:2])
        for hf in range(2):
            bi = 2 + hf
            pt = c.bank(bi)
            for k in range(8):
                P.op("pe", lambda e, pt=pt, k=k, a=a, hf=hf: e.matmul(
                    pt, lhsT=hT[:, a, k, :], rhs=w_in[:, k, hf * 512:(hf + 1) * 512], start=(k == 0), stop=(k == 7)),
                    reads=[K("hT", a), K("w_in")], writes=[("bank", bi)])
            P.op("act", lambda e, pt=pt, a=a, hf=hf: e.activation(out=ut[:, a, hf * 512:(hf + 1) * 512], in_=pt, func=AF.Copy),
                 reads=[("bank", bi)], writes=[K("ut", a, hf)])
        P.dma("sp", lambda e, tt=tt, a=a: e.dma_start(out=uv[tt], in_=ut[:, a, :]), "s5a_u%d" % a,
              reads=[K("ut", a, 0), K("ut", a, 1)])
    P.barrier()
    c.release(m)


def s5b_phase(c, U_d, Y_d, d, tag, nch):
    P = c.P
    m = c.mark()
    K = lambda *a: (tag,) + a
    nst = int(np.log2(nch))
    assert 1 << nst == nch
    NT = (nch + 511) // 512
    W = min(nch, 512)
    f = lambda n: c.alloc(n, F32)
    par = f(24).rearrange("p (a g) -> p a g", a=3)
    sgn = f(2)
    bA1, bA2, cP1, cP2 = [f(128).rearrange("p (g h) -> p g h", g=8) for _ in range(4)]
    idf, jsf, mkf = f(128), f(128), f(128)
    dt, x1, th, den, nr, fre, fim, t8a, t8b = [f(8) for _ in range(9)]
    EX, ANG, MAG, SN, CS, LR, LI, sLR = [f(128).rearrange("p (n g) -> p n g", n=16) for _ in range(8)]
    a1, a2, a3 = f(128), f(128), f(128)
    ai = c.alloc(128, I32)
    FR, FI, sFI = [f(64).rearrange("p (n g) -> p n g", n=8) for _ in range(3)]
    PR, PIm, sPI = [f(8 * (nst + 1)).rearrange("p (n g) -> p n g", g=8) for _ in range(3)]
    q1, q2 = f(8), f(8)
    SETS = []
    for _s in range(2):
        SETS.append(dict(
            Bm=f(128), WT=f(128), W2=f(128), t16=f(2 * 16).rearrange("p (a h) -> p a h", a=2),
            Bb=c.alloc(128, BF16), WTb=c.alloc(128, BF16), W2b=c.alloc(128, BF16), BTb=c.alloc(128, BF16),
            KTb=c.alloc(128, BF16), MTf=f(128),
            MTb=c.alloc((nst + 1) * 128, BF16).rearrange("p (k n) -> p k n", n=128),
            U=f(nch), Uh=c.alloc(nch, BF16), Ul=c.alloc(nch, BF16), X=f(nch),
            Xh=c.alloc(nch + 16, BF16), Xl=c.alloc(nch + 16, BF16), tmp=f(nch),
            yo=f(2 * 128).rearrange("p (a n) -> p a n", a=2)))

    for dst, src, key in ((par, d["par"], "par"), (sgn, d["sgn"], "sgn"), (bA1, d["bA1"], "bA1"), (bA2, d["bA2"], "bA2"),
                          (cP1, d["cP1"], "cP1"), (cP2, d["cP2"], "cP2"), (idf, d["ident"], "idf2"),
                          (jsf, d["jshift"], "jsf"), (mkf, d["maskK"], "mkf")):
        P.dma("sp", lambda e, dst=dst, src=src: e.dma_start(out=dst, in_=src), "const_" + tag, writes=[K(key)], group=True)

    def ew(eng, fn, reads, writes):
        P.op(eng, fn, reads=[K(r) for r in reads], writes=[K(w) for w in writes])

    ar, aim, ldt = par[:, 0, :], par[:, 1, :], par[:, 2, :]
    sg1, sA = sgn[:, 0:1], sgn[:, 1:2]
    ew("act", lambda e: e.activation(out=dt, in_=ldt, func=AF.Exp), ["par"], ["dt"])
    ew("dve", lambda e: e.tensor_tensor(out=x1, in0=dt, in1=ar, op=ALU.mult), ["dt", "par"], ["x1"])
    ew("dve", lambda e: e.tensor_tensor(out=th, in0=dt, in1=aim, op=ALU.mult), ["dt", "par"], ["th"])
    for idx, n in enumerate(NLIST):
        ew("dve", lambda e, idx=idx, n=n: e.tensor_scalar(out=EX[:, idx, :], in0=x1, scalar1=float(n), scalar2=None, op0=ALU.mult),
           ["x1"], ["EX"])
        ew("dve", lambda e, idx=idx, n=n: e.tensor_scalar(out=ANG[:, idx, :], in0=th, scalar1=float(n), scalar2=None, op0=ALU.mult),
           ["th"], ["ANG"])
    fl = lambda t: t.rearrange("p n g -> p (n g)")
    ew("act", lambda e: e.activation(out=fl(MAG), in_=fl(EX), func=AF.Exp), ["EX"], ["MAG"])

    def sin_of(dst, shift, key):
        ew("dve", lambda e: e.tensor_scalar(out=a1, in0=fl(ANG), scalar1=shift, scalar2=1.0 / (2 * PI), op0=ALU.add, op1=ALU.mult),
           ["ANG"], ["a1"])
        ew("dve", lambda e: e.tensor_copy(out=ai, in_=a1), ["a1"], ["ai"])
        ew("dve", lambda e: e.tensor_copy(out=a2, in_=ai), ["ai"], ["a2"])
        ew("dve", lambda e: e.tensor_scalar(out=a1, in0=fl(ANG), scalar1=shift, scalar2=None, op0=ALU.add), ["ANG", "a1"], ["a1"])
        ew("dve", lambda e: e.scalar_tensor_tensor(out=a1, in0=a2, scalar=-2 * PI, in1=a1, op0=ALU.mult, op1=ALU.add),
           ["a2", "a1"], ["a1"])
        for thr, op, add in ((PI, ALU.is_gt, -2 * PI), (-PI, ALU.is_lt, 2 * PI)):
            ew("dve", lambda e, thr=thr, op=op: e.tensor_scalar(out=a3, in0=a1, scalar1=thr, scalar2=None, op0=op), ["a1"], ["a3"])
            ew("dve", lambda e, add=add: e.scalar_tensor_tensor(out=a1, in0=a3, scalar=add, in1=a1, op0=ALU.mult, op1=ALU.add),
               ["a3", "a1"], ["a1"])
        ew("act", lambda e: e.activation(out=fl(dst), in_=a1, func=AF.Sin), ["a1"], [key])

    sin_of(SN, 0.0, "SN")
    sin_of(CS, PI / 2, "CS")
    ew("dve", lambda e: e.tensor_tensor(out=fl(LR), in0=fl(MAG), in1=fl(CS), op=ALU.mult), ["MAG", "CS"], ["LR"])
    ew("dve", lambda e: e.tensor_tensor(out=fl(LI), in0=fl(MAG), in1=fl(SN), op=ALU.mult), ["MAG", "SN"], ["LI"])
    ew("dve", lambda e: e.tensor_scalar(out=fl(sLR), in0=fl(LR), scalar1=sA, scalar2=None, op0=ALU.mult), ["LR", "sgn"], ["sLR"])
    i1 = NLIST.index(1)
    ew("dve", lambda e: e.tensor_tensor(out=den, in0=ar, in1=ar, op=ALU.mult), ["par"], ["den"])
    ew("dve", lambda e: e.tensor_tensor(out=t8a, in0=aim, in1=aim, op=ALU.mult), ["par"], ["t8a"])
    ew("dve", lambda e: e.tensor_tensor(out=den, in0=den, in1=t8a, op=ALU.add), ["den", "t8a"], ["den"])
    ew("dve", lambda e: e.reciprocal(out=den, in_=den), ["den"], ["den"])
    ew("dve", lambda e: e.tensor_scalar(out=nr, in0=LR[:, i1, :], scalar1=-1.0, scalar2=None, op0=ALU.add), ["LR"], ["nr"])
    ew("dve", lambda e: e.tensor_tensor(out=t8a, in0=nr, in1=ar, op=ALU.mult), ["nr", "par", "t8a"], ["t8a"])
    ew("dve", lambda e: e.tensor_tensor(out=t8b, in0=LI[:, i1, :], in1=aim, op=ALU.mult), ["LI", "par"], ["t8b"])
    ew("dve", lambda e: e.tensor_tensor(out=fre, in0=t8a, in1=t8b, op=ALU.add), ["t8a", "t8b"], ["fre"])
    ew("dve", lambda e: e.tensor_tensor(out=fre, in0=fre, in1=den, op=ALU.mult), ["fre", "den"], ["fre"])
    ew("dve", lambda e: e.tensor_tensor(out=t8a, in0=LI[:, i1, :], in1=ar, op=ALU.mult), ["LI", "par", "fre"], ["t8a"])
    ew("dve", lambda e: e.tensor_tensor(out=t8b, in0=nr, in1=aim, op=ALU.mult), ["nr", "par", "fre"], ["t8b"])
    ew("dve", lambda e: e.tensor_tensor(out=fim, in0=t8a, in1=t8b, op=ALU.subtract), ["t8a", "t8b"], ["fim"])
    ew("dve", lambda e: e.tensor_tensor(out=fim, in0=fim, in1=den, op=ALU.mult), ["fim", "den"], ["fim"])
    for n in range(8):
        idx = NLIST.index(n)
        ew("dve", lambda e, idx=idx: e.tensor_tensor(out=t8a, in0=LI[:, idx, :], in1=fim, op=ALU.mult), ["LI", "fim", "FR", "FI"], ["t8a"])
        ew("dve", lambda e, idx=idx: e.tensor_tensor(out=t8b, in0=LR[:, idx, :], in1=fre, op=ALU.mult), ["LR", "fre", "FR", "FI"], ["t8b"])
        ew("dve", lambda e, n=n: e.tensor_tensor(out=FR[:, n, :], in0=t8b, in1=t8a, op=ALU.subtract), ["t8a", "t8b"], ["FR"])
        ew("dve", lambda e, idx=idx: e.tensor_tensor(out=t8a, in0=LR[:, idx, :], in1=fim, op=ALU.mult), ["LR", "fim", "FR"], ["t8a"])
        ew("dve", lambda e, idx=idx: e.tensor_tensor(out=t8b, in0=LI[:, idx, :], in1=fre, op=ALU.mult), ["LI", "fre", "FR"], ["t8b"])
        ew("dve", lambda e, n=n: e.tensor_tensor(out=FI[:, n, :], in0=t8a, in1=t8b, op=ALU.add), ["t8a", "t8b"], ["FI"])
    fl8 = lambda t: t.rearrange("p n g -> p (n g)")
    ew("dve", lambda e: e.tensor_scalar(out=fl8(sFI), in0=fl8(FI), scalar1=sg1, scalar2=None, op0=ALU.mult), ["FI", "sgn"], ["sFI"])
    i8 = NLIST.index(8)
    ew("dve", lambda e: e.tensor_copy(out=PR[:, 0, :], in_=LR[:, i8, :]), ["LR"], ["PR"])
    ew("dve", lambda e: e.tensor_copy(out=PIm[:, 0, :], in_=LI[:, i8, :]), ["LI"], ["PI"])
    for k in range(nst):
        ew("dve", lambda e, k=k: e.tensor_tensor(out=q1, in0=PR[:, k, :], in1=PR[:, k, :], op=ALU.mult), ["PR", "PI"], ["q1"])
        ew("dve", lambda e, k=k: e.tensor_tensor(out=q2, in0=PIm[:, k, :], in1=PIm[:, k, :], op=ALU.mult), ["PR", "PI"], ["q2"])
        ew("dve", lambda e, k=k: e.tensor_tensor(out=PR[:, k + 1, :], in0=q1, in1=q2, op=ALU.subtract), ["q1", "q2"], ["PR"])
        ew("dve", lambda e, k=k: e.tensor_tensor(out=q1, in0=PR[:, k, :], in1=PIm[:, k, :], op=ALU.mult), ["PR", "PI", "q1"], ["q1"])
        ew("dve", lambda e, k=k: e.tensor_scalar(out=PIm[:, k + 1, :], in0=q1, scalar1=2.0, scalar2=None, op0=ALU.mult), ["q1"], ["PI"])
    ew("dve", lambda e: e.tensor_scalar(out=sPI.rearrange("p n g -> p (n g)"), in0=PIm.rearrange("p n g -> p (n g)"),
                                        scalar1=sA, scalar2=None, op0=ALU.mult), ["PI", "sgn"], ["sPI"])
    for _s in range(2):
        P.op("pool", lambda e, _s=_s: e.memset(SETS[_s]["Xh"][:, 0:16], 0.0), writes=[K("Xh0")])
        P.op("pool", lambda e, _s=_s: e.memset(SETS[_s]["Xl"][:, 0:16], 0.0), writes=[K("Xl0")])

    def emit_group(gl):
        S_ = SETS[gl % 2]
        sx = gl % 2
        Bm, WT, W2, t16, Bb, WTb, W2b, BTb, KTb, MTf, MTb = [S_[n] for n in ("Bm", "WT", "W2", "t16", "Bb", "WTb", "W2b", "BTb", "KTb", "MTf", "MTb")]
        U, Uh, Ul, X, Xh, Xl, tmp, yo = [S_[n] for n in ("U", "Uh", "Ul", "X", "Xh", "Xl", "tmp", "yo")]
        B3 = Bm.rearrange("p (i h) -> p i h", i=8)
        W3 = WT.rearrange("p (i h) -> p i h", i=8)
        W23 = W2.rearrange("p (i h) -> p i h", i=8)
        SK = ("Bm", "WT", "W2", "Bb", "WTb", "W2b", "BTb", "KTb", "MTf", "MTb", "U", "Uh", "Ul", "X", "Xh", "Xl", "tmp", "t16", "yo")

        def kk(r):
            if isinstance(r, tuple):
                return (r[0] + str(sx),) + r[1:] if r[0] in SK else r
            return (r + str(sx)) if r in SK else r

        def ew(eng, fn, reads, writes):
            P.op(eng, fn, reads=[K(kk(r)) for r in reads], writes=[K(kk(w)) for w in writes])

        KS = lambda *a: K(kk(a[0] if len(a) == 1 else tuple(a)))
        for i in range(8):
            n = 7 - i
            ta = i % 2
            ew("dve", lambda e, gl=gl, n=n, ta=ta: e.tensor_scalar(out=t16[:, ta, :], in0=bA2[:, gl, :], scalar1=sFI[:, n, gl:gl + 1],
                                                                scalar2=None, op0=ALU.mult), ["bA2", "sFI"], [("t16", ta)])
            ew("dve", lambda e, gl=gl, n=n, i=i, ta=ta: e.scalar_tensor_tensor(out=B3[:, i, :], in0=bA1[:, gl, :], scalar=FR[:, n, gl:gl + 1],
                                                                       in1=t16[:, ta, :], op0=ALU.mult, op1=ALU.add),
               ["bA1", "FR", ("t16", ta)], ["Bm"])
        for j in range(8):
            for (dst3, idx, key) in ((W3, NLIST.index(j + 1), "WT"), (W23, NLIST.index(j - 7), "W2")):
                ew("dve", lambda e, gl=gl, idx=idx: e.tensor_scalar(out=t16[:, 0, :], in0=cP2[:, gl, :], scalar1=LI[:, idx, gl:gl + 1],
                                                                  scalar2=None, op0=ALU.mult), ["cP2", "LI"], [("t16", 0)])
                ew("dve", lambda e, gl=gl, idx=idx, dst3=dst3, j=j: e.scalar_tensor_tensor(
                    out=dst3[:, j, :], in0=cP1[:, gl, :], scalar=sLR[:, idx, gl:gl + 1], in1=t16[:, 0, :], op0=ALU.mult, op1=ALU.subtract),
                    ["cP1", "sLR", ("t16", 0)], [key])
        ew("act", lambda e: e.activation(out=Bb, in_=Bm, func=AF.Copy), ["Bm"], ["Bb"])
        ew("act", lambda e: e.activation(out=WTb, in_=WT, func=AF.Copy), ["WT"], ["WTb"])
        ew("act", lambda e: e.activation(out=W2b, in_=W2, func=AF.Copy), ["W2"], ["W2b"])
        pb = c.bank(sx, BF16)[:, 0:128]
        P.op("pe", lambda e: e.transpose(out=pb, in_=Bb, identity=c.idb), reads=[KS("Bb"), "idb"], writes=[("bank", sx)])
        P.op("act", lambda e: e.activation(out=BTb, in_=pb, func=AF.Copy), reads=[("bank", sx)], writes=[KS("BTb")])
        pk = c.bank(sx)[:, 128:256]
        P.op("pe", lambda e: e.matmul(pk, lhsT=Bb, rhs=W2b, start=True, stop=True), reads=[KS("Bb"), KS("W2b")], writes=[("bank", sx)])
        P.op("dve", lambda e: e.tensor_tensor(out=KTb, in0=pk, in1=mkf, op=ALU.mult), reads=[("bank", sx), K("mkf")], writes=[KS("KTb")])
        for k in range(nst + 1):
            ew("dve", lambda e, k=k, gl=gl: e.tensor_scalar(out=MTf, in0=jsf, scalar1=sPI[:, k, gl:gl + 1], scalar2=None, op0=ALU.mult),
               ["jsf", "sPI"], ["MTf"])
            ew("dve", lambda e, k=k, gl=gl: e.scalar_tensor_tensor(out=MTb[:, k, :], in0=idf, scalar=PR[:, k, gl:gl + 1], in1=MTf,
                                                               op0=ALU.mult, op1=ALU.add), ["idf2", "PR", "MTf"], [("MTb", k)])

        def split(src, hi, lo, skey, hkey, lkey):
            ew("act", lambda e: e.activation(out=hi, in_=src, func=AF.Copy), [skey], [hkey])
            ew("dve", lambda e: e.tensor_tensor(out=tmp, in0=src, in1=hi, op=ALU.subtract), [skey, hkey], ["tmp"])
            ew("pool", lambda e: e.tensor_copy(out=lo, in_=tmp), ["tmp"], [lkey])

        c.dbg = dict(dt=dt, x1=x1, th=th, LR=LR, LI=LI, MAG=MAG, SN=SN, CS=CS, FR=FR, FI=FI, fre=fre, fim=fim, PR=PR, PIm=PIm,
                     Bm=Bm, WT=WT, W2=W2, X=X, U=U, ANG=ANG, a1=a1, a2=a2)
        P.dma("sp", lambda e, gl=gl: e.dma_start(out=U, in_=U_d[gl]), "s5b_u%d" % sx, writes=[KS("U")])
        split(U, Uh, Ul, "U", "Uh", "Ul")
        for nt in range(NT):
            bi = 2 + 2 * sx + nt % 2
            pt = c.bank(bi)[:, 0:W]
            for q, src in enumerate((Uh, Ul)):
                P.op("pe", lambda e, pt=pt, src=src, nt=nt, q=q: e.matmul(pt, lhsT=BTb, rhs=src[:, nt * W:(nt + 1) * W],
                                                                       start=(q == 0), stop=(q == 1)),
                     reads=[KS("BTb"), KS("Uh"), KS("Ul")], writes=[("bank", bi)])
            P.op("act", lambda e, pt=pt, nt=nt: e.activation(out=X[:, nt * W:(nt + 1) * W], in_=pt, func=AF.Copy),
                 reads=[("bank", bi)], writes=[KS("X")])
        for k in range(nst):
            s = 1 << k
            split(X, Xh[:, 16:16 + nch], Xl[:, 16:16 + nch], "X", "Xh", "Xl")
            n_in = nch - s
            for nt in range((n_in + W - 1) // W):
                c0 = nt * W
                w = min(W, n_in - c0)
                bi = 2 + 2 * sx + nt % 2
                pt = c.bank(bi)[:, 0:w]
                for q, src in enumerate((Xh, Xl)):
                    P.op("pe", lambda e, pt=pt, src=src, c0=c0, w=w, k=k, q=q: e.matmul(
                        pt, lhsT=MTb[:, k, :], rhs=src[:, 16 + c0:16 + c0 + w], start=(q == 0), stop=(q == 1)),
                        reads=[KS("MTb", k), KS("Xh"), KS("Xl")], writes=[("bank", bi)])
                P.op("dve", lambda e, pt=pt, c0=c0, w=w, s=s: e.tensor_tensor(
                    out=X[:, s + c0:s + c0 + w], in0=X[:, s + c0:s + c0 + w], in1=pt, op=ALU.add),
                    reads=[("bank", bi), KS("X")], writes=[KS("X")])
        split(X, Xh[:, 16:16 + nch], Xl[:, 16:16 + nch], "X", "Xh", "Xl")
        for ct in range(nch // 128):
            bi = 6 + sx
            pt = c.bank(bi)[:, 0:128]
            ops = [(Xh[:, 15 + ct * 128:15 + (ct + 1) * 128], WTb, "WTb"), (Xl[:, 15 + ct * 128:15 + (ct + 1) * 128], WTb, "WTb"),
                   (Uh[:, ct * 128:(ct + 1) * 128], KTb, "KTb"), (Ul[:, ct * 128:(ct + 1) * 128], KTb, "KTb")]
            for q, (lh, rh, rkey) in enumerate(ops):
                P.op("pe", lambda e, pt=pt, lh=lh, rh=rh, q=q: e.matmul(pt, lhsT=lh, rhs=rh, start=(q == 0), stop=(q == 3)),
                     reads=[KS("Xh"), KS("Xl"), KS("Uh"), KS("Ul"), KS(rkey), K("Xh0"), K("Xl0")], writes=[("bank", bi)])
            ya = ct % 2
            P.op("act", lambda e, pt=pt, ya=ya: e.activation(out=yo[:, ya, :], in_=pt, func=AF.Copy),
                 reads=[("bank", bi)], writes=[KS("yo", ya)])
            P.dma("sp", lambda e, gl=gl, ct=ct, ya=ya: e.dma_start(out=Y_d[gl, ct * 128:(ct + 1) * 128, :], in_=yo[:, ya, :]),
                  "s5b_y%d_%d" % (sx, ya), reads=[KS("yo", ya)])

    for gl in range(8):
        emit_group(gl)
    P.barrier()
    c.release(m)


def s5c_phase(c, x_d, y_d, d, tag):
    P = c.P
    m = c.mark()
    K = lambda *a: (tag,) + a
    w_glu = c.alloc(8 * 1024, BF16).rearrange("p (k n) -> p k n", k=8)
    w_out = c.alloc(8 * 1024, BF16).rearrange("p (k n) -> p k n", k=8)
    gpost, dv = c.alloc(D, F32), c.alloc(D, F32)
    bg = c.alloc(8, F32)
    xt = c.alloc(2 * D, F32).rearrange("p (a d) -> p a d", a=2)
    yt = c.alloc(2 * D, F32).rearrange("p (a d) -> p a d", a=2)
    ut = c.alloc(2 * D, F32).rearrange("p (a d) -> p a d", a=2)
    yb = c.alloc(D, BF16)
    yT = c.alloc(D, BF16).rearrange("p (k t) -> p k t", k=8)
    zT = c.alloc(D, BF16).rearrange("p (k t) -> p k t", k=8)
    sgm = c.alloc(2 * 128, F32).rearrange("p (a t) -> p a t", a=2)
    Hh = c.alloc(D, F32)
    junk = c.alloc(D, BF16)
    stt = c.alloc(8, F32)
    P.dma("pool", lambda e: e.dma_start(out=w_glu.rearrange("p k n -> p (k n)"), in_=d["w_glu"]), "s5c_w", writes=[K("w_glu")], group=True)
    P.dma("pool", lambda e: e.dma_start(out=w_out.rearrange("p k n -> p (k n)"), in_=d["w_out"]), "s5c_w", writes=[K("w_out")], group=True)
    for dst, src, key in ((gpost, d["gpost"].partition_broadcast(128), "gpost"), (dv, d["dvec"].partition_broadcast(128), "dv"),
                          (bg, d["b_glu"], "bg")):
        P.dma("sp", lambda e, dst=dst, src=src: e.dma_start(out=dst, in_=src), "const_" + tag, writes=[K(key)], group=True)
    xv = x_d.rearrange("(t p) d -> t p d", p=128)
    yv = y_d.rearrange("(t p) d -> t p d", p=128)
    ysv = d["ys"].rearrange("(t p) d -> t p d", p=128)
    uv = d["u"].rearrange("(t p) d -> t p d", p=128)
    for tt in range(NTT):
        a = tt % 2
        P.dma("sp", lambda e, tt=tt, a=a: e.dma_start(out=xt[:, a, :], in_=xv[tt]), "s5c_x%d" % a, writes=[K("xt", a)])
        P.dma("sp", lambda e, tt=tt, a=a: e.dma_start(out=yt[:, a, :], in_=ysv[tt]), "s5c_ys%d" % a, writes=[K("yt", a)])
        P.dma("sp", lambda e, tt=tt, a=a: e.dma_start(out=ut[:, a, :], in_=uv[tt]), "s5c_u%d" % a, writes=[K("ut", a)])
        P.op("dve", lambda e, a=a: e.tensor_tensor(out=ut[:, a, :], in0=ut[:, a, :], in1=dv, op=ALU.mult),
             reads=[K("ut", a), K("dv")], writes=[K("ut", a)])
        P.op("dve", lambda e, a=a: e.tensor_tensor(out=yt[:, a, :], in0=yt[:, a, :], in1=ut[:, a, :], op=ALU.add),
             reads=[K("ut", a), K("yt", a)], writes=[K("yt", a)])
        P.op("act", lambda e, a=a: e.activation(out=yb, in_=yt[:, a, :], func=AF.Gelu_apprx_tanh),
             reads=[K("yt", a)], writes=[K("yb")])
        pb = c.bank(a, BF16)
        for k in range(8):
            P.op("pe", lambda e, k=k, pb=pb: e.transpose(out=pb[:, k * 128:(k + 1) * 128], in_=yb[:, k * 128:(k + 1) * 128],
                                                       identity=c.idb), reads=[K("yb"), "idb"], writes=[("bank", a)])
        P.op("act", lambda e, pb=pb: e.activation(out=yT.rearrange("p k t -> p (k t)"), in_=pb, func=AF.Copy),
             reads=[("bank", a)], writes=[K("yT")])
        for g4 in range(2):
            bi = 2 + g4
            for fc in range(4 * g4, 4 * g4 + 4):
                pt = c.bank(bi)[:, (fc % 4) * 128:(fc % 4 + 1) * 128]
                for k in range(8):
                    P.op("pe", lambda e, pt=pt, k=k, fc=fc: e.matmul(pt, lhsT=w_glu[:, k, fc * 128:(fc + 1) * 128], rhs=yT[:, k, :],
                                                                 start=(k == 0), stop=(k == 7)),
                         reads=[K("w_glu"), K("yT")], writes=[("bank", bi)])
            for fc in range(4 * g4, 4 * g4 + 4):
                pt = c.bank(bi)[:, (fc % 4) * 128:(fc % 4 + 1) * 128]
                sa = fc % 2
                P.op("act", lambda e, pt=pt, fc=fc, sa=sa: e.activation(out=sgm[:, sa, :], in_=pt, func=AF.Sigmoid, bias=bg[:, fc:fc + 1]),
                     reads=[("bank", bi), K("bg")], writes=[K("sgm", sa)])
                P.op("dve", lambda e, fc=fc, sa=sa: e.tensor_tensor(out=zT[:, fc, :], in0=yT[:, fc, :], in1=sgm[:, sa, :], op=ALU.mult),
                     reads=[K("yT"), K("sgm", sa)], writes=[K("zT", fc)])
        for hf in range(2):
            bi = 4 + hf
            pt = c.bank(bi)
            for fc in range(8):
                P.op("pe", lambda e, pt=pt, fc=fc, hf=hf: e.matmul(pt, lhsT=zT[:, fc, :], rhs=w_out[:, fc, hf * 512:(hf + 1) * 512],
                                                               start=(fc == 0), stop=(fc == 7)),
                     reads=[K("zT", fc), K("w_out")], writes=[("bank", bi)])
            P.op("act", lambda e, pt=pt, hf=hf: e.activation(out=Hh[:, hf * 512:(hf + 1) * 512], in_=pt, func=AF.Copy),
                 reads=[("bank", bi)], writes=[K("Hh", hf)])
        post_norm_add(c, Hh, [K("Hh", 0), K("Hh", 1)], xt[:, a, :], K("xt", a), gpost, K("gpost"), junk, stt[:, 2:3], stt[:, 3:4], K, 1.0)
        P.dma("sp", lambda e, tt=tt, a=a: e.dma_start(out=yv[tt], in_=xt[:, a, :]), "s5c_o%d" % a, reads=[K("xt", a)])
    P.barrier()
    c.release(m)


def build_s5a_prog():
    st = contextlib.ExitStack()
    c = Ctx(st)
    x_d = c.dram_in("x", [TOK, D])
    ident_d = c.dram_in("ident", [128, 128])
    d = dict(w_in=c.dram_in("w_in", [128, 8 * 1024]), gpre=c.dram_in("gpre", [128, 8]))
    u_d = c.dram_out("u", [TOK, D])
    setup_common(c, ident_d)
    s5a_phase(c, x_d, u_d, d, "s5a")
    c.P.barrier()
    c.P.emit(st)
    return c, st


def build_s5b_prog(nch=NCH):
    st = contextlib.ExitStack()
    c = Ctx(st)
    ident_d = c.dram_in("ident", [128, 128])
    U_d = c.dram_in("U", [8, 128, nch])
    d = dict(par=c.dram_in("par", [128, 3, 8]), sgn=c.dram_in("sgn", [128, 2]), bA1=c.dram_in("bA1", [128, 8, 16]),
             bA2=c.dram_in("bA2", [128, 8, 16]), cP1=c.dram_in("cP1", [128, 8, 16]), cP2=c.dram_in("cP2", [128, 8, 16]),
             ident=ident_d, jshift=c.dram_in("jshift", [128, 128]), maskK=c.dram_in("maskK", [128, 128]))
    Y_d = c.dram_out("Y", [8, nch, 128])
    setup_common(c, ident_d)
    s5b_phase(c, U_d, Y_d, d, "s5b", nch)
    c.P.barrier()
    c.P.emit(st)
    return c, st


def build_s5c_prog():
    st = contextlib.ExitStack()
    c = Ctx(st)
    x_d = c.dram_in("x", [TOK, D])
    ident_d = c.dram_in("ident", [128, 128])
    d = dict(ys=c.dram_in("ys", [TOK, D]), u=c.dram_in("u", [TOK, D]), w_glu=c.dram_in("w_glu", [128, 8 * 1024]),
             w_out=c.dram_in("w_out", [128, 8 * 1024]), gpost=c.dram_in("gpost", [D]), dvec=c.dram_in("dvec", [D]),
             b_glu=c.dram_in("b_glu", [128, 8]))
    y_d = c.dram_out("y", [TOK, D])
    setup_common(c, ident_d)
    s5c_phase(c, x_d, y_d, d, "s5c")
    c.P.barrier()
    c.P.emit(st)
    return c, st


def wlay8(w):
    n = w.shape[1]
    return np.ascontiguousarray(w.reshape(8, 128, n).transpose(1, 0, 2).reshape(128, 8 * n))


def s5b_inputs(I, core):
    gs = slice(8 * core, 8 * core + 8)
    two = lambda a: np.concatenate([a, a], axis=0)
    ar = two(I["s5_a_re"][0][gs].T)
    ai = two(I["s5_a_im"][0][gs].T)
    ldt = np.broadcast_to(I["s5_log_dt"][0][gs][None, :], (128, 8))
    par = np.ascontiguousarray(np.stack([ar, ai, ldt], axis=1).astype(np.float32))
    sgn = np.ones((128, 2), np.float32)
    sgn[:64, 0] = -1.0
    sgn[64:, 1] = -1.0
    br = I["s5_b_re"][0][gs].transpose(1, 0, 2)
    bi = I["s5_b_im"][0][gs].transpose(1, 0, 2)
    cr = I["s5_c_re"][0][gs].transpose(2, 0, 1)
    ci = I["s5_c_im"][0][gs].transpose(2, 0, 1)
    js = np.zeros((128, 128), np.float32)
    js[np.arange(64), np.arange(64) + 64] = 1.0
    js[np.arange(64) + 64, np.arange(64)] = 1.0
    ii = np.arange(128) // 16
    mk = (ii[None, :] >= ii[:, None]).astype(np.float32)
    return dict(par=par, sgn=sgn, bA1=np.ascontiguousarray(np.concatenate([br, bi], 0)),
                bA2=np.ascontiguousarray(np.concatenate([bi, br], 0)),
                cP1=np.ascontiguousarray(np.concatenate([cr, ci], 0)), cP2=np.ascontiguousarray(np.concatenate([ci, cr], 0)),
                jshift=js, maskK=mk)


def run_s5_layer(I, i, xfull):
    j = i // 4
    shards = lambda a: [np.ascontiguousarray(a[k * TOK:(k + 1) * TOK]) for k in range(NCORES)]
    sha = dict(w_in=wlay8(I["s5_w_in"][j]), gpre=col128(I["norm_pre"][i, 1], 8))
    ra = run_prog("s5a", build_s5a_prog, sha, [dict(x=s) for s in shards(xfull)])
    u = np.concatenate([ra[k]["u"] for k in range(NCORES)], axis=0)
    ug = u.reshape(NCH, 8, 64, 16)
    Uall = np.ascontiguousarray(ug.transpose(2, 1, 3, 0).reshape(64, 128, NCH))
    rb = run_prog("s5b", build_s5b_prog, {}, [dict(s5b_inputs(I, k), U=np.ascontiguousarray(Uall[8 * k:8 * k + 8]))
                                              for k in range(NCORES)])
    Y = np.concatenate([rb[k]["Y"] for k in range(NCORES)], axis=0)
    ys = np.ascontiguousarray(Y.reshape(64, NCH, 8, 16).transpose(1, 2, 0, 3).reshape(SEQ, D))
    shc = dict(w_glu=wlay8(I["s5_w_glu"][j]), w_out=wlay8(I["s5_w_out"][j]), gpost=np.ascontiguousarray(I["norm_post"][i, 1]),
               dvec=np.ascontiguousarray(I["s5_d"][j]), b_glu=col128(I["s5_b_glu"][j], 8))
    ysh, ush, xsh = shards(ys), shards(u), shards(xfull)
    rc = run_prog("s5c", build_s5c_prog, shc, [dict(x=xsh[k], ys=ysh[k], u=ush[k]) for k in range(NCORES)])
    return np.concatenate([rc[k]["y"] for k in range(NCORES)], axis=0)


def ffn_params(I, i, which):
    slot = 0 if which == 0 else 2
    return (I["ffn_w1"][i, which], I["ffn_w3"][i, which], I["ffn_w2"][i, which], I["norm_pre"][i, slot], I["norm_post"][i, slot])


def run_ffn_chain(I, specs, xfull):
    xs = [xfull[k * TOK:(k + 1) * TOK] for k in range(NCORES)]
    ys = run_ffn(xs, [ffn_params(I, i, w) for (i, w) in specs])
    return np.concatenate(ys, axis=0)


def run_mixer_layer(I, i, xfull):
    kind = i % 4
    if kind == 0:
        return run_s5_layer(I, i, xfull)
    if kind == 1:
        sh = conv_inputs(I, i)
        res = run_prog("conv", build_conv_prog, sh, [conv_percore(sh["small"], xfull, k) for k in range(NCORES)])
    elif kind == 2:
        sh = gmlp_inputs(I, i)
        res = run_prog("gmlp", build_gmlp_prog, sh, [dict(x=np.ascontiguousarray(xfull[k * TOK:(k + 1) * TOK]))
                                                    for k in range(NCORES)])
    else:
        sh = attn_inputs(I, i)
        res = run_prog("attn", build_attn_prog, sh, [attn_percore(sh["small"], xfull, k) for k in range(NCORES)])
    return np.concatenate([res[k]["y"] for k in range(NCORES)], axis=0)


def kernel(**inputs):
    I = {k: np.asarray(v) for k, v in inputs.items()}
    x = np.ascontiguousarray(I["x"][0], dtype=np.float32)
    x = run_ffn_chain(I, [(0, 0)], x)
    for i in range(4):
        x = run_mixer_layer(I, i, x)
        x = run_ffn_chain(I, [(i, 1), (i + 1, 0)] if i < 3 else [(i, 1)], x)
    return np.ascontiguousarray(x[None].astype(np.float32))
```

```python
import contextlib
import numpy as np
import concourse.bass as bass
import concourse.mybir as mybir
from concourse.bass_utils import run_bass_kernel_spmd

F32 = mybir.dt.float32
BF16 = mybir.dt.bfloat16
AF = mybir.ActivationFunctionType
ALU = mybir.AluOpType
AX = mybir.AxisListType

NCORES = 8
D = 1024
FF = 2816
NFC = FF // 128
SEQ = 16384
TOK = SEQ // NCORES
NTT = TOK // 128
EPS = 1e-6

ENGS = ("pe", "act", "dve", "pool", "sp")
SEM_ROT = 30000
DMA_ROT = 2000


class Prog:
    def __init__(self, nc):
        self.nc = nc
        self.ops = {e: [] for e in ENGS}
        self.res = {}
        self.dma_sems = {}
        self.waited = {e: {} for e in ENGS}
        self.signaled = {e: set() for e in ENGS}
        self.grouped = set()

    def _deps(self, eng, reads, writes):
        deps = []
        for r in reads:
            ent = self.res.get(r)
            if ent and ent[0] is not None:
                deps.append(ent[0])
            if ent and isinstance(r, tuple) and r[0] == "bank":
                deps.extend((k, v) for k, v in ent[1].items() if k != eng)
        for w in writes:
            ent = self.res.get(w)
            if ent:
                if ent[0] is not None:
                    deps.append(ent[0])
                deps.extend(ent[1].items())
        out = {}
        for k, v in deps:
            if k == eng and eng == "pe":
                continue
            if v > out.get(k, -1):
                out[k] = v
        waits = []
        for k, v in out.items():
            if self.waited[eng].get(k, -1) >= v:
                continue
            self.waited[eng][k] = v
            waits.append((k, v))
            if k in ENGS:
                self.signaled[k].add(v)
        return waits

    def op(self, eng, fn, reads=(), writes=()):
        waits = self._deps(eng, reads, writes)
        seq = len(self.ops[eng])
        self.ops[eng].append(dict(fn=fn, waits=waits, dma=None))
        for r in reads:
            self.res.setdefault(r, [None, {}])[1][eng] = seq
        for w in writes:
            self.res[w] = [(eng, seq), {}]
        return seq

    def dma(self, eng, fn, sem, reads=(), writes=(), group=False):
        waits = self._deps(eng, reads, writes)
        self.dma_sems[sem] = self.dma_sems.get(sem, 0) + 1
        val = self.dma_sems[sem]
        if group:
            self.grouped.add(sem)
            assert all(k != "dma:" + sem for k, _ in waits), ("dep inside grouped dma batch", sem)
            val = 1 << 30
        self.ops[eng].append(dict(fn=fn, waits=waits, dma=sem))
        key = "dma:" + sem
        for r in reads:
            self.res.setdefault(r, [None, {}])[1][key] = val
        for w in writes:
            self.res[w] = [(key, val), {}]
        return val

    def barrier(self):
        last = {}
        for e in ENGS:
            for i in range(len(self.ops[e]) - 1, -1, -1):
                o = self.ops[e][i]
                if o["dma"] is None and o["fn"] is not None:
                    last[e] = i
                    break
        dl = {"dma:" + s: ((1 << 30) if s in self.grouped else c) for s, c in self.dma_sems.items()}
        for e in ENGS:
            waits = []
            for k, v in list(last.items()) + list(dl.items()):
                if k == e:
                    continue
                if self.waited[e].get(k, -1) >= v:
                    continue
                self.waited[e][k] = v
                waits.append((k, v))
                if k in ENGS:
                    self.signaled[k].add(v)
            if waits:
                self.ops[e].append(dict(fn=None, waits=waits, dma=None))

    def emit(self, st):
        nc = self.nc
        rank, esems = {}, {}
        for e in ENGS:
            sig = sorted(self.signaled[e])
            rank[e] = {s: i for i, s in enumerate(sig)}
            n = max(1, (len(sig) + SEM_ROT - 1) // SEM_ROT)
            esems[e] = [st.enter_context(nc.semaphore("s_%s_%d" % (e, i))) for i in range(n)]
        dsems = {}
        for s, c in self.dma_sems.items():
            n = max(1, (c + DMA_ROT - 1) // DMA_ROT)
            dsems[s] = [st.enter_context(nc.semaphore("d_%s_%d" % (s, i))) for i in range(n)]
        dcount = {s: 0 for s in self.dma_sems}

        def sem_for(k, v):
            if k in ENGS:
                r = rank[k][v]
                return esems[k][r // SEM_ROT], (r % SEM_ROT) + 1
            if k[4:] in self.grouped:
                assert self.dma_sems[k[4:]] <= DMA_ROT
                return dsems[k[4:]][0], self.dma_sems[k[4:]] * 16
            i = v - 1
            return dsems[k[4:]][i // DMA_ROT], ((i % DMA_ROT) + 1) * 16

        block = st.enter_context(nc.Block())
        engmap = dict(pe=block.tensor, act=block.scalar, dve=block.vector,
                      pool=block.gpsimd, sp=block.sync)

        def make(e):
            def body(eng):
                for seq, o in enumerate(self.ops[e]):
                    for k, v in o["waits"]:
                        sm, val = sem_for(k, v)
                        eng.wait_ge(sm, val)
                    if o["fn"] is None:
                        continue
                    ins = o["fn"](eng)
                    if o["dma"] is not None:
                        s = o["dma"]
                        i = dcount[s]
                        dcount[s] += 1
                        ins.then_inc(dsems[s][i // DMA_ROT], 16)
                    elif seq in rank[e]:
                        r = rank[e][seq]
                        ins.then_inc(esems[e][r // SEM_ROT], 1)
            return body

        for e in ENGS:
            if self.ops[e]:
                engmap[e](make(e))


class Ctx:
    ARENA = 100 * 1024

    def __init__(self, st):
        self.st = st
        self.nc = bass.Bass("TRN2", target_bir_lowering=False)
        self.P = Prog(self.nc)
        self.arena = st.enter_context(self.nc.sbuf_tensor("arena", [128, self.ARENA], BF16))
        self.top = 0
        self.banks = [st.enter_context(self.nc.psum_tensor("bank%d" % i, [128, 512], F32))
                      for i in range(8)]
        self.din = {}

    def dram_in(self, name, shape, dtype=F32):
        t = self.nc.dram_tensor(name, list(shape), dtype, kind="ExternalInput").ap()
        self.din[name] = t
        return t

    def dram_out(self, name, shape, dtype=F32):
        return self.nc.dram_tensor(name, list(shape), dtype, kind="ExternalOutput").ap()

    def mark(self):
        return self.top

    def release(self, m):
        self.top = m

    def alloc(self, n, dtype=BF16):
        four = dtype in (F32, mybir.dt.int32)
        units = n * (2 if four else 1)
        units = (units + 15) // 16 * 16
        a = self.arena[:, self.top:self.top + units]
        assert self.top + units <= self.ARENA, ("SBUF arena overflow", self.top + units)
        self.top += units
        if four:
            a = a.bitcast(dtype)
        return a[:, 0:n]

    def bank(self, i, dtype=F32):
        b = self.banks[i][:, :]
        return b if dtype == F32 else b.bitcast(dtype)


def load_consts(c, ident_d):
    P = c.P
    idf = c.alloc(128, F32)
    idb = c.alloc(128, BF16)
    P.dma("sp", lambda e: e.dma_start(out=idf, in_=ident_d), "const0", writes=["idf"], group=True)
    P.op("dve", lambda e: e.tensor_copy(out=idb, in_=idf), reads=["idf"], writes=["idb"])
    c.idf, c.idb = idf, idb


def load_x(c, X, x_d):
    P = c.P
    xv = x_d.rearrange("(t p) d -> p t d", p=128)
    Xv = X.rearrange("p (t d) -> p t d", t=NTT)
    for h in range(4):
        P.dma("sp", lambda e, h=h: e.dma_start(out=Xv[:, 4 * h:4 * h + 4, :], in_=xv[:, 4 * h:4 * h + 4, :]),
              "xin", writes=[("X", t) for t in range(4 * h, 4 * h + 4)], group=True)


def store_x(c, X, y_d):
    P = c.P
    yv = y_d.rearrange("(t p) d -> p t d", p=128)
    Xv = X.rearrange("p (t d) -> p t d", t=NTT)
    for h in range(4):
        P.dma("sp", lambda e, h=h: e.dma_start(out=yv[:, 4 * h:4 * h + 4, :], in_=Xv[:, 4 * h:4 * h + 4, :]),
              "xout", reads=[("X", t) for t in range(4 * h, 4 * h + 4)], group=True)


def rms_rstd(c, src, ss, rstd, key_in, key_ss, junk, n=D):
    P = c.P
    P.op("act", lambda e: e.activation(out=junk, in_=src, func=AF.Square, accum_out=ss),
         reads=[key_in], writes=[key_ss, "junk"])
    P.op("dve", lambda e: e.tensor_scalar(out=ss, in0=ss, scalar1=1.0 / n, scalar2=EPS,
                                          op0=ALU.mult, op1=ALU.add),
         reads=[key_ss], writes=[key_ss])
    P.op("pool", lambda e: e.tensor_tensor(out=rstd, in0=ss, in1=c.mhalf, op=ALU.pow),
         reads=[key_ss, "mhalf"], writes=[key_ss + "r"])


def ffn_phase(c, X, w13r, w2r, gpre_d, gpost_d, tag):
    P = c.P
    m = c.mark()
    Xv = X.rearrange("p (t d) -> p t d", t=NTT)
    gbc = c.alloc(2 * D, F32).rearrange("p (a d) -> p a d", a=2)
    xn = c.alloc(2 * D, BF16).rearrange("p (a d) -> p a d", a=2)
    regA = c.alloc(16384, BF16)
    hT = regA[:, 0:8192].rearrange("p (k t) -> p k t", k=8)
    w13 = regA[:, 8192:16384].rearrange("p (s w f) -> p s w f", s=4, w=2)
    H = regA.bitcast(F32).rearrange("p (t d) -> p t d", t=8)
    gT = c.alloc(NFC * 1024, BF16).rearrange("p (f t) -> p f t", f=NFC)
    w2q = c.alloc(2 * NFC * 256, BF16).rearrange("p (a f j) -> p a f j", a=2, f=NFC)
    sil = c.alloc(2 * 512, F32).rearrange("p (a t) -> p a t", a=2)
    junk = c.alloc(D, BF16)
    stt = c.alloc(64, F32)
    K = lambda *a: (tag,) + a

    P.dma("sp", lambda e: e.dma_start(out=gbc[:, 0, :], in_=gpre_d.partition_broadcast(128)),
          "const_" + tag, writes=[K("gbc", 0)], group=True)
    P.dma("sp", lambda e: e.dma_start(out=gbc[:, 1, :], in_=gpost_d.partition_broadcast(128)),
          "const_" + tag, writes=[K("gbc", 1)], group=True)

    w13_n = [0]

    def load_w13(fc):
        s = fc % 4
        P.dma("pool", lambda e: e.dma_start(out=w13[:, s], in_=w13r[fc]), "w13_%d" % s,
              writes=[K("w13", s, 0), K("w13", s, 1), K("H", 4 + s)])
        return s

    w2_n = [0]

    def load_w2(dq):
        s = w2_n[0] % 2
        w2_n[0] += 1
        P.dma("pool", lambda e: e.dma_start(out=w2q[:, s].rearrange("p f j -> p (f j)"), in_=w2r[dq]),
              "w2q_%d" % s, writes=[K("w2q", s)])
        return s

    for b in range(2):
        for fc in range(3):
            load_w13(fc)
        for tt in range(8):
            gt = b * 8 + tt
            a = tt % 2
            ss = stt[:, 2 * tt:2 * tt + 1]
            rstd = stt[:, 2 * tt + 1:2 * tt + 2]
            P.op("act", lambda e, gt=gt, ss=ss: e.activation(out=junk, in_=Xv[:, gt, :], func=AF.Square,
                                                           accum_out=ss),
                 reads=[("X", gt)], writes=[K("ss", tt)])
            P.op("dve", lambda e, ss=ss: e.tensor_scalar(out=ss, in0=ss, scalar1=1.0 / D, scalar2=EPS,
                                                       op0=ALU.mult, op1=ALU.add),
                 reads=[K("ss", tt)], writes=[K("ss", tt)])
            P.op("pool", lambda e, ss=ss, rstd=rstd: e.tensor_tensor(out=rstd, in0=ss, in1=c.mhalf, op=ALU.pow),
                 reads=[K("ss", tt), "mhalf"], writes=[K("rstd", tt)])
            P.op("dve", lambda e, gt=gt, a=a, rstd=rstd: e.scalar_tensor_tensor(
                out=xn[:, a, :], in0=Xv[:, gt, :], scalar=rstd, in1=gbc[:, 0, :], op0=ALU.mult, op1=ALU.mult),
                reads=[("X", gt), K("rstd", tt), K("gbc", 0)], writes=[K("xn", a)])
            pb = c.bank(a, BF16)
            for k in range(8):
                P.op("pe", lambda e, k=k, a=a, pb=pb: e.transpose(out=pb[:, k * 128:(k + 1) * 128],
                                                               in_=xn[:, a, k * 128:(k + 1) * 128],
                                                               identity=c.idb),
                     reads=[K("xn", a), "idb"], writes=[("bank", a)])
            P.op("act", lambda e, tt=tt, pb=pb: e.activation(
                out=hT[:, :, tt * 128:(tt + 1) * 128], in_=pb.rearrange("p (k t) -> p k t", k=8), func=AF.Copy),
                reads=[("bank", a)], writes=[K("hT", tt)] + [K("H", i) for i in range(4)])
        s2 = load_w2(0)
        for fc in range(NFC):
            if fc + 3 < NFC:
                load_w13(fc + 3)
            s = fc % 4
            for sb in range(2):
                bi = 2 + 2 * ((fc * 2 + sb) % 2)
                pa, pbk = c.bank(bi), c.bank(bi + 1)
                for w, pt in ((0, pa), (1, pbk)):
                    for k in range(8):
                        P.op("pe", lambda e, w=w, pt=pt, k=k, s=s, sb=sb: e.matmul(
                            pt, lhsT=w13[:, s, w, k * 128:(k + 1) * 128], rhs=hT[:, k, sb * 512:(sb + 1) * 512],
                            start=(k == 0), stop=(k == 7)),
                            reads=[K("w13", s, w)] + [K("hT", 4 * sb + i) for i in range(4)],
                            writes=[("bank", bi + w)])
                sa = (fc * 2 + sb) % 2
                P.op("act", lambda e, pa=pa, sa=sa: e.activation(out=sil[:, sa, :], in_=pa, func=AF.Silu),
                     reads=[("bank", bi)], writes=[K("sil", sa)])
                P.op("dve", lambda e, pbk=pbk, sa=sa, fc=fc, sb=sb: e.tensor_tensor(
                    out=gT[:, fc, sb * 512:(sb + 1) * 512], in0=sil[:, sa, :], in1=pbk, op=ALU.mult),
                    reads=[K("sil", sa), ("bank", bi + 1)], writes=[K("gT", fc)])
        for dq in range(4):
            s = s2
            if dq < 3:
                s2 = load_w2(dq + 1)
            for tt in range(8):
                bi = 6 + (tt % 2)
                pt = c.bank(bi)[:, 0:256]
                for fc in range(NFC):
                    P.op("pe", lambda e, pt=pt, fc=fc, tt=tt, s=s: e.matmul(
                        pt, lhsT=gT[:, fc, tt * 128:(tt + 1) * 128], rhs=w2q[:, s, fc, :],
                        start=(fc == 0), stop=(fc == NFC - 1)),
                        reads=[K("gT", fc), K("w2q", s)], writes=[("bank", bi)])
                P.op("act", lambda e, pt=pt, tt=tt, dq=dq: e.activation(
                    out=H[:, tt, dq * 256:(dq + 1) * 256], in_=pt, func=AF.Copy),
                    reads=[("bank", bi)],
                    writes=[K("H", tt)] + ([K("hT", i) for i in range(8)] if tt < 4 else
                                           [K("w13", tt - 4, 0), K("w13", tt - 4, 1)]))
        for tt in range(8):
            gt = b * 8 + tt
            ss = stt[:, 16 + 2 * tt:16 + 2 * tt + 1]
            rstd = stt[:, 17 + 2 * tt:17 + 2 * tt + 1]
            P.op("act", lambda e, tt=tt, ss=ss: e.activation(out=junk, in_=H[:, tt, :], func=AF.Square,
                                                           accum_out=ss),
                 reads=[K("H", tt)], writes=[K("ss2", tt)])
            P.op("dve", lambda e, ss=ss: e.tensor_scalar(out=ss, in0=ss, scalar1=1.0 / D, scalar2=EPS,
                                                       op0=ALU.mult, op1=ALU.add),
                 reads=[K("ss2", tt)], writes=[K("ss2", tt)])
            P.op("pool", lambda e, ss=ss, rstd=rstd: e.tensor_tensor(out=rstd, in0=ss, in1=c.mhalf, op=ALU.pow),
                 reads=[K("ss2", tt), "mhalf"], writes=[K("rstd2", tt)])
            P.op("pool", lambda e, rstd=rstd: e.tensor_scalar(out=rstd, in0=rstd, scalar1=0.5, scalar2=None,
                                                            op0=ALU.mult),
                 reads=[K("rstd2", tt)], writes=[K("rstd2", tt)])
            P.op("dve", lambda e, tt=tt, rstd=rstd: e.scalar_tensor_tensor(
                out=H[:, tt, :], in0=H[:, tt, :], scalar=rstd, in1=gbc[:, 1, :], op0=ALU.mult, op1=ALU.mult),
                reads=[K("H", tt), K("rstd2", tt), K("gbc", 1)], writes=[K("H", tt)])
            P.op("pool", lambda e, tt=tt, gt=gt: e.tensor_tensor(out=Xv[:, gt, :], in0=Xv[:, gt, :], in1=H[:, tt, :],
                                                                op=ALU.add),
                 reads=[K("H", tt), ("X", gt)], writes=[("X", gt)])
    P.barrier()
    c.release(m)


def setup_common(c, ident_d):
    P = c.P
    load_consts(c, ident_d)
    c.mhalf = c.alloc(1, F32)
    P.op("pool", lambda e: e.memset(c.mhalf, -0.5), writes=["mhalf"])


def build_ffn_prog(nph=1):
    st = contextlib.ExitStack()
    c = Ctx(st)
    x_d = c.dram_in("x", [TOK, D])
    ident_d = c.dram_in("ident", [128, 128])
    ws = []
    for q in range(nph):
        ws.append((c.dram_in("w13r%d" % q, [NFC, 128, 2, 1024]), c.dram_in("w2r%d" % q, [4, 128, NFC * 256]),
                   c.dram_in("gpre%d" % q, [D]), c.dram_in("gpost%d" % q, [D])))
    y_d = c.dram_out("y", [TOK, D])
    setup_common(c, ident_d)
    X = c.alloc(NTT * D, F32)
    load_x(c, X, x_d)
    for q in range(nph):
        ffn_phase(c, X, ws[q][0], ws[q][1], ws[q][2], ws[q][3], "f%d" % q)
    store_x(c, X, y_d)
    c.P.barrier()
    c.P.emit(st)
    return c, st


def lay_w13(w):
    return np.ascontiguousarray(w.reshape(8, 128, NFC, 128).transpose(2, 1, 0, 3).reshape(NFC, 128, 1024))


def lay_w2(w):
    return np.ascontiguousarray(w.reshape(NFC, 128, 4, 256).transpose(2, 1, 0, 3).reshape(4, 128, NFC * 256))


_PROGS = {}


def run_ffn(xs, plist):
    nph = len(plist)
    name = "ffn%d" % nph
    if name not in _PROGS:
        _PROGS[name] = build_ffn_prog(nph)
    c, st = _PROGS[name]
    sh = dict(ident=np.eye(128, dtype=np.float32))
    for q, (w1, w3, w2, gpre, gpost) in enumerate(plist):
        sh["w13r%d" % q] = np.ascontiguousarray(np.stack([lay_w13(w1), lay_w13(w3)], axis=2))
        sh["w2r%d" % q] = lay_w2(w2)
        sh["gpre%d" % q] = np.ascontiguousarray(gpre)
        sh["gpost%d" % q] = np.ascontiguousarray(gpost)
    in_maps = [dict(sh, x=np.ascontiguousarray(xs[i])) for i in range(NCORES)]
    res = run_bass_kernel_spmd(c.nc, in_maps, core_ids=list(range(NCORES)))
    return [res.results[i]["y"] for i in range(NCORES)]


GELU_NATIVE = True


def gelu_act(c, out, in_, bias, reads, writes, tmp=None):
    P = c.P
    P.op("act", lambda e: e.activation(out=out, in_=in_, func=AF.Gelu_apprx_tanh, bias=bias),
         reads=reads, writes=writes)


def gmlp_phase(c, x_d, y_d, d, tag):
    P = c.P
    m = c.mark()
    K = lambda *a: (tag,) + a
    E = 2048
    w_in = c.alloc(8 * 4096, BF16).rearrange("p (k n) -> p k n", k=8)
    w_out = c.alloc(16 * 1024, BF16).rearrange("p (k n) -> p k n", k=16)
    gpost = c.alloc(D, F32)
    gpre = c.alloc(8, F32)
    binu = c.alloc(16, F32)
    lng = c.alloc(16, F32)
    lnb = c.alloc(16, F32)
    xt = c.alloc(2 * D, F32).rearrange("p (a d) -> p a d", a=2)
    xn = c.alloc(2 * D, BF16).rearrange("p (a d) -> p a d", a=2)
    hT = c.alloc(2 * D, BF16).rearrange("p (a k t) -> p a k t", a=2, k=8)
    v = c.alloc(E, F32)
    vhat = c.alloc(E, BF16)
    uT = c.alloc(E, BF16).rearrange("p (k t) -> p k t", k=16)
    gated = c.alloc(E, BF16).rearrange("p (k t) -> p k t", k=16)
    tmpc = c.alloc(E, F32).rearrange("p (k t) -> p k t", k=16)
    rs = c.alloc(1024, F32)
    bs = c.alloc(1024, F32)
    wsf = c.alloc(1024, F32)
    mk = c.alloc(1024, F32)
    wsm = c.alloc(1024, BF16).rearrange("p (h t) -> p h t", h=8)
    Hh = c.alloc(D, F32)
    stmp = c.alloc(2 * 128, F32).rearrange("p (a t) -> p a t", a=2)
    junk = c.alloc(E, BF16)
    brow = c.alloc(E + D + 128, BF16)
    browf = c.alloc(E + D, F32)
    onesb = c.alloc(128, BF16)
    stt = c.alloc(16, F32)

    for k in range(8):
        P.dma("pool", lambda e, k=k: e.dma_start(out=w_in[:, k, :], in_=d["w_in"][:, k, :]), "gm_w",
              writes=[K("w_in", k)], group=True)
    P.dma("pool", lambda e: e.dma_start(out=w_out.rearrange("p k n -> p (k n)"), in_=d["w_out"]), "gm_w",
          writes=[K("w_out")], group=True)
    for dst, src, key in ((gpost, d["gpost"].partition_broadcast(128), "gpost"), (gpre, d["gpre"], "gpre"),
                          (binu, d["b_in_u"], "binu"), (lng, d["ln_g"], "lng"), (lnb, d["ln_b"], "lnb"),
                          (bs, d["b_s"].partition_broadcast(128), "bs"), (wsf, d["wsT"], "wsf"),
                          (mk, d["mask"], "mk"), (browf[0:1, 0:E], d["b_in_v"], "browf"),
                          (browf[0:1, E:E + D], d["b_out"], "browf2")):
        P.dma("sp", lambda e, dst=dst, src=src: e.dma_start(out=dst, in_=src), "const_" + tag, writes=[K(key)],
              group=True)
    P.op("dve", lambda e: e.tensor_copy(out=brow[0:1, 0:E + D], in_=browf[0:1, :]),
         reads=[K("browf"), K("browf2")], writes=[K("brow")])
    P.op("pool", lambda e: e.memset(brow[0:1, E + D:E + D + 128], 1.0), writes=[K("brow1")])
    P.op("pool", lambda e: e.memset(onesb, 1.0), writes=[K("onesb")])
    ones_row = brow[0:1, E + D:E + D + 128]
    P.op("dve", lambda e: e.tensor_tensor(out=wsm.rearrange("p h t -> p (h t)"), in0=wsf, in1=mk, op=ALU.mult),
         reads=[K("wsf"), K("mk")], writes=[K("wsm")])
    for hf in range(2):
        pt = c.bank(hf)
        P.op("pe", lambda e, pt=pt, hf=hf: e.matmul(pt, lhsT=onesb, rhs=wsm.rearrange("p h t -> p (h t)")[:, hf * 512:(hf + 1) * 512],
                                                   start=True, stop=True),
             reads=[K("onesb"), K("wsm")], writes=[("bank", hf)])
        P.op("act", lambda e, pt=pt, hf=hf: e.activation(out=rs[:, hf * 512:(hf + 1) * 512], in_=pt, func=AF.Copy),
             reads=[("bank", hf)], writes=[K("rs", hf)])
    for cc in range(16):
        hd = cc // 2
        P.op("dve", lambda e, cc=cc, hd=hd: e.scalar_tensor_tensor(
            out=tmpc[:, cc, :], in0=rs[:, hd * 128:(hd + 1) * 128], scalar=lnb[:, cc:cc + 1],
            in1=bs[:, hd * 128:(hd + 1) * 128], op0=ALU.mult, op1=ALU.add),
            reads=[K("rs", hd // 4), K("lnb"), K("bs")], writes=[K("tmpc", cc)])

    xv = x_d.rearrange("(t p) d -> t p d", p=128)
    yv = y_d.rearrange("(t p) d -> t p d", p=128)
    for tt in range(NTT):
        a = tt % 2
        P.dma("sp", lambda e, tt=tt, a=a: e.dma_start(out=xt[:, a, :], in_=xv[tt]), "gm_x%d" % a,
              writes=[K("xt", a)])
        ss, rstd = stt[:, 0:1], stt[:, 1:2]
        P.op("act", lambda e, a=a: e.activation(out=junk[:, 0:D], in_=xt[:, a, :], func=AF.Square, accum_out=ss),
             reads=[K("xt", a)], writes=[K("ss")])
        P.op("dve", lambda e: e.tensor_scalar(out=ss, in0=ss, scalar1=1.0 / D, scalar2=EPS, op0=ALU.mult, op1=ALU.add),
             reads=[K("ss")], writes=[K("ss")])
        P.op("pool", lambda e: e.tensor_tensor(out=rstd, in0=ss, in1=c.mhalf, op=ALU.pow),
             reads=[K("ss"), "mhalf"], writes=[K("rstd")])
        P.op("dve", lambda e, a=a: e.tensor_scalar(out=xn[:, a, :], in0=xt[:, a, :], scalar1=rstd, scalar2=None,
                                                   op0=ALU.mult),
             reads=[K("xt", a), K("rstd")], writes=[K("xn", a)])
        pb = c.bank(a, BF16)
        for k in range(8):
            P.op("pe", lambda e, k=k, a=a, pb=pb: e.transpose(out=pb[:, k * 128:(k + 1) * 128],
                                                           in_=xn[:, a, k * 128:(k + 1) * 128], identity=c.idb),
                 reads=[K("xn", a), "idb"], writes=[("bank", a)])
        for k in range(8):
            P.op("act", lambda e, k=k, a=a, pb=pb: e.activation(out=hT[:, a, k, :], in_=pb[:, k * 128:(k + 1) * 128],
                                                             func=AF.Copy, scale=gpre[:, k:k + 1]),
                 reads=[("bank", a), K("gpre")], writes=[K("hT", a)])
        for cb in range(4):
            bi = 2 + cb % 2
            pt = c.bank(bi)
            for k in range(8):
                P.op("pe", lambda e, k=k, a=a, pt=pt, cb=cb: e.matmul(
                    pt, lhsT=hT[:, a, k, :], rhs=w_in[:, k, E + cb * 512:E + (cb + 1) * 512], start=(k == 0), stop=False),
                    reads=[K("hT", a), K("w_in", k)], writes=[("bank", bi)])
            P.op("pe", lambda e, pt=pt, cb=cb: e.matmul(pt, lhsT=ones_row, rhs=brow[0:1, cb * 512:(cb + 1) * 512],
                                                       start=False, stop=True),
                 reads=[K("brow"), K("brow1")], writes=[("bank", bi)])
            P.op("act", lambda e, pt=pt, cb=cb: e.activation(out=v[:, cb * 512:(cb + 1) * 512], in_=pt,
                                                           func=AF.Gelu_apprx_tanh, accum_out=stt[:, 4 + cb:5 + cb]),
                 reads=[("bank", bi)], writes=[K("v", cb), K("s1", cb)])
        P.op("act", lambda e: e.activation(out=junk, in_=v, func=AF.Square, accum_out=stt[:, 8:9]),
             reads=[K("v", i) for i in range(4)], writes=[K("s2")])
        P.op("dve", lambda e: e.tensor_reduce(out=stt[:, 9:10], in_=stt[:, 4:8], axis=AX.X, op=ALU.add),
             reads=[K("s1", i) for i in range(4)], writes=[K("mean")])
        P.op("dve", lambda e: e.tensor_scalar(out=stt[:, 9:10], in0=stt[:, 9:10], scalar1=1.0 / E, scalar2=None,
                                              op0=ALU.mult), reads=[K("mean")], writes=[K("mean")])
        P.op("dve", lambda e: e.tensor_tensor(out=stt[:, 10:11], in0=stt[:, 9:10], in1=stt[:, 9:10], op=ALU.mult),
             reads=[K("mean")], writes=[K("msq")])
        P.op("dve", lambda e: e.scalar_tensor_tensor(out=stt[:, 11:12], in0=stt[:, 8:9], scalar=1.0 / E,
                                                     in1=stt[:, 10:11], op0=ALU.mult, op1=ALU.subtract),
             reads=[K("s2"), K("msq")], writes=[K("var")])
        P.op("dve", lambda e: e.tensor_scalar(out=stt[:, 11:12], in0=stt[:, 11:12], scalar1=EPS, scalar2=None,
                                              op0=ALU.add), reads=[K("var")], writes=[K("var")])
        P.op("pool", lambda e: e.tensor_tensor(out=stt[:, 12:13], in0=stt[:, 11:12], in1=c.mhalf, op=ALU.pow),
             reads=[K("var"), "mhalf"], writes=[K("lrstd")])
        P.op("dve", lambda e: e.tensor_scalar(out=vhat, in0=v, scalar1=stt[:, 9:10], scalar2=stt[:, 12:13],
                                              op0=ALU.subtract, op1=ALU.mult),
             reads=[K("v", i) for i in range(4)] + [K("mean"), K("lrstd")], writes=[K("vhat")])
        for g4 in range(4):
            bi = 4 + g4 % 2
            for cc in range(4 * g4, 4 * g4 + 4):
                pt = c.bank(bi)[:, (cc % 4) * 128:(cc % 4 + 1) * 128]
                for k in range(8):
                    P.op("pe", lambda e, k=k, a=a, pt=pt, cc=cc: e.matmul(
                        pt, lhsT=w_in[:, k, cc * 128:(cc + 1) * 128], rhs=hT[:, a, k, :], start=(k == 0), stop=(k == 7)),
                        reads=[K("hT", a), K("w_in", k)], writes=[("bank", bi)])
            for cc in range(4 * g4, 4 * g4 + 4):
                pt = c.bank(bi)[:, (cc % 4) * 128:(cc % 4 + 1) * 128]
                P.op("act", lambda e, pt=pt, cc=cc: e.activation(out=uT[:, cc, :], in_=pt, func=AF.Gelu_apprx_tanh,
                                                               bias=binu[:, cc:cc + 1]),
                     reads=[("bank", bi), K("binu")], writes=[K("uT", cc)])
        for g4 in range(4):
            bi = 6 + g4 % 2
            for cc in range(4 * g4, 4 * g4 + 4):
                pt = c.bank(bi)[:, (cc % 4) * 128:(cc % 4 + 1) * 128]
                hd = cc // 2
                P.op("pe", lambda e, pt=pt, cc=cc, hd=hd: e.matmul(pt, lhsT=vhat[:, cc * 128:(cc + 1) * 128],
                                                                  rhs=wsm[:, hd, :], start=True, stop=True),
                     reads=[K("vhat"), K("wsm")], writes=[("bank", bi)])
            for cc in range(4 * g4, 4 * g4 + 4):
                pt = c.bank(bi)[:, (cc % 4) * 128:(cc % 4 + 1) * 128]
                sa = cc % 2
                P.op("dve", lambda e, pt=pt, cc=cc, sa=sa: e.scalar_tensor_tensor(
                    out=stmp[:, sa, :], in0=pt, scalar=lng[:, cc:cc + 1], in1=tmpc[:, cc, :], op0=ALU.mult, op1=ALU.add),
                    reads=[("bank", bi), K("lng"), K("tmpc", cc)], writes=[K("stmp", sa)])
                P.op("dve", lambda e, cc=cc, sa=sa: e.tensor_tensor(out=gated[:, cc, :], in0=uT[:, cc, :],
                                                                    in1=stmp[:, sa, :], op=ALU.mult),
                     reads=[K("uT", cc), K("stmp", sa)], writes=[K("gated", cc)])
        for hf in range(2):
            bi = 2 + hf
            pt = c.bank(bi)
            for cc in range(16):
                P.op("pe", lambda e, pt=pt, cc=cc, hf=hf: e.matmul(
                    pt, lhsT=gated[:, cc, :], rhs=w_out[:, cc, hf * 512:(hf + 1) * 512], start=(cc == 0), stop=False),
                    reads=[K("gated", cc), K("w_out")], writes=[("bank", bi)])
            P.op("pe", lambda e, pt=pt, hf=hf: e.matmul(pt, lhsT=ones_row, rhs=brow[0:1, E + hf * 512:E + (hf + 1) * 512],
                                                       start=False, stop=True),
                 reads=[K("brow"), K("brow1")], writes=[("bank", bi)])
            P.op("act", lambda e, pt=pt, hf=hf: e.activation(out=Hh[:, hf * 512:(hf + 1) * 512], in_=pt, func=AF.Copy),
                 reads=[("bank", bi)], writes=[K("Hh", hf)])
        post_norm_add(c, Hh, [K("Hh", 0), K("Hh", 1)], xt[:, a, :], K("xt", a), gpost, K("gpost"), junk[:, 0:D],
                      stt[:, 13:14], stt[:, 14:15], K, 1.0)
        P.dma("sp", lambda e, tt=tt, a=a: e.dma_start(out=yv[tt], in_=xt[:, a, :]), "gm_y%d" % a,
              reads=[K("xt", a)])
    P.barrier()
    c.release(m)


def post_norm_add(c, Hh, hkeys, xtile, xkey, gpost, gkey, junk, ss, rstd, K, coef):
    P = c.P
    P.op("act", lambda e: e.activation(out=junk, in_=Hh, func=AF.Square, accum_out=ss),
         reads=hkeys, writes=[K("pss")])
    P.op("dve", lambda e: e.tensor_scalar(out=ss, in0=ss, scalar1=1.0 / D, scalar2=EPS, op0=ALU.mult, op1=ALU.add),
         reads=[K("pss")], writes=[K("pss")])
    P.op("pool", lambda e: e.tensor_tensor(out=rstd, in0=ss, in1=c.mhalf, op=ALU.pow),
         reads=[K("pss"), "mhalf"], writes=[K("prstd")])
    if coef != 1.0:
        P.op("pool", lambda e: e.tensor_scalar(out=rstd, in0=rstd, scalar1=coef, scalar2=None, op0=ALU.mult),
             reads=[K("prstd")], writes=[K("prstd")])
    P.op("dve", lambda e: e.scalar_tensor_tensor(out=Hh, in0=Hh, scalar=rstd, in1=gpost, op0=ALU.mult, op1=ALU.mult),
         reads=hkeys + [K("prstd"), gkey], writes=hkeys)
    P.op("pool", lambda e: e.tensor_tensor(out=xtile, in0=xtile, in1=Hh, op=ALU.add),
         reads=hkeys + [xkey], writes=[xkey])


def build_gmlp_prog():
    st = contextlib.ExitStack()
    c = Ctx(st)
    x_d = c.dram_in("x", [TOK, D])
    ident_d = c.dram_in("ident", [128, 128])
    d = dict(w_in=c.dram_in("w_in", [128, 8, 4096]), w_out=c.dram_in("w_out", [128, 16 * 1024]),
             gpost=c.dram_in("gpost", [D]), gpre=c.dram_in("gpre", [128, 8]), b_in_u=c.dram_in("b_in_u", [128, 16]),
             ln_g=c.dram_in("ln_g", [128, 16]), ln_b=c.dram_in("ln_b", [128, 16]), b_s=c.dram_in("b_s", [1024]),
             wsT=c.dram_in("wsT", [128, 1024]), mask=c.dram_in("mask", [128, 1024]),
             b_in_v=c.dram_in("b_in_v", [1, 2048]), b_out=c.dram_in("b_out", [1, D]))
    y_d = c.dram_out("y", [TOK, D])
    setup_common(c, ident_d)
    gmlp_phase(c, x_d, y_d, d, "gm")
    c.P.barrier()
    c.P.emit(st)
    return c, st


def col128(v, n):
    return np.ascontiguousarray(v.reshape(n, 128).T)


def gmlp_inputs(I, i):
    j = i // 4
    w_in = I["gm_w_in"][j]
    causal = np.tril(np.ones((128, 128), np.float32))
    mask = np.ascontiguousarray(np.tile(causal.T[:, None, :], (1, 8, 1)).reshape(128, 1024))
    return dict(
        w_in=np.ascontiguousarray(w_in.reshape(8, 128, 4096).transpose(1, 0, 2)),
        w_out=np.ascontiguousarray(I["gm_w_out"][j].reshape(16, 128, 1024).transpose(1, 0, 2).reshape(128, 16 * 1024)),
        gpost=np.ascontiguousarray(I["norm_post"][i, 1]), gpre=col128(I["norm_pre"][i, 1], 8),
        b_in_u=col128(I["gm_b_in"][j][:2048], 16), ln_g=col128(I["gm_ln_g"][j], 16), ln_b=col128(I["gm_ln_b"][j], 16),
        b_s=np.ascontiguousarray(I["gm_b_s"][j].reshape(1024)),
        wsT=np.ascontiguousarray(I["gm_w_s"][j].transpose(2, 0, 1).reshape(128, 1024)),
        mask=mask, b_in_v=np.ascontiguousarray(I["gm_b_in"][j][2048:].reshape(1, 2048)),
        b_out=np.ascontiguousarray(I["gm_b_out"][j].reshape(1, D)))


def run_prog(name, builder, shared, xs):
    if name not in _PROGS:
        _PROGS[name] = builder()
    c, st = _PROGS[name]
    ident = np.eye(128, dtype=np.float32)
    in_maps = []
    for i in range(NCORES):
        mp = dict(shared)
        mp["ident"] = ident
        mp.update(xs[i])
        in_maps.append(mp)
    res = run_bass_kernel_spmd(c.nc, in_maps, core_ids=list(range(NCORES)))
    return res.results


def norm_T(c, K, xt_ap, xkey, xn_ap, xnkey, dst, dkey, gpre, gkey, bi, junk, ss, rstd):
    P = c.P
    P.op("act", lambda e: e.activation(out=junk, in_=xt_ap, func=AF.Square, accum_out=ss),
         reads=[xkey], writes=[K("nss")])
    P.op("dve", lambda e: e.tensor_scalar(out=ss, in0=ss, scalar1=1.0 / D, scalar2=EPS, op0=ALU.mult, op1=ALU.add),
         reads=[K("nss")], writes=[K("nss")])
    P.op("pool", lambda e: e.tensor_tensor(out=rstd, in0=ss, in1=c.mhalf, op=ALU.pow),
         reads=[K("nss"), "mhalf"], writes=[K("nrstd")])
    P.op("dve", lambda e: e.tensor_scalar(out=xn_ap, in0=xt_ap, scalar1=rstd, scalar2=None, op0=ALU.mult),
         reads=[xkey, K("nrstd")], writes=[xnkey])
    pb = c.bank(bi, BF16)
    for k in range(8):
        P.op("pe", lambda e, k=k: e.transpose(out=pb[:, k * 128:(k + 1) * 128], in_=xn_ap[:, k * 128:(k + 1) * 128],
                                              identity=c.idb),
             reads=[xnkey, "idb"], writes=[("bank", bi)])
    for k in range(8):
        P.op("act", lambda e, k=k: e.activation(out=dst[:, k, :], in_=pb[:, k * 128:(k + 1) * 128], func=AF.Copy,
                                                scale=gpre[:, k:k + 1]),
             reads=[("bank", bi), gkey], writes=[dkey])


def conv_phase(c, x_d, y_d, d, tag):
    P = c.P
    m = c.mark()
    K = lambda *a: (tag,) + a
    NB = TOK // 256
    w_in = c.alloc(8 * 2048, BF16).rearrange("p (k n) -> p k n", k=8)
    w_out = c.alloc(8 * 1024, BF16).rearrange("p (k n) -> p k n", k=8)
    hT = c.alloc(8 * (128 + TOK), BF16).rearrange("p (k t) -> p k t", k=8)
    gpost = c.alloc(D, F32)
    sm = c.alloc(8 * 6 + 8 * 31 + 1, F32)
    gpre, bia, big, dwb, lng, lnb = [sm[:, 8 * i:8 * i + 8] for i in range(6)]
    dw = sm[:, 48:48 + 248].rearrange("p (k j) -> p k j", k=8)
    flag = sm[:, 296:297]
    xt = c.alloc(2 * D, F32).rearrange("p (a d) -> p a d", a=2)
    xn = c.alloc(2 * D, BF16).rearrange("p (a d) -> p a d", a=2)
    sg = c.alloc(2 * 288, F32).rearrange("p (a t) -> p a t", a=2)
    z = c.alloc(2 * 288, BF16).rearrange("p (a t) -> p a t", a=2)
    dg = c.alloc(8 * 31 * 128, BF16).rearrange("p (k j n) -> p k j n", k=8, j=31)
    y = c.alloc(8 * 256, F32).rearrange("p (k t) -> p k t", k=8)
    yb = c.alloc(8 * 256, BF16).rearrange("p (k t) -> p k t", k=8)
    ysq = c.alloc(8 * 256, BF16).rearrange("p (k t) -> p k t", k=8)
    mean = c.alloc(256, F32)
    var = c.alloc(256, F32)
    tt_ = c.alloc(2 * 256, F32).rearrange("p (a t) -> p a t", a=2)
    znT = c.alloc(8 * 256, BF16).rearrange("p (k t) -> p k t", k=8)
    Hh = c.alloc(D, F32)
    junk = c.alloc(D, BF16)
    brow = c.alloc(D + 128, BF16)
    onesb = c.alloc(128, BF16)
    mh256 = c.alloc(256, F32)
    stt = c.alloc(8, F32)

    for k in range(8):
        P.dma("pool", lambda e, k=k: e.dma_start(out=w_in[:, k, :], in_=d["w_in"][:, k, :]), "cv_w",
              writes=[K("w_in", k)], group=True)
    P.dma("pool", lambda e: e.dma_start(out=w_out.rearrange("p k n -> p (k n)"), in_=d["w_out"]), "cv_w",
          writes=[K("w_out")], group=True)
    for dst, src, key in ((gpost, d["gpost"].partition_broadcast(128), "gpost"), (sm[:, 0:297], d["small"], "sm"),
                          ):
        P.dma("sp", lambda e, dst=dst, src=src: e.dma_start(out=dst, in_=src), "const_" + tag, writes=[K(key)],
              group=True)
    P.dma("pool", lambda e: e.dma_start(out=brow[0:1, 0:D], in_=d["b_out"]), "cv_w", writes=[K("brow")], group=True)
    P.op("pool", lambda e: e.memset(brow[0:1, D:D + 128], 1.0), writes=[K("brow1")])
    P.op("pool", lambda e: e.memset(onesb, 1.0), writes=[K("onesb")])
    P.op("pool", lambda e: e.memset(mh256, -0.5), writes=[K("mh256")])
    ones_row = brow[0:1, D:D + 128]
    for cc in range(8):
        for j in range(31):
            P.op("pool", lambda e, cc=cc, j=j: e.tensor_scalar(out=dg[:, cc, j, :], in0=c.idb, scalar1=dw[:, cc, j:j + 1],
                                                               scalar2=None, op0=ALU.mult),
                 reads=["idb", K("sm")], writes=[K("dg", cc)])

    xv = x_d.rearrange("(t p) d -> t p d", p=128)
    yv = y_d.rearrange("(t p) d -> t p d", p=128)
    for ti in range(NTT + 1):
        a = ti % 2
        src = d["xh"] if ti == 0 else xv[ti - 1]
        P.dma("sp", lambda e, src=src, a=a: e.dma_start(out=xt[:, a, :], in_=src), "cv_x%d" % a, writes=[K("xt", a)])
        norm_T(c, K, xt[:, a, :], K("xt", a), xn[:, a, :], K("xn", a), hT[:, :, ti * 128:(ti + 1) * 128],
               K("hT", ti), gpre, K("sm"), a, junk, stt[:, 0:1], stt[:, 1:2])
    for blk in range(NB):
        c0 = 128 + blk * 256 - 32
        tiles = list(range(c0 // 128, (c0 + 287) // 128 + 1))
        hkeys = [K("hT", t) for t in tiles]
        for cc in range(8):
            sa = cc % 2
            ba, bg = 2 + 2 * sa, 3 + 2 * sa
            pa, pg = c.bank(ba)[:, 0:288], c.bank(bg)[:, 0:288]
            for (pt, bi, off) in ((pa, ba, 0), (pg, bg, 1024)):
                for k in range(8):
                    P.op("pe", lambda e, pt=pt, k=k, cc=cc, off=off, c0=c0: e.matmul(
                        pt, lhsT=w_in[:, k, off + cc * 128:off + (cc + 1) * 128], rhs=hT[:, k, c0:c0 + 288],
                        start=(k == 0), stop=(k == 7)),
                        reads=hkeys + [K("w_in", k)], writes=[("bank", bi)])
            P.op("act", lambda e, pg=pg, sa=sa, cc=cc: e.activation(out=sg[:, sa, :], in_=pg, func=AF.Sigmoid,
                                                                 bias=big[:, cc:cc + 1]),
                 reads=[("bank", bg), K("sm")], writes=[K("sg", sa)])
            P.op("dve", lambda e, pa=pa, sa=sa, cc=cc: e.scalar_tensor_tensor(
                out=z[:, sa, :], in0=pa, scalar=bia[:, cc:cc + 1], in1=sg[:, sa, :], op0=ALU.add, op1=ALU.mult),
                reads=[("bank", ba), K("sg", sa), K("sm")], writes=[K("z", sa)])
            if blk == 0:
                P.op("dve", lambda e, sa=sa: e.tensor_scalar(out=z[:, sa, 0:32], in0=z[:, sa, 0:32], scalar1=flag,
                                                             scalar2=None, op0=ALU.mult),
                     reads=[K("z", sa), K("sm")], writes=[K("z", sa)])
            bt = 6 + cc % 2
            py = c.bank(bt)[:, 0:256]
            for j in range(31):
                P.op("pe", lambda e, py=py, sa=sa, cc=cc, j=j: e.matmul(py, lhsT=dg[:, cc, j, :], rhs=z[:, sa, 2 + j:258 + j],
                                                                     start=(j == 0), stop=(j == 30)),
                     reads=[K("dg", cc), K("z", sa)], writes=[("bank", bt)])
            P.op("act", lambda e, py=py, cc=cc: e.activation(out=y[:, cc, :], in_=py, func=AF.Identity, bias=dwb[:, cc:cc + 1]),
                 reads=[("bank", bt), K("sm")], writes=[K("y", cc)])
            P.op("act", lambda e, cc=cc: e.activation(out=yb[:, cc, :], in_=y[:, cc, :], func=AF.Copy),
                 reads=[K("y", cc)], writes=[K("yb", cc)])
            P.op("act", lambda e, cc=cc: e.activation(out=ysq[:, cc, :], in_=y[:, cc, :], func=AF.Square),
                 reads=[K("y", cc)], writes=[K("ysq", cc)])
        pm, pq = c.bank(6)[:, 0:256], c.bank(7)[:, 0:256]
        for cc in range(8):
            P.op("pe", lambda e, cc=cc: e.matmul(pm, lhsT=onesb, rhs=yb[:, cc, :], start=(cc == 0), stop=(cc == 7)),
                 reads=[K("onesb"), K("yb", cc)], writes=[("bank", 6)])
        for cc in range(8):
            P.op("pe", lambda e, cc=cc: e.matmul(pq, lhsT=onesb, rhs=ysq[:, cc, :], start=(cc == 0), stop=(cc == 7)),
                 reads=[K("onesb"), K("ysq", cc)], writes=[("bank", 7)])
        P.op("dve", lambda e: e.tensor_scalar(out=mean, in0=pm, scalar1=1.0 / D, scalar2=None, op0=ALU.mult),
             reads=[("bank", 6)], writes=[K("mean")])
        P.op("dve", lambda e: e.tensor_tensor(out=var, in0=mean, in1=mean, op=ALU.mult),
             reads=[K("mean")], writes=[K("var")])
        P.op("dve", lambda e: e.scalar_tensor_tensor(out=var, in0=pq, scalar=1.0 / D, in1=var, op0=ALU.mult,
                                                     op1=ALU.subtract),
             reads=[("bank", 7), K("var")], writes=[K("var")])
        P.op("dve", lambda e: e.tensor_scalar(out=var, in0=var, scalar1=EPS, scalar2=None, op0=ALU.add),
             reads=[K("var")], writes=[K("var")])
        P.op("pool", lambda e: e.tensor_tensor(out=var, in0=var, in1=mh256, op=ALU.pow),
             reads=[K("var"), K("mh256")], writes=[K("var")])
        for cc in range(8):
            sa = cc % 2
            P.op("dve", lambda e, cc=cc, sa=sa: e.tensor_tensor(out=tt_[:, sa, :], in0=y[:, cc, :], in1=mean,
                                                                op=ALU.subtract),
                 reads=[K("y", cc), K("mean")], writes=[K("t", sa)])
            P.op("dve", lambda e, sa=sa: e.tensor_tensor(out=tt_[:, sa, :], in0=tt_[:, sa, :], in1=var, op=ALU.mult),
                 reads=[K("t", sa), K("var")], writes=[K("t", sa)])
            P.op("act", lambda e, cc=cc, sa=sa: e.activation(out=znT[:, cc, :], in_=tt_[:, sa, :], func=AF.Silu,
                                                           scale=lng[:, cc:cc + 1], bias=lnb[:, cc:cc + 1]),
                 reads=[K("t", sa), K("sm")], writes=[K("znT", cc)])
        for t2 in range(2):
            ti = blk * 2 + t2
            a = ti % 2
            P.dma("sp", lambda e, ti=ti, a=a: e.dma_start(out=xt[:, a, :], in_=xv[ti]), "cv_x%d" % a,
                  writes=[K("xt", a)])
            for hf in range(2):
                bi = hf
                pt = c.bank(bi)
                for cc in range(8):
                    P.op("pe", lambda e, pt=pt, cc=cc, hf=hf, t2=t2: e.matmul(
                        pt, lhsT=znT[:, cc, t2 * 128:(t2 + 1) * 128], rhs=w_out[:, cc, hf * 512:(hf + 1) * 512],
                        start=(cc == 0), stop=False),
                        reads=[K("znT", cc), K("w_out")], writes=[("bank", bi)])
                P.op("pe", lambda e, pt=pt, hf=hf: e.matmul(pt, lhsT=ones_row, rhs=brow[0:1, hf * 512:(hf + 1) * 512],
                                                           start=False, stop=True),
                     reads=[K("brow"), K("brow1")], writes=[("bank", bi)])
                P.op("act", lambda e, pt=pt, hf=hf: e.activation(out=Hh[:, hf * 512:(hf + 1) * 512], in_=pt, func=AF.Copy),
                     reads=[("bank", bi)], writes=[K("Hh", hf)])
            post_norm_add(c, Hh, [K("Hh", 0), K("Hh", 1)], xt[:, a, :], K("xt", a), gpost, K("gpost"), junk,
                          stt[:, 2:3], stt[:, 3:4], K, 1.0)
            P.dma("sp", lambda e, ti=ti, a=a: e.dma_start(out=yv[ti], in_=xt[:, a, :]), "cv_y%d" % a,
                  reads=[K("xt", a)])
    P.barrier()
    c.release(m)


def build_conv_prog():
    st = contextlib.ExitStack()
    c = Ctx(st)
    x_d = c.dram_in("x", [TOK, D])
    ident_d = c.dram_in("ident", [128, 128])
    d = dict(w_in=c.dram_in("w_in", [128, 8, 2048]), w_out=c.dram_in("w_out", [128, 8 * 1024]),
             gpost=c.dram_in("gpost", [D]), small=c.dram_in("small", [128, 297]), b_out=c.dram_in("b_out", [1, D]),
             xh=c.dram_in("xh", [128, D]))
    y_d = c.dram_out("y", [TOK, D])
    setup_common(c, ident_d)
    conv_phase(c, x_d, y_d, d, "cv")
    c.P.barrier()
    c.P.emit(st)
    return c, st


def conv_inputs(I, i):
    j = i // 4
    small = np.zeros((128, 297), np.float32)
    small[:, 0:8] = col128(I["norm_pre"][i, 1], 8)
    small[:, 8:16] = col128(I["cv_b_in"][j][:1024], 8)
    small[:, 16:24] = col128(I["cv_b_in"][j][1024:], 8)
    small[:, 24:32] = col128(I["cv_dw_b"][j], 8)
    small[:, 32:40] = col128(I["cv_ln_g"][j], 8)
    small[:, 40:48] = col128(I["cv_ln_b"][j], 8)
    small[:, 48:296] = I["cv_dw"][j].reshape(31, 8, 128).transpose(2, 1, 0).reshape(128, 248)
    return dict(
        w_in=np.ascontiguousarray(I["cv_w_in"][j].reshape(8, 128, 2048).transpose(1, 0, 2)),
        w_out=np.ascontiguousarray(I["cv_w_out"][j].reshape(8, 128, 1024).transpose(1, 0, 2).reshape(128, 8 * 1024)),
        gpost=np.ascontiguousarray(I["norm_post"][i, 1]), small=small,
        b_out=np.ascontiguousarray(I["cv_b_out"][j].reshape(1, D)))


def conv_percore(small, xfull, i):
    sm = small.copy()
    sm[:, 296] = 0.0 if i == 0 else 1.0
    xh = np.zeros((128, D), np.float32) if i == 0 else np.ascontiguousarray(xfull[i * TOK - 128:i * TOK])
    return dict(x=np.ascontiguousarray(xfull[i * TOK:(i + 1) * TOK]), xh=xh, small=sm)


PATTERNS = ((128, 1), (512, 4), (2048, 16))
NEG = -30000.0
ATT_DEBUG = 0


def t5_onehot():
    oh = np.zeros((3, 32, 129), np.float32)
    for g, (_, dil) in enumerate(PATTERNS):
        dist = (np.arange(129) * dil).astype(np.int32)
        distf = np.maximum(dist, 1).astype(np.float32)
        large = 16 + (np.log(distf / np.float32(16)) / np.float32(np.log(2048 / 16)) * np.float32(16)).astype(np.int32)
        large = np.minimum(large, 31)
        b = np.where(dist < 16, dist, large)
        oh[g, b, np.arange(129)] = 1.0
    return oh


def attn_phase(c, x_d, y_d, d, tag):
    P = c.P
    nc = c.nc
    m = c.mark()
    K = lambda *a: (tag,) + a
    T2 = 2 * TOK
    hT = c.alloc(8 * T2, BF16).rearrange("p (k t) -> p k t", k=8)
    OT = c.alloc(8 * TOK, BF16).rearrange("p (h t) -> p h t", h=8)
    gpost = c.alloc(D, F32)
    sm = c.alloc(16, F32)
    gpre, hneg = sm[:, 0:8], sm[:, 8:9]
    junk = c.alloc(D, BF16)
    stt = c.alloc(8, F32)
    onesE = c.alloc(2 * 128, BF16).rearrange("p (h n) -> p h n", h=2)
    aid = c.alloc(128, F32)
    aidb = c.alloc(128, BF16)
    Fd = nc.dram_tensor("Fd_" + tag, [48, 384], F32)
    xv = x_d.rearrange("(t p) d -> t p d", p=128)
    xhv = d["xh"].rearrange("(t p) d -> t p d", p=128)
    yv = y_d.rearrange("(t p) d -> t p d", p=128)

    for dst, src, key in ((gpost, d["gpost"].partition_broadcast(128), "gpost"), (sm[:, 0:9], d["small"], "sm"),
                          (aid, d["antiid"], "aid")):
        P.dma("sp", lambda e, dst=dst, src=src: e.dma_start(out=dst, in_=src), "const_" + tag, writes=[K(key)],
              group=True)
    P.op("dve", lambda e: e.tensor_copy(out=aidb, in_=aid), reads=[K("aid")], writes=[K("aidb")])
    P.op("pool", lambda e: e.memset(onesE, 0.0), writes=[K("onesE")])
    P.op("pool", lambda e: e.memset(onesE[:, 0, 0:64], 1.0), writes=[K("onesE")])
    P.op("pool", lambda e: e.memset(onesE[:, 1, 64:128], 1.0), writes=[K("onesE")])

    m1 = c.mark()
    rb = c.alloc(48, F32)
    oh = c.alloc(3 * 129, F32).rearrange("p (g n) -> p g n", g=3)
    Fs = c.alloc(384, F32)
    stg = c.alloc(3 * 129, F32).rearrange("p (g n) -> p g n", g=3)
    P.dma("sp", lambda e: e.dma_start(out=rb[0:32, :], in_=d["rel_bias"]), "const2_" + tag, writes=[K("rb")], group=True)
    P.dma("sp", lambda e: e.dma_start(out=oh[0:32], in_=d["onehot"]), "const2_" + tag, writes=[K("oh")], group=True)
    P.op("pool", lambda e: e.memset(Fs[0:48, :], NEG), writes=[K("Fs")])
    P.dma("sp", lambda e: e.dma_start(out=Fd.ap(), in_=Fs[0:48, :]), "fd_" + tag, reads=[K("Fs")], writes=[K("Fd")])
    rbh = c.alloc(48, BF16)
    rbl = c.alloc(48, BF16)
    rbr = c.alloc(48, F32)
    ohb = c.alloc(3 * 129, BF16).rearrange("p (g n) -> p g n", g=3)
    P.op("dve", lambda e: e.tensor_copy(out=rbh[0:32, :], in_=rb[0:32, :]), reads=[K("rb")], writes=[K("rbh")])
    P.op("dve", lambda e: e.tensor_tensor(out=rbr[0:32, :], in0=rb[0:32, :], in1=rbh[0:32, :], op=ALU.subtract),
         reads=[K("rb"), K("rbh")], writes=[K("rbr")])
    P.op("dve", lambda e: e.tensor_copy(out=rbl[0:32, :], in_=rbr[0:32, :]), reads=[K("rbr")], writes=[K("rbl")])
    P.op("dve", lambda e: e.tensor_copy(out=ohb[0:32], in_=oh[0:32]), reads=[K("oh")], writes=[K("ohb")])
    for g in range(3):
        pt = c.bank(g)[0:48, 0:129]
        P.op("pe", lambda e, g=g, pt=pt: e.matmul(pt, lhsT=rbh[0:32, :], rhs=ohb[0:32, g, :], start=True, stop=False),
             reads=[K("rbh"), K("ohb")], writes=[("bank", g)])
        P.op("pe", lambda e, g=g, pt=pt: e.matmul(pt, lhsT=rbl[0:32, :], rhs=ohb[0:32, g, :], start=False, stop=True),
             reads=[K("rbl"), K("ohb")], writes=[("bank", g)])
        P.op("act", lambda e, g=g, pt=pt: e.activation(out=stg[0:48, g, :], in_=pt, func=AF.Copy),
             reads=[("bank", g)], writes=[K("stg", g)])
        P.dma("sp", lambda e, g=g: e.dma_start(out=Fd.ap()[16 * g:16 * g + 16, 127:256],
                                               in_=stg[16 * g:16 * g + 16, g, :]),
              "fd_" + tag, reads=[K("stg", g)], writes=[K("Fd")])

    xt = c.alloc(2 * D, F32).rearrange("p (a d) -> p a d", a=2)
    xn = c.alloc(2 * D, BF16).rearrange("p (a d) -> p a d", a=2)
    for ti in range(2 * NTT):
        a = ti % 2
        src = xhv[ti] if ti < NTT else xv[ti - NTT]
        P.dma("sp", lambda e, src=src, a=a: e.dma_start(out=xt[:, a, :], in_=src), "at_x%d" % a, writes=[K("xt", a)])
        norm_T(c, K, xt[:, a, :], K("xt", a), xn[:, a, :], K("xn", a), hT[:, :, ti * 128:(ti + 1) * 128],
               K("hT", ti), gpre, K("sm"), a, junk, stt[:, 0:1], stt[:, 1:2])
    P.barrier()
    c.release(m1)

    m2 = c.mark()
    wq = c.alloc(2 * 3 * 1024, BF16).rearrange("p (a j k n) -> p a j k n", a=2, j=3, k=8)
    qTm = c.alloc(2 * TOK, BF16).rearrange("p (h t) -> p h t", h=2)
    kT = c.alloc(T2, BF16)
    VE = c.alloc(32 * 2 * 128, BF16).rearrange("p (u h n) -> p u h n", u=32, h=2)
    accn = c.alloc(TOK, F32)
    accd = c.alloc(TOK, F32)
    bT = c.alloc(2 * 512, F32).rearrange("p (a x) -> p a x", a=2)
    bTs = c.alloc(2 * 512, F32).rearrange("p (a x) -> p a x", a=2)
    bTh = c.alloc(2 * 512, BF16).rearrange("p (a x) -> p a x", a=2)
    bTl = c.alloc(2 * 512, BF16).rearrange("p (a x) -> p a x", a=2)
    sc = c.alloc(2 * 256, F32).rearrange("p (a x) -> p a x", a=2)
    pT = c.alloc(2 * 512, BF16).rearrange("p (a h x) -> p a h x", a=2, h=2)
    P.op("pool", lambda e: e.memset(VE.rearrange("p u h n -> p (u h n)"), 0.0), writes=[K("VE")])
    P.op("pool", lambda e: e.memset(qTm.rearrange("p h t -> p (h t)"), 0.0), writes=[K("qT")])
    hkeys_all = [K("hT", t) for t in range(2 * NTT)]
    widx = 0
    for hp in range(8):
        P.op("pool", lambda e: e.memset(accn, 0.0), writes=[K("accn")])
        P.op("pool", lambda e: e.memset(accd, 1.0 if ATT_DEBUG else 0.0), writes=[K("accd")])
        for g, (_, dil) in enumerate(PATTERNS):
            if ATT_DEBUG == 1:
                continue
            S = 128 * dil
            nS = TOK // S
            wa = widx % 2
            widx += 1
            P.dma("pool", lambda e, g=g, hp=hp, wa=wa: e.dma_start(
                out=wq[:, wa].rearrange("p j k n -> p j (k n)"), in_=d["wqkv"][g, hp].rearrange("j p n -> p j n")),
                "at_w%d" % wa, writes=[K("wq", wa)])
            r0 = 16 * g + 2 * hp
            src = bass.AP(Fd, r0 * 384, [[1, 128], [384, 2], [128, 2], [1, 128]])
            P.dma("sp", lambda e, src=src, wa=wa: e.dma_start(
                out=bTs[:, wa, :].rearrange("p (h t q) -> p h t q", h=2, t=2), in_=src),
                "at_b%d" % wa, reads=[K("Fd")], writes=[K("bTs", wa)])
            pt = c.bank(wa)
            P.op("dve", lambda e, wa=wa: e.tensor_copy(out=bTh[:, wa, :], in_=bTs[:, wa, :]),
                 reads=[K("bTs", wa)], writes=[K("bTh", wa)])
            P.op("dve", lambda e, wa=wa: e.tensor_tensor(out=bTs[:, wa, :], in0=bTs[:, wa, :], in1=bTh[:, wa, :],
                                                         op=ALU.subtract),
                 reads=[K("bTs", wa), K("bTh", wa)], writes=[K("bTs", wa)])
            P.op("dve", lambda e, wa=wa: e.tensor_copy(out=bTl[:, wa, :], in_=bTs[:, wa, :]),
                 reads=[K("bTs", wa)], writes=[K("bTl", wa)])
            P.op("pe", lambda e, pt=pt, wa=wa: e.matmul(pt, lhsT=aidb, rhs=bTh[:, wa, :], start=True, stop=False),
                 reads=[K("aidb"), K("bTh", wa)], writes=[("bank", wa)])
            P.op("pe", lambda e, pt=pt, wa=wa: e.matmul(pt, lhsT=aidb, rhs=bTl[:, wa, :], start=False, stop=True),
                 reads=[K("aidb"), K("bTl", wa)], writes=[("bank", wa)])
            P.op("act", lambda e, pt=pt, wa=wa: e.activation(out=bT[:, wa, :], in_=pt, func=AF.Copy),
                 reads=[("bank", wa)], writes=[K("bT", wa)])
            if ATT_DEBUG == 2:
                continue
            blocks = [(TOK + b * 512, 512, True) for b in range(4)]
            if S >= 512:
                blocks = [(TOK - S + b * 512, 512, False) for b in range(S // 512)] + blocks
            else:
                blocks = [(TOK - S, S, False)] + blocks
            bi_n = 0
            for (c0, n, own) in blocks:
                tiles = [K("hT", t) for t in range(c0 // 128, (c0 + n - 1) // 128 + 1)]
                for j in ((0, 1) if own else (1,)):
                    bi = 2 + bi_n % 2
                    bi_n += 1
                    pt = c.bank(bi)[:, 0:n]
                    for k in range(8):
                        P.op("pe", lambda e, pt=pt, wa=wa, j=j, k=k, c0=c0, n=n: e.matmul(
                            pt, lhsT=wq[:, wa, j, k, :], rhs=hT[:, k, c0:c0 + n], start=(k == 0), stop=(k == 7)),
                            reads=tiles + [K("wq", wa)], writes=[("bank", bi)])
                    if j == 0:
                        for hh in range(2):
                            P.op("act", lambda e, pt=pt, hh=hh, c0=c0, n=n: e.activation(
                                out=qTm[64 * hh:64 * hh + 64, hh, c0 - TOK:c0 - TOK + n], in_=pt[64 * hh:64 * hh + 64, :],
                                func=AF.Copy), reads=[("bank", bi)], writes=[K("qT")])
                    else:
                        P.op("act", lambda e, pt=pt, c0=c0, n=n: e.activation(out=kT[:, c0:c0 + n], in_=pt, func=AF.Copy),
                             reads=[("bank", bi)], writes=[K("kT")])
            if ATT_DEBUG == 3:
                continue
            units = [(n_, r_) for n_ in range(-1, nS) for r_ in range(dil)]
            for ui, (n_, r_) in enumerate(units):
                bi = 4 + (ui // 4) % 2
                pt = c.bank(bi)[:, (ui % 4) * 128:(ui % 4 + 1) * 128]
                t0 = TOK + n_ * S + r_
                tiles = [K("hT", t) for t in range(t0 // 128, (t0 + 127 * dil) // 128 + 1)]
                for k in range(8):
                    P.op("pe", lambda e, pt=pt, wa=wa, k=k, t0=t0, dil=dil: e.matmul(
                        pt, lhsT=hT[:, k, t0:t0 + 127 * dil + 1:dil], rhs=wq[:, wa, 2, k, :], start=(k == 0), stop=(k == 7)),
                        reads=tiles + [K("wq", wa)], writes=[("bank", bi)])
                if ui % 4 == 3 or ui == len(units) - 1:
                    for u2 in range(ui - ui % 4, ui + 1):
                        p2 = c.bank(bi)[:, (u2 % 4) * 128:(u2 % 4 + 1) * 128]
                        P.op("act", lambda e, p2=p2, u2=u2: e.activation(out=VE[:, u2, 0, 0:64], in_=p2[:, 0:64], func=AF.Copy),
                             reads=[("bank", bi)], writes=[K("VE", u2)])
                        P.op("act", lambda e, p2=p2, u2=u2: e.activation(out=VE[:, u2, 1, 64:128], in_=p2[:, 64:128], func=AF.Copy),
                             reads=[("bank", bi)], writes=[K("VE", u2)])
            if ATT_DEBUG == 4:
                continue
            for n_ in range(nS):
                for r_ in range(dil):
                    uc = (n_ + 1) * dil + r_
                    up = n_ * dil + r_
                    q0 = n_ * S + r_
                    kp0 = TOK + (n_ - 1) * S + r_
                    kc0 = TOK + n_ * S + r_
                    ua = (n_ * dil + r_) % 2
                    bs_ = 6
                    for hh in range(2):
                        ps = c.bank(bs_)[:, hh * 256:(hh + 1) * 256]
                        for tl, k0 in ((0, kc0), (1, kp0)):
                            P.op("pe", lambda e, ps=ps, hh=hh, tl=tl, k0=k0, q0=q0, dil=dil: e.matmul(
                                ps[:, tl * 128:(tl + 1) * 128],
                                lhsT=kT[:, k0:k0 + 127 * dil + 1:dil],
                                rhs=qTm[:, hh, q0:q0 + 127 * dil + 1:dil], start=True, stop=True),
                                reads=[K("kT"), K("qT")], writes=[("bank", bs_)])
                    for hh in range(2):
                        ps = c.bank(bs_)[:, hh * 256:(hh + 1) * 256]
                        P.op("dve", lambda e, ps=ps, hh=hh, wa=wa: e.scalar_tensor_tensor(
                            out=sc[:, hh, :], in0=ps, scalar=0.125, in1=bT[:, wa, hh * 256:(hh + 1) * 256],
                            op0=ALU.mult, op1=ALU.add),
                            reads=[("bank", bs_), K("bT", wa)], writes=[K("sc", hh)])
                        if n_ == 0:
                            P.op("act", lambda e, hh=hh, ua=ua: e.activation(out=pT[:, ua, hh, 128:256], in_=sc[:, hh, 128:256],
                                                                           func=AF.Exp, bias=hneg),
                                 reads=[K("sc", hh), K("sm")], writes=[K("pT", ua, hh)])
                            P.op("act", lambda e, hh=hh, ua=ua: e.activation(out=pT[:, ua, hh, 0:128],
                                                                           in_=sc[:, hh, 0:128], func=AF.Exp),
                                 reads=[K("sc", hh)], writes=[K("pT", ua, hh)])
                        else:
                            P.op("act", lambda e, hh=hh, ua=ua: e.activation(out=pT[:, ua, hh, :], in_=sc[:, hh, :],
                                                                           func=AF.Exp),
                                 reads=[K("sc", hh)], writes=[K("pT", ua, hh)])
                    if ATT_DEBUG == 5:
                        continue
                    pn, pd = c.bank(7)[:, 0:128], c.bank(7)[:, 128:256]
                    for (pt, lh) in ((pn, None), (pd, onesE)):
                        i4 = 0
                        for hh in range(2):
                            for tl, uu in ((0, uc), (1, up)):
                                lhs = VE[:, uu, hh, :] if lh is None else onesE[:, hh, :]
                                P.op("pe", lambda e, pt=pt, lhs=lhs, ua=ua, hh=hh, tl=tl, i4=i4: e.matmul(
                                    pt, lhsT=lhs, rhs=pT[:, ua, hh, tl * 128:(tl + 1) * 128], start=(i4 == 0), stop=(i4 == 3)),
                                    reads=[K("pT", ua, hh), K("VE", uu), K("onesE")], writes=[("bank", 7)])
                                i4 += 1
                    sl = slice(q0, q0 + 127 * dil + 1, dil)
                    P.op("dve", lambda e, sl=sl, pn=pn: e.tensor_tensor(out=accn[:, sl], in0=accn[:, sl], in1=pn, op=ALU.add),
                         reads=[("bank", 7), K("accn")], writes=[K("accn")])
                    P.op("dve", lambda e, sl=sl, pd=pd: e.tensor_tensor(out=accd[:, sl], in0=accd[:, sl], in1=pd, op=ALU.add),
                         reads=[("bank", 7), K("accd")], writes=[K("accd")])
        P.op("dve", lambda e: e.reciprocal(out=accd, in_=accd), reads=[K("accd")], writes=[K("accd")])
        P.op("dve", lambda e, hp=hp: e.tensor_tensor(out=OT[:, hp, :], in0=accn, in1=accd, op=ALU.mult),
             reads=[K("accn"), K("accd")], writes=[K("OT", hp)])
    P.barrier()
    c.release(m2)

    w_out = c.alloc(8 * 1024, BF16).rearrange("p (k n) -> p k n", k=8)
    Hh = c.alloc(D, F32)
    xtc = c.alloc(2 * D, F32).rearrange("p (a d) -> p a d", a=2)
    P.dma("pool", lambda e: e.dma_start(out=w_out.rearrange("p k n -> p (k n)"), in_=d["w_out"]), "at_wo",
          writes=[K("w_out")])
    for ti in range(NTT):
        a = ti % 2
        P.dma("sp", lambda e, ti=ti, a=a: e.dma_start(out=xtc[:, a, :], in_=xv[ti]), "at_x%d" % a, writes=[K("xt2", a)])
        for hf in range(2):
            pt = c.bank(hf)
            for hp in range(8):
                P.op("pe", lambda e, pt=pt, hp=hp, hf=hf, ti=ti: e.matmul(
                    pt, lhsT=OT[:, hp, ti * 128:(ti + 1) * 128], rhs=w_out[:, hp, hf * 512:(hf + 1) * 512],
                    start=(hp == 0), stop=(hp == 7)),
                    reads=[K("OT", hp), K("w_out")], writes=[("bank", hf)])
            P.op("act", lambda e, pt=pt, hf=hf: e.activation(out=Hh[:, hf * 512:(hf + 1) * 512], in_=pt, func=AF.Copy),
                 reads=[("bank", hf)], writes=[K("Hh", hf)])
        post_norm_add(c, Hh, [K("Hh", 0), K("Hh", 1)], xtc[:, a, :], K("xt2", a), gpost, K("gpost"), junk,
                      stt[:, 2:3], stt[:, 3:4], K, 1.0)
        P.dma("sp", lambda e, ti=ti, a=a: e.dma_start(out=yv[ti], in_=xtc[:, a, :]), "at_y%d" % a, reads=[K("xt2", a)])
    P.barrier()
    c.release(m)


def build_attn_prog():
    st = contextlib.ExitStack()
    c = Ctx(st)
    x_d = c.dram_in("x", [TOK, D])
    ident_d = c.dram_in("ident", [128, 128])
    d = dict(wqkv=c.dram_in("wqkv", [3, 8, 3, 128, 1024]), w_out=c.dram_in("w_out", [128, 8 * 1024]),
             gpost=c.dram_in("gpost", [D]), small=c.dram_in("small", [128, 9]), xh=c.dram_in("xh", [TOK, D]),
             rel_bias=c.dram_in("rel_bias", [32, 48]), onehot=c.dram_in("onehot", [32, 3, 129]),
             antiid=c.dram_in("antiid", [128, 128]))
    y_d = c.dram_out("y", [TOK, D])
    setup_common(c, ident_d)
    attn_phase(c, x_d, y_d, d, "at")
    c.P.barrier()
    c.P.emit(st)
    return c, st


def attn_inputs(I, i):
    j = i // 4
    w = I["at_w_qkv"][j].reshape(8, 128, 3, 3, 8, 128)
    wqkv = np.ascontiguousarray(w.transpose(2, 4, 3, 1, 0, 5).reshape(3, 8, 3, 128, 1024))
    small = np.zeros((128, 9), np.float32)
    small[:, 0:8] = col128(I["norm_pre"][i, 1], 8)
    return dict(
        wqkv=wqkv,
        w_out=np.ascontiguousarray(I["at_w_out"][j].reshape(8, 128, 1024).transpose(1, 0, 2).reshape(128, 8 * 1024)),
        gpost=np.ascontiguousarray(I["norm_post"][i, 1]), small=small,
        rel_bias=np.ascontiguousarray(I["rel_bias"]), antiid=np.ascontiguousarray(np.eye(128, dtype=np.float32)[::-1]),
        onehot=np.ascontiguousarray(t5_onehot().transpose(1, 0, 2)))


def attn_percore(small, xfull, i):
    sm = small.copy()
    sm[:, 8] = NEG if i == 0 else 0.0
    xh = np.zeros((TOK, D), np.float32) if i == 0 else np.ascontiguousarray(xfull[(i - 1) * TOK:i * TOK])
    return dict(x=np.ascontiguousarray(xfull[i * TOK:(i + 1) * TOK]), xh=xh, small=sm)


I32 = mybir.dt.int32
NLIST = list(range(-7, 9))
NCH = SEQ // 8
PI = float(np.pi)


def s5a_phase(c, x_d, u_d, d, tag):
    P = c.P
    m = c.mark()
    K = lambda *a: (tag,) + a
    w_in = c.alloc(8 * 1024, BF16).rearrange("p (k n) -> p k n", k=8)
    gpre = c.alloc(8, F32)
    xt = c.alloc(2 * D, F32).rearrange("p (a d) -> p a d", a=2)
    xn = c.alloc(2 * D, BF16).rearrange("p (a d) -> p a d", a=2)
    hT = c.alloc(2 * D, BF16).rearrange("p (a k t) -> p a k t", a=2, k=8)
    ut = c.alloc(2 * D, F32).rearrange("p (a d) -> p a d", a=2)
    junk = c.alloc(D, BF16)
    stt = c.alloc(8, F32)
    P.dma("pool", lambda e: e.dma_start(out=w_in.rearrange("p k n -> p (k n)"), in_=d["w_in"]), "s5a_w", writes=[K("w_in")])
    P.dma("sp", lambda e: e.dma_start(out=gpre, in_=d["gpre"]), "const_" + tag, writes=[K("gpre")], group=True)
    xv = x_d.rearrange("(t p) d -> t p d", p=128)
    uv = u_d.rearrange("(t p) d -> t p d", p=128)
    for tt in range(NTT):
        a = tt % 2
        P.dma("sp", lambda e, tt=tt, a=a: e.dma_start(out=xt[:, a, :], in_=xv[tt]), "s5a_x%d" % a, writes=[K("xt", a)])
        norm_T(c, K, xt[:, a, :], K("xt", a), xn[:, a, :], K("xn", a), hT[:, a], K("hT", a), gpre, K("gpre"), a, junk,
               stt[:, 0:1], stt[:, 1:2])
        for hf in range(2):
            bi = 2 + hf
            pt = c.bank(bi)
            for k in range(8):
                P.op("pe", lambda e, pt=pt, k=k, a=a, hf=hf: e.matmul(
                    pt, lhsT=hT[:, a, k, :], rhs=w_in[:, k, hf * 512:(hf + 1) * 512], start=(k == 0), stop=(k == 7)),
                    reads=[K("hT", a), K("w_in")], writes=[("bank", bi)])
            P.op("act", lambda e, pt=pt, a=a, hf=hf: e.activation(out=ut[:, a, hf * 512:(hf + 1) * 512], in_=pt, func=AF.Copy),
                 reads=[("bank", bi)], writes=[K("ut", a, hf)])
        P.dma("sp", lambda e, tt=tt, a=a: e.dma_start(out=uv[tt], in_=ut[:, a, :]), "s5a_u%d" % a,
              reads=[K("ut", a, 0), K("ut", a, 1)])
    P.barrier()
    c.release(m)


def s5b_phase(c, U_d, Y_d, d, tag, nch):
    P = c.P
    m = c.mark()
    K = lambda *a: (tag,) + a
    nst = int(np.log2(nch))
    assert 1 << nst == nch
    NT = (nch + 511) // 512
    W = min(nch, 512)
    f = lambda n: c.alloc(n, F32)
    par = f(24).rearrange("p (a g) -> p a g", a=3)
    sgn = f(2)
    bA1, bA2, cP1, cP2 = [f(128).rearrange("p (g h) -> p g h", g=8) for _ in range(4)]
    idf, jsf, mkf = f(128), f(128), f(128)
    dt, x1, th, den, nr, fre, fim, t8a, t8b = [f(8) for _ in range(9)]
    EX, ANG, MAG, SN, CS, LR, LI, sLR = [f(128).rearrange("p (n g) -> p n g", n=16) for _ in range(8)]
    a1, a2, a3 = f(128), f(128), f(128)
    ai = c.alloc(128, I32)
    FR, FI, sFI = [f(64).rearrange("p (n g) -> p n g", n=8) for _ in range(3)]
    PR, PIm, sPI = [f(8 * (nst + 1)).rearrange("p (n g) -> p n g", g=8) for _ in range(3)]
    q1, q2 = f(8), f(8)
    SETS = []
    for _s in range(2):
        SETS.append(dict(
            Bm=f(128), WT=f(128), W2=f(128), t16=f(2 * 16).rearrange("p (a h) -> p a h", a=2),
            Bb=c.alloc(128, BF16), WTb=c.alloc(128, BF16), W2b=c.alloc(128, BF16), BTb=c.alloc(128, BF16),
            KTb=c.alloc(128, BF16), MTf=f(128),
            MTb=c.alloc((nst + 1) * 128, BF16).rearrange("p (k n) -> p k n", n=128),
            U=f(nch), Uh=c.alloc(nch, BF16), Ul=c.alloc(nch, BF16), X=f(nch),
            Xh=c.alloc(nch + 16, BF16), Xl=c.alloc(nch + 16, BF16), tmp=f(nch),
            yo=f(2 * 128).rearrange("p (a n) -> p a n", a=2)))

    for dst, src, key in ((par, d["par"], "par"), (sgn, d["sgn"], "sgn"), (bA1, d["bA1"], "bA1"), (bA2, d["bA2"], "bA2"),
                          (cP1, d["cP1"], "cP1"), (cP2, d["cP2"], "cP2"), (idf, d["ident"], "idf2"),
                          (jsf, d["jshift"], "jsf"), (mkf, d["maskK"], "mkf")):
        P.dma("sp", lambda e, dst=dst, src=src: e.dma_start(out=dst, in_=src), "const_" + tag, writes=[K(key)], group=True)

    def ew(eng, fn, reads, writes):
        P.op(eng, fn, reads=[K(r) for r in reads], writes=[K(w) for w in writes])

    ar, aim, ldt = par[:, 0, :], par[:, 1, :], par[:, 2, :]
    sg1, sA = sgn[:, 0:1], sgn[:, 1:2]
    ew("act", lambda e: e.activation(out=dt, in_=ldt, func=AF.Exp), ["par"], ["dt"])
    ew("dve", lambda e: e.tensor_tensor(out=x1, in0=dt, in1=ar, op=ALU.mult), ["dt", "par"], ["x1"])
    ew("dve", lambda e: e.tensor_tensor(out=th, in0=dt, in1=aim, op=ALU.mult), ["dt", "par"], ["th"])
    for idx, n in enumerate(NLIST):
        ew("dve", lambda e, idx=idx, n=n: e.tensor_scalar(out=EX[:, idx, :], in0=x1, scalar1=float(n), scalar2=None, op0=ALU.mult),
           ["x1"], ["EX"])
        ew("dve", lambda e, idx=idx, n=n: e.tensor_scalar(out=ANG[:, idx, :], in0=th, scalar1=float(n), scalar2=None, op0=ALU.mult),
           ["th"], ["ANG"])
    fl = lambda t: t.rearrange("p n g -> p (n g)")
    ew("act", lambda e: e.activation(out=fl(MAG), in_=fl(EX), func=AF.Exp), ["EX"], ["MAG"])

    def sin_of(dst, shift, key):
        ew("dve", lambda e: e.tensor_scalar(out=a1, in0=fl(ANG), scalar1=shift, scalar2=1.0 / (2 * PI), op0=ALU.add, op1=ALU.mult),
           ["ANG"], ["a1"])
        ew("dve", lambda e: e.tensor_copy(out=ai, in_=a1), ["a1"], ["ai"])
        ew("dve", lambda e: e.tensor_copy(out=a2, in_=ai), ["ai"], ["a2"])
        ew("dve", lambda e: e.tensor_scalar(out=a1, in0=fl(ANG), scalar1=shift, scalar2=None, op0=ALU.add), ["ANG", "a1"], ["a1"])
        ew("dve", lambda e: e.scalar_tensor_tensor(out=a1, in0=a2, scalar=-2 * PI, in1=a1, op0=ALU.mult, op1=ALU.add),
           ["a2", "a1"], ["a1"])
        for thr, op, add in ((PI, ALU.is_gt, -2 * PI), (-PI, ALU.is_lt, 2 * PI)):
            ew("dve", lambda e, thr=thr, op=op: e.tensor_scalar(out=a3, in0=a1, scalar1=thr, scalar2=None, op0=op), ["a1"], ["a3"])
            ew("dve", lambda e, add=add: e.scalar_tensor_tensor(out=a1, in0=a3, scalar=add, in1=a1, op0=ALU.mult, op1=ALU.add),
               ["a3", "a1"], ["a1"])
        ew("act", lambda e: e.activation(out=fl(dst), in_=a1, func=AF.Sin), ["a1"], [key])

    sin_of(SN, 0.0, "SN")
    sin_of(CS, PI / 2, "CS")
    ew("dve", lambda e: e.tensor_tensor(out=fl(LR), in0=fl(MAG), in1=fl(CS), op=ALU.mult), ["MAG", "CS"], ["LR"])
    ew("dve", lambda e: e.tensor_tensor(out=fl(LI), in0=fl(MAG), in1=fl(SN), op=ALU.mult), ["MAG", "SN"], ["LI"])
    ew("dve", lambda e: e.tensor_scalar(out=fl(sLR), in0=fl(LR), scalar1=sA, scalar2=None, op0=ALU.mult), ["LR", "sgn"], ["sLR"])
    i1 = NLIST.index(1)
    ew("dve", lambda e: e.tensor_tensor(out=den, in0=ar, in1=ar, op=ALU.mult), ["par"], ["den"])
    ew("dve", lambda e: e.tensor_tensor(out=t8a, in0=aim, in1=aim, op=ALU.mult), ["par"], ["t8a"])
    ew("dve", lambda e: e.tensor_tensor(out=den, in0=den, in1=t8a, op=ALU.add), ["den", "t8a"], ["den"])
    ew("dve", lambda e: e.reciprocal(out=den, in_=den), ["den"], ["den"])
    ew("dve", lambda e: e.tensor_scalar(out=nr, in0=LR[:, i1, :], scalar1=-1.0, scalar2=None, op0=ALU.add), ["LR"], ["nr"])
    ew("dve", lambda e: e.tensor_tensor(out=t8a, in0=nr, in1=ar, op=ALU.mult), ["nr", "par", "t8a"], ["t8a"])
    ew("dve", lambda e: e.tensor_tensor(out=t8b, in0=LI[:, i1, :], in1=aim, op=ALU.mult), ["LI", "par"], ["t8b"])
    ew("dve", lambda e: e.tensor_tensor(out=fre, in0=t8a, in1=t8b, op=ALU.add), ["t8a", "t8b"], ["fre"])
    ew("dve", lambda e: e.tensor_tensor(out=fre, in0=fre, in1=den, op=ALU.mult), ["fre", "den"], ["fre"])
    ew("dve", lambda e: e.tensor_tensor(out=t8a, in0=LI[:, i1, :], in1=ar, op=ALU.mult), ["LI", "par", "fre"], ["t8a"])
    ew("dve", lambda e: e.tensor_tensor(out=t8b, in0=nr, in1=aim, op=ALU.mult), ["nr", "par", "fre"], ["t8b"])
    ew("dve", lambda e: e.tensor_tensor(out=fim, in0=t8a, in1=t8b, op=ALU.subtract), ["t8a", "t8b"], ["fim"])
    ew("dve", lambda e: e.tensor_tensor(out=fim, in0=fim, in1=den, op=ALU.mult), ["fim", "den"], ["fim"])
    for n in range(8):
        idx = NLIST.index(n)
        ew("dve", lambda e, idx=idx: e.tensor_tensor(out=t8a, in0=LI[:, idx, :], in1=fim, op=ALU.mult), ["LI", "fim", "FR", "FI"], ["t8a"])
        ew("dve", lambda e, idx=idx: e.tensor_tensor(out=t8b, in0=LR[:, idx, :], in1=fre, op=ALU.mult), ["LR", "fre", "FR", "FI"], ["t8b"])
        ew("dve", lambda e, n=n: e.tensor_tensor(out=FR[:, n, :], in0=t8b, in1=t8a, op=ALU.subtract), ["t8a", "t8b"], ["FR"])
        ew("dve", lambda e, idx=idx: e.tensor_tensor(out=t8a, in0=LR[:, idx, :], in1=fim, op=ALU.mult), ["LR", "fim", "FR"], ["t8a"])
        ew("dve", lambda e, idx=idx: e.tensor_tensor(out=t8b, in0=LI[:, idx, :], in1=fre, op=ALU.mult), ["LI", "fre", "FR"], ["t8b"])
        ew("dve", lambda e, n=n: e.tensor_tensor(out=FI[:, n, :], in0=t8a, in1=t8b, op=ALU.add), ["t8a", "t8b"], ["FI"])
    fl8 = lambda t: t.rearrange("p n g -> p (n g)")
    ew("dve", lambda e: e.tensor_scalar(out=fl8(sFI), in0=fl8(FI), scalar1=sg1, scalar2=None, op0=ALU.mult), ["FI", "sgn"], ["sFI"])
    i8 = NLIST.index(8)
    ew("dve", lambda e: e.tensor_copy(out=PR[:, 0, :], in_=LR[:, i8, :]), ["LR"], ["PR"])
    ew("dve", lambda e: e.tensor_copy(out=PIm[:, 0, :], in_=LI[:, i8, :]), ["LI"], ["PI"])
    for k in range(nst):
        ew("dve", lambda e, k=k: e.tensor_tensor(out=q1, in0=PR[:, k, :], in1=PR[:, k, :], op=ALU.mult), ["PR", "PI"], ["q1"])
        ew("dve", lambda e, k=k: e.tensor_tensor(out=q2, in0=PIm[:, k, :], in1=PIm[:, k, :], op=ALU.mult), ["PR", "PI"], ["q2"])
        ew("dve", lambda e, k=k: e.tensor_tensor(out=PR[:, k + 1, :], in0=q1, in1=q2, op=ALU.subtract), ["q1", "q2"], ["PR"])
        ew("dve", lambda e, k=k: e.tensor_tensor(out=q1, in0=PR[:, k, :], in1=PIm[:, k, :], op=ALU.mult), ["PR", "PI", "q1"], ["q1"])
        ew("dve", lambda e, k=k: e.tensor_scalar(out=PIm[:, k + 1, :], in0=q1, scalar1=2.0, scalar2=None, op0=ALU.mult), ["q1"], ["PI"])
    ew("dve", lambda e: e.tensor_scalar(out=sPI.rearrange("p n g -> p (n g)"), in0=PIm.rearrange("p n g -> p (n g)"),
                                        scalar1=sA, scalar2=None, op0=ALU.mult), ["PI", "sgn"], ["sPI"])
    for _s in range(2):
        P.op("pool", lambda e, _s=_s: e.memset(SETS[_s]["Xh"][:, 0:16], 0.0), writes=[K("Xh0")])
        P.op("pool", lambda e, _s=_s: e.memset(SETS[_s]["Xl"][:, 0:16], 0.0), writes=[K("Xl0")])

    def emit_group(gl):
        S_ = SETS[gl % 2]
        sx = gl % 2
        Bm, WT, W2, t16, Bb, WTb, W2b, BTb, KTb, MTf, MTb = [S_[n] for n in ("Bm", "WT", "W2", "t16", "Bb", "WTb", "W2b", "BTb", "KTb", "MTf", "MTb")]
        U, Uh, Ul, X, Xh, Xl, tmp, yo = [S_[n] for n in ("U", "Uh", "Ul", "X", "Xh", "Xl", "tmp", "yo")]
        B3 = Bm.rearrange("p (i h) -> p i h", i=8)
        W3 = WT.rearrange("p (i h) -> p i h", i=8)
        W23 = W2.rearrange("p (i h) -> p i h", i=8)
        SK = ("Bm", "WT", "W2", "Bb", "WTb", "W2b", "BTb", "KTb", "MTf", "MTb", "U", "Uh", "Ul", "X", "Xh", "Xl", "tmp", "t16", "yo")

        def kk(r):
            if isinstance(r, tuple):
                return (r[0] + str(sx),) + r[1:] if r[0] in SK else r
            return (r + str(sx)) if r in SK else r

        def ew(eng, fn, reads, writes):
            P.op(eng, fn, reads=[K(kk(r)) for r in reads], writes=[K(kk(w)) for w in writes])

        KS = lambda *a: K(kk(a[0] if len(a) == 1 else tuple(a)))
        for i in range(8):
            n = 7 - i
            ta = i % 2
            ew("dve", lambda e, gl=gl, n=n, ta=ta: e.tensor_scalar(out=t16[:, ta, :], in0=bA2[:, gl, :], scalar1=sFI[:, n, gl:gl + 1],
                                                                scalar2=None, op0=ALU.mult), ["bA2", "sFI"], [("t16", ta)])
            ew("dve", lambda e, gl=gl, n=n, i=i, ta=ta: e.scalar_tensor_tensor(out=B3[:, i, :], in0=bA1[:, gl, :], scalar=FR[:, n, gl:gl + 1],
                                                                       in1=t16[:, ta, :], op0=ALU.mult, op1=ALU.add),
               ["bA1", "FR", ("t16", ta)], ["Bm"])
        for j in range(8):
            for (dst3, idx, key) in ((W3, NLIST.index(j + 1), "WT"), (W23, NLIST.index(j - 7), "W2")):
                ew("dve", lambda e, gl=gl, idx=idx: e.tensor_scalar(out=t16[:, 0, :], in0=cP2[:, gl, :], scalar1=LI[:, idx, gl:gl + 1],
                                                                  scalar2=None, op0=ALU.mult), ["cP2", "LI"], [("t16", 0)])
                ew("dve", lambda e, gl=gl, idx=idx, dst3=dst3, j=j: e.scalar_tensor_tensor(
                    out=dst3[:, j, :], in0=cP1[:, gl, :], scalar=sLR[:, idx, gl:gl + 1], in1=t16[:, 0, :], op0=ALU.mult, op1=ALU.subtract),
                    ["cP1", "sLR", ("t16", 0)], [key])
        ew("act", lambda e: e.activation(out=Bb, in_=Bm, func=AF.Copy), ["Bm"], ["Bb"])
        ew("act", lambda e: e.activation(out=WTb, in_=WT, func=AF.Copy), ["WT"], ["WTb"])
        ew("act", lambda e: e.activation(out=W2b, in_=W2, func=AF.Copy), ["W2"], ["W2b"])
        pb = c.bank(sx, BF16)[:, 0:128]
        P.op("pe", lambda e: e.transpose(out=pb, in_=Bb, identity=c.idb), reads=[KS("Bb"), "idb"], writes=[("bank", sx)])
        P.op("act", lambda e: e.activation(out=BTb, in_=pb, func=AF.Copy), reads=[("bank", sx)], writes=[KS("BTb")])
        pk = c.bank(sx)[:, 128:256]
        P.op("pe", lambda e: e.matmul(pk, lhsT=Bb, rhs=W2b, start=True, stop=True), reads=[KS("Bb"), KS("W2b")], writes=[("bank", sx)])
        P.op("dve", lambda e: e.tensor_tensor(out=KTb, in0=pk, in1=mkf, op=ALU.mult), reads=[("bank", sx), K("mkf")], writes=[KS("KTb")])
        for k in range(nst + 1):
            ew("dve", lambda e, k=k, gl=gl: e.tensor_scalar(out=MTf, in0=jsf, scalar1=sPI[:, k, gl:gl + 1], scalar2=None, op0=ALU.mult),
               ["jsf", "sPI"], ["MTf"])
            ew("dve", lambda e, k=k, gl=gl: e.scalar_tensor_tensor(out=MTb[:, k, :], in0=idf, scalar=PR[:, k, gl:gl + 1], in1=MTf,
                                                               op0=ALU.mult, op1=ALU.add), ["idf2", "PR", "MTf"], [("MTb", k)])

        def split(src, hi, lo, skey, hkey, lkey):
            ew("act", lambda e: e.activation(out=hi, in_=src, func=AF.Copy), [skey], [hkey])
            ew("dve", lambda e: e.tensor_tensor(out=tmp, in0=src, in1=hi, op=ALU.subtract), [skey, hkey], ["tmp"])
            ew("act", lambda e: e.activation(out=lo, in_=tmp, func=AF.Copy), ["tmp"], [lkey])

        c.dbg = dict(dt=dt, x1=x1, th=th, LR=LR, LI=LI, MAG=MAG, SN=SN, CS=CS, FR=FR, FI=FI, fre=fre, fim=fim, PR=PR, PIm=PIm,
                     Bm=Bm, WT=WT, W2=W2, X=X, U=U, ANG=ANG, a1=a1, a2=a2)
        P.dma("sp", lambda e, gl=gl: e.dma_start(out=U, in_=U_d[gl]), "s5b_u%d" % sx, writes=[KS("U")])
        split(U, Uh, Ul, "U", "Uh", "Ul")
        for nt in range(NT):
            bi = 2 + 2 * sx + nt % 2
            pt = c.bank(bi)[:, 0:W]
            for q, src in enumerate((Uh, Ul)):
                P.op("pe", lambda e, pt=pt, src=src, nt=nt, q=q: e.matmul(pt, lhsT=BTb, rhs=src[:, nt * W:(nt + 1) * W],
                                                                       start=(q == 0), stop=(q == 1)),
                     reads=[KS("BTb"), KS("Uh"), KS("Ul")], writes=[("bank", bi)])
            P.op("act", lambda e, pt=pt, nt=nt: e.activation(out=X[:, nt * W:(nt + 1) * W], in_=pt, func=AF.Copy),
                 reads=[("bank", bi)], writes=[KS("X")])
        for k in range(nst):
            s = 1 << k
            split(X, Xh[:, 16:16 + nch], Xl[:, 16:16 + nch], "X", "Xh", "Xl")
            n_in = nch - s
            for nt in range((n_in + W - 1) // W):
                c0 = nt * W
                w = min(W, n_in - c0)
                bi = 2 + 2 * sx + nt % 2
                pt = c.bank(bi)[:, 0:w]
                for q, src in enumerate((Xh, Xl)):
                    P.op("pe", lambda e, pt=pt, src=src, c0=c0, w=w, k=k, q=q: e.matmul(
                        pt, lhsT=MTb[:, k, :], rhs=src[:, 16 + c0:16 + c0 + w], start=(q == 0), stop=(q == 1)),
                        reads=[KS("MTb", k), KS("Xh"), KS("Xl")], writes=[("bank", bi)])
                P.op("dve", lambda e, pt=pt, c0=c0, w=w, s=s: e.tensor_tensor(
                    out=X[:, s + c0:s + c0 + w], in0=X[:, s + c0:s + c0 + w], in1=pt, op=ALU.add),
                    reads=[("bank", bi), KS("X")], writes=[KS("X")])
        split(X, Xh[:, 16:16 + nch], Xl[:, 16:16 + nch], "X", "Xh", "Xl")
        for ct in range(nch // 128):
            bi = 6 + sx
            pt = c.bank(bi)[:, 0:128]
            ops = [(Xh[:, 15 + ct * 128:15 + (ct + 1) * 128], WTb, "WTb"), (Xl[:, 15 + ct * 128:15 + (ct + 1) * 128], WTb, "WTb"),
                   (Uh[:, ct * 128:(ct + 1) * 128], KTb, "KTb"), (Ul[:, ct * 128:(ct + 1) * 128], KTb, "KTb")]
            for q, (lh, rh, rkey) in enumerate(ops):
                P.op("pe", lambda e, pt=pt, lh=lh, rh=rh, q=q: e.matmul(pt, lhsT=lh, rhs=rh, start=(q == 0), stop=(q == 3)),
                     reads=[KS("Xh"), KS("Xl"), KS("Uh"), KS("Ul"), KS(rkey), K("Xh0"), K("Xl0")], writes=[("bank", bi)])
            ya = ct % 2
            P.op("act", lambda e, pt=pt, ya=ya: e.activation(out=yo[:, ya, :], in_=pt, func=AF.Copy),
                 reads=[("bank", bi)], writes=[KS("yo", ya)])
            P.dma("sp", lambda e, gl=gl, ct=ct, ya=ya: e.dma_start(out=Y_d[gl, ct * 128:(ct + 1) * 128, :], in_=yo[:, ya, :]),
                  "s5b_y%d_%d" % (sx, ya), reads=[KS("yo", ya)])

    for gl in range(8):
        emit_group(gl)
    P.barrier()
    c.release(m)


def s5c_phase(c, x_d, y_d, d, tag):
    P = c.P
    m = c.mark()
    K = lambda *a: (tag,) + a
    w_glu = c.alloc(8 * 1024, BF16).rearrange("p (k n) -> p k n", k=8)
    w_out = c.alloc(8 * 1024, BF16).rearrange("p (k n) -> p k n", k=8)
    gpost, dv = c.alloc(D, F32), c.alloc(D, F32)
    bg = c.alloc(8, F32)
    xt = c.alloc(2 * D, F32).rearrange("p (a d) -> p a d", a=2)
    yt = c.alloc(2 * D, F32).rearrange("p (a d) -> p a d", a=2)
    ut = c.alloc(2 * D, F32).rearrange("p (a d) -> p a d", a=2)
    yb = c.alloc(D, BF16)
    yT = c.alloc(D, BF16).rearrange("p (k t) -> p k t", k=8)
    zT = c.alloc(D, BF16).rearrange("p (k t) -> p k t", k=8)
    sgm = c.alloc(2 * 128, F32).rearrange("p (a t) -> p a t", a=2)
    Hh = c.alloc(D, F32)
    junk = c.alloc(D, BF16)
    stt = c.alloc(8, F32)
    P.dma("pool", lambda e: e.dma_start(out=w_glu.rearrange("p k n -> p (k n)"), in_=d["w_glu"]), "s5c_w", writes=[K("w_glu")], group=True)
    P.dma("pool", lambda e: e.dma_start(out=w_out.rearrange("p k n -> p (k n)"), in_=d["w_out"]), "s5c_w", writes=[K("w_out")], group=True)
    for dst, src, key in ((gpost, d["gpost"].partition_broadcast(128), "gpost"), (dv, d["dvec"].partition_broadcast(128), "dv"),
                          (bg, d["b_glu"], "bg")):
        P.dma("sp", lambda e, dst=dst, src=src: e.dma_start(out=dst, in_=src), "const_" + tag, writes=[K(key)], group=True)
    xv = x_d.rearrange("(t p) d -> t p d", p=128)
    yv = y_d.rearrange("(t p) d -> t p d", p=128)
    ysv = d["ys"].rearrange("(t p) d -> t p d", p=128)
    uv = d["u"].rearrange("(t p) d -> t p d", p=128)
    for tt in range(NTT):
        a = tt % 2
        P.dma("sp", lambda e, tt=tt, a=a: e.dma_start(out=xt[:, a, :], in_=xv[tt]), "s5c_x%d" % a, writes=[K("xt", a)])
        P.dma("sp", lambda e, tt=tt, a=a: e.dma_start(out=yt[:, a, :], in_=ysv[tt]), "s5c_ys%d" % a, writes=[K("yt", a)])
        P.dma("sp", lambda e, tt=tt, a=a: e.dma_start(out=ut[:, a, :], in_=uv[tt]), "s5c_u%d" % a, writes=[K("ut", a)])
        P.op("dve", lambda e, a=a: e.tensor_tensor(out=ut[:, a, :], in0=ut[:, a, :], in1=dv, op=ALU.mult),
             reads=[K("ut", a), K("dv")], writes=[K("ut", a)])
        P.op("dve", lambda e, a=a: e.tensor_tensor(out=yt[:, a, :], in0=yt[:, a, :], in1=ut[:, a, :], op=ALU.add),
             reads=[K("ut", a), K("yt", a)], writes=[K("yt", a)])
        P.op("act", lambda e, a=a: e.activation(out=yb, in_=yt[:, a, :], func=AF.Gelu_apprx_tanh),
             reads=[K("yt", a)], writes=[K("yb")])
        pb = c.bank(a, BF16)
        for k in range(8):
            P.op("pe", lambda e, k=k, pb=pb: e.transpose(out=pb[:, k * 128:(k + 1) * 128], in_=yb[:, k * 128:(k + 1) * 128],
                                                       identity=c.idb), reads=[K("yb"), "idb"], writes=[("bank", a)])
        P.op("act", lambda e, pb=pb: e.activation(out=yT.rearrange("p k t -> p (k t)"), in_=pb, func=AF.Copy),
             reads=[("bank", a)], writes=[K("yT")])
        for g4 in range(2):
            bi = 2 + g4
            for fc in range(4 * g4, 4 * g4 + 4):
                pt = c.bank(bi)[:, (fc % 4) * 128:(fc % 4 + 1) * 128]
                for k in range(8):
                    P.op("pe", lambda e, pt=pt, k=k, fc=fc: e.matmul(pt, lhsT=w_glu[:, k, fc * 128:(fc + 1) * 128], rhs=yT[:, k, :],
                                                                 start=(k == 0), stop=(k == 7)),
                         reads=[K("w_glu"), K("yT")], writes=[("bank", bi)])
            for fc in range(4 * g4, 4 * g4 + 4):
                pt = c.bank(bi)[:, (fc % 4) * 128:(fc % 4 + 1) * 128]
                sa = fc % 2
                P.op("act", lambda e, pt=pt, fc=fc, sa=sa: e.activation(out=sgm[:, sa, :], in_=pt, func=AF.Sigmoid, bias=bg[:, fc:fc + 1]),
                     reads=[("bank", bi), K("bg")], writes=[K("sgm", sa)])
                P.op("dve", lambda e, fc=fc, sa=sa: e.tensor_tensor(out=zT[:, fc, :], in0=yT[:, fc, :], in1=sgm[:, sa, :], op=ALU.mult),
                     reads=[K("yT"), K("sgm", sa)], writes=[K("zT", fc)])
        for hf in range(2):
            bi = 4 + hf
            pt = c.bank(bi)
            for fc in range(8):
                P.op("pe", lambda e, pt=pt, fc=fc, hf=hf: e.matmul(pt, lhsT=zT[:, fc, :], rhs=w_out[:, fc, hf * 512:(hf + 1) * 512],
                                                               start=(fc == 0), stop=(fc == 7)),
                     reads=[K("zT", fc), K("w_out")], writes=[("bank", bi)])
            P.op("act", lambda e, pt=pt, hf=hf: e.activation(out=Hh[:, hf * 512:(hf + 1) * 512], in_=pt, func=AF.Copy),
                 reads=[("bank", bi)], writes=[K("Hh", hf)])
        post_norm_add(c, Hh, [K("Hh", 0), K("Hh", 1)], xt[:, a, :], K("xt", a), gpost, K("gpost"), junk, stt[:, 2:3], stt[:, 3:4], K, 1.0)
        P.dma("sp", lambda e, tt=tt, a=a: e.dma_start(out=yv[tt], in_=xt[:, a, :]), "s5c_o%d" % a, reads=[K("xt", a)])
    P.barrier()
    c.release(m)


def build_s5a_prog():
    st = contextlib.ExitStack()
    c = Ctx(st)
    x_d = c.dram_in("x", [TOK, D])
    ident_d = c.dram_in("ident", [128, 128])
    d = dict(w_in=c.dram_in("w_in", [128, 8 * 1024]), gpre=c.dram_in("gpre", [128, 8]))
    u_d = c.dram_out("u", [TOK, D])
    setup_common(c, ident_d)
    s5a_phase(c, x_d, u_d, d, "s5a")
    c.P.barrier()
    c.P.emit(st)
    return c, st


def build_s5b_prog(nch=NCH):
    st = contextlib.ExitStack()
    c = Ctx(st)
    ident_d = c.dram_in("ident", [128, 128])
    U_d = c.dram_in("U", [8, 128, nch])
    d = dict(par=c.dram_in("par", [128, 3, 8]), sgn=c.dram_in("sgn", [128, 2]), bA1=c.dram_in("bA1", [128, 8, 16]),
             bA2=c.dram_in("bA2", [128, 8, 16]), cP1=c.dram_in("cP1", [128, 8, 16]), cP2=c.dram_in("cP2", [128, 8, 16]),
             ident=ident_d, jshift=c.dram_in("jshift", [128, 128]), maskK=c.dram_in("maskK", [128, 128]))
    Y_d = c.dram_out("Y", [8, nch, 128])
    setup_common(c, ident_d)
    s5b_phase(c, U_d, Y_d, d, "s5b", nch)
    c.P.barrier()
    c.P.emit(st)
    return c, st


def build_s5c_prog():
    st = contextlib.ExitStack()
    c = Ctx(st)
    x_d = c.dram_in("x", [TOK, D])
    ident_d = c.dram_in("ident", [128, 128])
    d = dict(ys=c.dram_in("ys", [TOK, D]), u=c.dram_in("u", [TOK, D]), w_glu=c.dram_in("w_glu", [128, 8 * 1024]),
             w_out=c.dram_in("w_out", [128, 8 * 1024]), gpost=c.dram_in("gpost", [D]), dvec=c.dram_in("dvec", [D]),
             b_glu=c.dram_in("b_glu", [128, 8]))
    y_d = c.dram_out("y", [TOK, D])
    setup_common(c, ident_d)
    s5c_phase(c, x_d, y_d, d, "s5c")
    c.P.barrier()
    c.P.emit(st)
    return c, st


def wlay8(w):
    n = w.shape[1]
    return np.ascontiguousarray(w.reshape(8, 128, n).transpose(1, 0, 2).reshape(128, 8 * n))


def s5b_inputs(I, core):
    gs = slice(8 * core, 8 * core + 8)
    two = lambda a: np.concatenate([a, a], axis=0)
    ar = two(I["s5_a_re"][0][gs].T)
    ai = two(I["s5_a_im"][0][gs].T)
    ldt = np.broadcast_to(I["s5_log_dt"][0][gs][None, :], (128, 8))
    par = np.ascontiguousarray(np.stack([ar, ai, ldt], axis=1).astype(np.float32))
    sgn = np.ones((128, 2), np.float32)
    sgn[:64, 0] = -1.0
    sgn[64:, 1] = -1.0
    br = I["s5_b_re"][0][gs].transpose(1, 0, 2)
    bi = I["s5_b_im"][0][gs].transpose(1, 0, 2)
    cr = I["s5_c_re"][0][gs].transpose(2, 0, 1)
    ci = I["s5_c_im"][0][gs].transpose(2, 0, 1)
    js = np.zeros((128, 128), np.float32)
    js[np.arange(64), np.arange(64) + 64] = 1.0
    js[np.arange(64) + 64, np.arange(64)] = 1.0
    ii = np.arange(128) // 16
    mk = (ii[None, :] >= ii[:, None]).astype(np.float32)
    return dict(par=par, sgn=sgn, bA1=np.ascontiguousarray(np.concatenate([br, bi], 0)),
                bA2=np.ascontiguousarray(np.concatenate([bi, br], 0)),
                cP1=np.ascontiguousarray(np.concatenate([cr, ci], 0)), cP2=np.ascontiguousarray(np.concatenate([ci, cr], 0)),
                jshift=js, maskK=mk)


def run_s5_layer(I, i, xfull):
    j = i // 4
    shards = lambda a: [np.ascontiguousarray(a[k * TOK:(k + 1) * TOK]) for k in range(NCORES)]
    sha = dict(w_in=wlay8(I["s5_w_in"][j]), gpre=col128(I["norm_pre"][i, 1], 8))
    ra = run_prog("s5a", build_s5a_prog, sha, [dict(x=s) for s in shards(xfull)])
    u = np.concatenate([ra[k]["u"] for k in range(NCORES)], axis=0)
    ug = u.reshape(NCH, 8, 64, 16)
    Uall = np.ascontiguousarray(ug.transpose(2, 1, 3, 0).reshape(64, 128, NCH))
    rb = run_prog("s5b", build_s5b_prog, {}, [dict(s5b_inputs(I, k), U=np.ascontiguousarray(Uall[8 * k:8 * k + 8]))
                                              for k in range(NCORES)])
    Y = np.concatenate([rb[k]["Y"] for k in range(NCORES)], axis=0)
    ys = np.ascontiguousarray(Y.reshape(64, NCH, 8, 16).transpose(1, 2, 0, 3).reshape(SEQ, D))
    shc = dict(w_glu=wlay8(I["s5_w_glu"][j]), w_out=wlay8(I["s5_w_out"][j]), gpost=np.ascontiguousarray(I["norm_post"][i, 1]),
               dvec=np.ascontiguousarray(I["s5_d"][j]), b_glu=col128(I["s5_b_glu"][j], 8))
    ysh, ush, xsh = shards(ys), shards(u), shards(xfull)
    rc = run_prog("s5c", build_s5c_prog, shc, [dict(x=xsh[k], ys=ysh[k], u=ush[k]) for k in range(NCORES)])
    return np.concatenate([rc[k]["y"] for k in range(NCORES)], axis=0)


def ffn_params(I, i, which):
    slot = 0 if which == 0 else 2
    return (I["ffn_w1"][i, which], I["ffn_w3"][i, which], I["ffn_w2"][i, which], I["norm_pre"][i, slot], I["norm_post"][i, slot])


def run_ffn_chain(I, specs, xfull):
    xs = [xfull[k * TOK:(k + 1) * TOK] for k in range(NCORES)]
    ys = run_ffn(xs, [ffn_params(I, i, w) for (i, w) in specs])
    return np.concatenate(ys, axis=0)


def run_mixer_layer(I, i, xfull):
    kind = i % 4
    if kind == 0:
        return run_s5_layer(I, i, xfull)
    if kind == 1:
        sh = conv_inputs(I, i)
        res = run_prog("conv", build_conv_prog, sh, [conv_percore(sh["small"], xfull, k) for k in range(NCORES)])
    elif kind == 2:
        sh = gmlp_inputs(I, i)
        res = run_prog("gmlp", build_gmlp_prog, sh, [dict(x=np.ascontiguousarray(xfull[k * TOK:(k + 1) * TOK]))
                                                    for k in range(NCORES)])
    else:
        sh = attn_inputs(I, i)
        res = run_prog("attn", build_attn_prog, sh, [attn_percore(sh["small"], xfull, k) for k in range(NCORES)])
    return np.concatenate([res[k]["y"] for k in range(NCORES)], axis=0)


def kernel(**inputs):
    I = {k: np.asarray(v) for k, v in inputs.items()}
    x = np.ascontiguousarray(I["x"][0], dtype=np.float32)
    x = run_ffn_chain(I, [(0, 0)], x)
    for i in range(4):
        x = run_mixer_layer(I, i, x)
        x = run_ffn_chain(I, [(i, 1), (i + 1, 0)] if i < 3 else [(i, 1)], x)
    return np.ascontiguousarray(x[None].astype(np.float32))
```
